# Optimizing a Trainium2 kernel written in Bass

```python
import jax, jax.numpy as jnp
from jax import lax
import numpy as np

D_MODEL = 1024
BATCH = 16
SEQ = 2048
DEPTH = 1

CHUNK = 64
MLSTM_HEADS = 4
MLSTM_HEAD_DIM = 256
MLSTM_DIM = MLSTM_HEADS * MLSTM_HEAD_DIM
FOX_HEADS = 16
FOX_HEAD_DIM = 64
FOX_DIM = FOX_HEADS * FOX_HEAD_DIM
Q_BLOCK = 128
FFN_DIM = 2816
CONV_WIDTH = 3
EPS = 1e-6
IN_SPLITS = (MLSTM_DIM, MLSTM_DIM, MLSTM_DIM, MLSTM_DIM, MLSTM_HEADS, MLSTM_HEADS,
             FOX_DIM, FOX_DIM, FOX_DIM, FOX_HEADS, D_MODEL, D_MODEL)
IN_COLS = sum(IN_SPLITS)

kernel_name = "hybrid_mlstm_fox_convffn_block"


def rms_norm(x, w):
    xf = x.astype(jnp.float32)
    y = xf * lax.rsqrt(jnp.mean(xf * xf, axis=-1, keepdims=True) + EPS)
    return (y * w.astype(jnp.float32)).astype(x.dtype)


def modulate(x, w, shift, scale):
    return rms_norm(x, w) * (1 + scale[:, None, :]) + shift[:, None, :]


def mlstm_chunkwise(q, k, v, i_pre, log_f):
    B, S, H, Dh = q.shape
    nc = S // CHUNK

    def heads_chunks(a):
        return a.reshape(B, nc, CHUNK, H, Dh).transpose(1, 0, 3, 2, 4)

    def gates_chunks(a):
        return a.reshape(B, nc, CHUNK, H).transpose(1, 0, 3, 2)

    xs = (heads_chunks(q * Dh ** -0.5), heads_chunks(k), heads_chunks(v),
          gates_chunks(i_pre), gates_chunks(log_f))
    causal = jnp.tril(jnp.ones((CHUNK, CHUNK), dtype=bool))

    def step(carry, chunk):
        C, n, m = carry
        q_, k_, v_, i_, f_ = chunk
        b = jnp.cumsum(f_, axis=-1)
        log_intra = jnp.where(causal, b[..., :, None] - b[..., None, :] + i_[..., None, :], -jnp.inf)
        log_inter = b + m[..., None]
        m_t = jnp.maximum(log_inter, jnp.max(log_intra, axis=-1))
        d_intra = jnp.exp(log_intra - m_t[..., None])
        d_inter = jnp.exp(log_inter - m_t)
        s = jnp.einsum("bhtd,bhsd->bhts", q_, k_) * d_intra
        num = (jnp.einsum("bhts,bhse->bhte", s, v_)
               + d_inter[..., None] * jnp.einsum("bhtd,bhde->bhte", q_, C))
        den = jnp.sum(s, axis=-1) + d_inter * jnp.einsum("bhtd,bhd->bht", q_, n)
        h = num / jnp.maximum(jnp.abs(den), jnp.exp(-m_t))[..., None]
        b_last = b[..., -1]
        log_w = b_last[..., None] - b + i_
        m_new = jnp.maximum(b_last + m, jnp.max(log_w, axis=-1))
        w = jnp.exp(log_w - m_new[..., None])
        decay = jnp.exp(b_last + m - m_new)
        C_new = decay[..., None, None] * C + jnp.einsum("bhs,bhsd,bhse->bhde", w, k_, v_)
        n_new = decay[..., None] * n + jnp.einsum("bhs,bhsd->bhd", w, k_)
        return (C_new, n_new, m_new), h

    init = (jnp.zeros((B, H, Dh, Dh), jnp.float32),
            jnp.zeros((B, H, Dh), jnp.float32),
            jnp.zeros((B, H), jnp.float32))
    _, h = lax.scan(step, init, xs)
    return h.transpose(1, 0, 3, 2, 4).reshape(B, S, H, Dh)


def forgetting_attention(q, k, v, log_f):
    S = q.shape[2]
    Fcum = jnp.cumsum(log_f, axis=-1)
    scale = q.shape[-1] ** -0.5
    outs = []
    for blk in range(S // Q_BLOCK):
        q0, q1 = blk * Q_BLOCK, (blk + 1) * Q_BLOCK
        logits = jnp.einsum("bhqd,bhkd->bhqk", q[:, :, q0:q1], k[:, :, :q1]).astype(jnp.float32) * scale
        logits = logits + Fcum[:, :, q0:q1, None] - Fcum[:, :, None, :q1]
        mask = jnp.arange(q0, q1)[:, None] >= jnp.arange(q1)[None, :]
        p = jax.nn.softmax(jnp.where(mask, logits, -jnp.inf), axis=-1)
        outs.append(jnp.einsum("bhqk,bhkd->bhqd", p.astype(v.dtype), v[:, :, :q1]))
    return jnp.concatenate(outs, axis=2)


def setup_inputs(seed: int = 0) -> dict:
    key = jax.random.key(seed)
    ks = jax.random.split(key, 24)

    def nrm(k, shape, scale):
        return jax.random.normal(k, shape, jnp.float32) * scale

    L = DEPTH
    return {
        "x": nrm(ks[0], (BATCH, SEQ, D_MODEL), 1.0),
        "c": nrm(ks[1], (BATCH, D_MODEL), 1.0),
        "w_ada": nrm(ks[2], (L, D_MODEL, 6 * D_MODEL), 0.02),
        "b_ada": nrm(ks[3], (L, 6 * D_MODEL), 0.01),
        "norm1_w": 1.0 + nrm(ks[4], (L, D_MODEL), 0.02),
        "w_in": nrm(ks[5], (L, D_MODEL, IN_COLS), D_MODEL ** -0.5),
        "b_mlstm_i": nrm(ks[6], (L, MLSTM_HEADS), 0.1),
        "b_mlstm_f": jnp.linspace(3.0, 6.0, MLSTM_HEADS, dtype=jnp.float32) + nrm(ks[7], (L, MLSTM_HEADS), 0.1),
        "mlstm_norm_w": 1.0 + nrm(ks[8], (L, MLSTM_DIM), 0.02),
        "b_fox_f": jnp.linspace(1.0, 5.0, FOX_HEADS, dtype=jnp.float32) + nrm(ks[9], (L, FOX_HEADS), 0.1),
        "fox_q_norm_w": 1.0 + nrm(ks[10], (L, FOX_HEAD_DIM), 0.02),
        "fox_k_norm_w": 1.0 + nrm(ks[11], (L, FOX_HEAD_DIM), 0.02),
        "w_branch_mlstm": nrm(ks[12], (L, MLSTM_DIM, D_MODEL), MLSTM_DIM ** -0.5),
        "w_branch_fox": nrm(ks[13], (L, FOX_DIM, D_MODEL), FOX_DIM ** -0.5),
        "w_out": nrm(ks[14], (L, D_MODEL, D_MODEL), D_MODEL ** -0.5),
        "norm2_w": 1.0 + nrm(ks[15], (L, D_MODEL), 0.02),
        "w_up": nrm(ks[16], (L, D_MODEL, 2 * FFN_DIM), D_MODEL ** -0.5),
        "conv_w": nrm(ks[17], (L, CONV_WIDTH, 2 * FFN_DIM), CONV_WIDTH ** -0.5),
        "conv_b": nrm(ks[18], (L, 2 * FFN_DIM), 0.01),
        "w_down": nrm(ks[19], (L, FFN_DIM, D_MODEL), FFN_DIM ** -0.5),
    }


def reference(x, c, w_ada, b_ada, norm1_w, w_in, b_mlstm_i, b_mlstm_f, mlstm_norm_w,
              b_fox_f, fox_q_norm_w, fox_k_norm_w, w_branch_mlstm, w_branch_fox, w_out,
              norm2_w, w_up, conv_w, conv_b, w_down):
    B, S, _ = x.shape
    f32 = jnp.float32
    split_points = np.cumsum(IN_SPLITS)[:-1].tolist()
    for l in range(DEPTH):
        ada = jnp.dot(jax.nn.silu(c), w_ada[l]) + b_ada[l]
        shift1, scale1, gate1, shift2, scale2, gate2 = jnp.split(ada, 6, axis=-1)

        h = modulate(x, norm1_w[l], shift1, scale1)
        proj = jnp.einsum("bsd,de->bse", h, w_in[l])
        mq, mk, mv, mo, mi, mf, fq, fk, fv, ff, ga, gb = jnp.split(proj, split_points, axis=-1)

        i_pre = (mi + b_mlstm_i[l]).astype(f32)
        m_log_f = jax.nn.log_sigmoid((mf + b_mlstm_f[l]).astype(f32))
        hm = mlstm_chunkwise(mq.reshape(B, S, MLSTM_HEADS, MLSTM_HEAD_DIM).astype(f32),
                             mk.reshape(B, S, MLSTM_HEADS, MLSTM_HEAD_DIM).astype(f32),
                             mv.reshape(B, S, MLSTM_HEADS, MLSTM_HEAD_DIM).astype(f32),
                             i_pre, m_log_f)
        hm = rms_norm(hm, mlstm_norm_w[l].reshape(MLSTM_HEADS, MLSTM_HEAD_DIM))
        hm = (hm.reshape(B, S, MLSTM_DIM) * jax.nn.sigmoid(mo.astype(f32))).astype(x.dtype)

        fq = rms_norm(fq.reshape(B, S, FOX_HEADS, FOX_HEAD_DIM), fox_q_norm_w[l]).transpose(0, 2, 1, 3)
        fk = rms_norm(fk.reshape(B, S, FOX_HEADS, FOX_HEAD_DIM), fox_k_norm_w[l]).transpose(0, 2, 1, 3)
        fv = fv.reshape(B, S, FOX_HEADS, FOX_HEAD_DIM).transpose(0, 2, 1, 3)
        fox_log_f = jax.nn.log_sigmoid((ff + b_fox_f[l]).astype(f32)).transpose(0, 2, 1)
        hf = forgetting_attention(fq, fk, fv, fox_log_f).transpose(0, 2, 1, 3).reshape(B, S, FOX_DIM)

        ya = jnp.einsum("bse,ed->bsd", hm, w_branch_mlstm[l])
        yb = jnp.einsum("bse,ed->bsd", hf, w_branch_fox[l])
        merged = jax.nn.sigmoid(ga) * ya + jax.nn.sigmoid(gb) * yb
        x = x + gate1[:, None, :] * jnp.einsum("bsd,de->bse", merged, w_out[l])

        h2 = modulate(x, norm2_w[l], shift2, scale2)
        u = jnp.einsum("bsd,df->bsf", h2, w_up[l])
        u = lax.conv_general_dilated(u, conv_w[l][:, None, :], window_strides=(1,),
                                     padding=[(CONV_WIDTH - 1, 0)],
                                     dimension_numbers=("NWC", "WIO", "NWC"),
                                     feature_group_count=2 * FFN_DIM) + conv_b[l]
        u_gate, u_val = jnp.split(u, 2, axis=-1)
        y2 = jnp.einsum("bsf,fd->bsd", jax.nn.silu(u_gate) * u_val, w_down[l])
        x = x + gate2[:, None, :] * y2
    return x
```

```python
import contextlib
import os
import numpy as np
import concourse.bass as bass
import concourse.mybir as mybir
from concourse.bass_utils import run_bass_kernel_spmd

F32 = mybir.dt.float32
BF16 = mybir.dt.bfloat16
AF = mybir.ActivationFunctionType
ALU = mybir.AluOpType

ENGS = ("pe", "act", "dve", "pool", "sp")
NCORES = 8
S = 2048
NB = 16
D = 1024
KC = 8
FF = 2816
NF = 22
EPS = 1e-6
O_MQ, O_MK, O_MV, O_MO, O_MI, O_MF = 0, 1024, 2048, 3072, 4096, 4100
O_FQ, O_FK, O_FV, O_FF, O_GA, O_GB = 4104, 5128, 6152, 7176, 7192, 8216


class Prog:
    def __init__(self, nc, stack):
        self.nc = nc
        self.q = {e: [] for e in ENGS}
        self.cnt = {}
        self.sem = {}
        self.stack = stack
        for e in ENGS:
            self.sem[e] = stack.enter_context(nc.semaphore("s_" + e))
            self.cnt[e] = 0
        self.waited = {e: {} for e in ENGS}
        self.last_w = {}
        self.readers = {}

    def dma_sem(self, key):
        if key not in self.sem:
            self.sem[key] = self.stack.enter_context(self.nc.semaphore("d_" + key))
            self.cnt[key] = 0
        return key

    def _deps(self, eng, reads, writes):
        deps = {}

        def add(d):
            if d is None:
                return
            k, v = d
            if deps.get(k, 0) < v:
                deps[k] = v

        for r in reads:
            add(self.last_w.get(r))
            if r.startswith("bk"):
                for k, v in self.readers.get(r, {}).items():
                    if k != eng:
                        add((k, v))
        for w in writes:
            add(self.last_w.get(w))
            for k, v in self.readers.get(w, {}).items():
                add((k, v))
        return deps

    def _emit_waits(self, eng, deps):
        for k, v in deps.items():
            if self.waited[eng].get(k, 0) >= v:
                continue
            self.waited[eng][k] = v
            sem = self.sem[k]
            self.q[eng].append(lambda e, sem=sem, v=v: e.wait_ge(sem, v))

    def _mark(self, key, val, reads, writes):
        for r in reads:
            self.readers.setdefault(r, {})[key] = val
        for w in writes:
            self.last_w[w] = (key, val)
            self.readers[w] = {}

    def op(self, eng, fns, reads=(), writes=()):
        if not isinstance(fns, (list, tuple)):
            fns = [fns]
        self._emit_waits(eng, self._deps(eng, reads, writes))
        sem = self.sem[eng]
        self.cnt[eng] += 1
        n = self.cnt[eng]
        for f in fns[:-1]:
            self.q[eng].append(f)
        last = fns[-1]
        self.q[eng].append(lambda e, last=last, sem=sem: last(e).then_inc(sem, 1))
        self._mark(eng, n, reads, writes)
        return n

    def dma(self, qeng, out, in_, semkey, reads=(), writes=()):
        self.dma_sem(semkey)
        self._emit_waits(qeng, self._deps(None, reads, writes))
        sem = self.sem[semkey]
        self.cnt[semkey] += 16
        v = self.cnt[semkey]
        self.q[qeng].append(lambda e, out=out, in_=in_, sem=sem: e.dma_start(out=out, in_=in_).then_inc(sem, 16))
        self._mark(semkey, v, reads, writes)
        return v

    def barrier(self):
        allk = {k: v for k, v in self.cnt.items() if v > 0}
        for e in ENGS:
            self._emit_waits(e, dict(allk))
        self.last_w = {}
        self.readers = {}

    def run(self):
        nc = self.nc
        with nc.Block() as block:
            @block.tensor
            def _(e):
                for f in self.q["pe"]:
                    f(e)

            @block.scalar
            def _(e):
                for f in self.q["act"]:
                    f(e)

            @block.vector
            def _(e):
                for f in self.q["dve"]:
                    f(e)

            @block.gpsimd
            def _(e):
                for f in self.q["pool"]:
                    f(e)

            @block.sync
            def _(e):
                for f in self.q["sp"]:
                    f(e)


def build_program(nseq=2, stop_after=99, dbg=None, fox_limit=99):
    nc = bass.Bass("TRN2", target_bir_lowering=False)

    def din(name, shape):
        return nc.dram_tensor(name, list(shape), F32, kind="ExternalInput").ap()

    x = din("x", [2, S, D])
    cT = din("cT", [128, KC, 2])
    w_ada = din("w_ada", [D, 6 * D])
    b_ada = din("b_ada", [6 * D])
    b_adaT = din("b_adaT", [128, 48])
    n1wT = din("n1wT", [128, KC])
    n2wT = din("n2wT", [128, KC])
    mnwT = din("mnwT", [128, KC])
    w_in = din("w_in", [D, 9240])
    b_i = din("b_i", [4, 1])
    b_f20 = din("b_f20", [20, 1])
    fqw = din("fqw", [128, 1])
    fkw = din("fkw", [128, 1])
    w_bm = din("w_bm", [D, D])
    w_bf = din("w_bf", [D, D])
    w_out = din("w_out", [D, D])
    w_up = din("w_up", [D, 2 * FF])
    convT = din("convT", [128, 44, 3])
    convbT = din("convbT", [128, 44])
    w_down = din("w_down", [FF, D])
    out = nc.dram_tensor("out", [2, S, D], F32, kind="ExternalOutput").ap()
    npscr = nc.dram_tensor("npscr", [20, 3 * S], BF16).ap()
    dbg_t = None
    if dbg is not None:
        dbg_t = nc.dram_tensor("dbg", list(dbg), F32, kind="ExternalOutput").ap()

    w_in_v = w_in.rearrange("(k p) n -> p k n", p=128)
    w_ada_v = w_ada.rearrange("(k p) n -> p k n", p=128)
    w_bm_v = w_bm.rearrange("(k p) n -> p k n", p=128)
    w_bf_v = w_bf.rearrange("(k p) n -> p k n", p=128)
    w_out_v = w_out.rearrange("(k p) n -> p k n", p=128)
    w_up_v = w_up.rearrange("(k p) n -> p k n", p=128)
    w_down_v = w_down.rearrange("(k p) n -> p k n", p=128)

    with contextlib.ExitStack() as st0:
        P = Prog(nc, st0)

        uid = [0]

        def SB(st, name, shape, dt):
            uid[0] += 1
            return st.enter_context(nc.sbuf_tensor(f"{name}_{uid[0]}", list(shape), dt))

        bk = [st0.enter_context(nc.psum_tensor(f"bk{i}", [128, 512], F32)) for i in range(8)]
        BK = [f"bk{i}" for i in range(8)]

        def bkb(i):
            return bk[i][:].bitcast(BF16)

        def I(name, *a, **kw):
            return lambda e: getattr(e, name)(*a, **kw)

        def mm(out_ap, pairs, reads, writes):
            n = len(pairs)
            fns = []
            for i, (l, r) in enumerate(pairs):
                fns.append(I("matmul", out_ap, lhsT=l, rhs=r, start=(i == 0), stop=(i == n - 1)))
            P.op("pe", fns, reads, writes)

        def act(out_ap, in_ap, func, reads, writes, bias=None, scale=None, accum_out=None):
            kw = {}
            if bias is not None:
                kw["bias"] = bias
            if scale is not None:
                kw["scale"] = scale
            if accum_out is not None:
                kw["accum_out"] = accum_out
            P.op("act", I("activation", out=out_ap, in_=in_ap, func=func, **kw), reads, writes)

        ident_bf = SB(st0, "ident_bf", [128, 128], BF16)
        ident_f = SB(st0, "ident_f", [128, 128], F32)
        maskb = SB(st0, "maskb", [128, 128], BF16)
        mask01 = SB(st0, "mask01", [128, 128], F32)
        bdones = SB(st0, "bdones", [128, 128], BF16)
        sel = SB(st0, "sel", [4, 512], F32)
        tmpc = SB(st0, "tmpc", [128, 512], F32)
        mask4 = SB(st0, "mask4", [128, 512], F32)

        P.op("pool", I("memset", tmpc[:], 1.0), (), ["tmpc"])
        P.op("pool", I("affine_select", out=ident_f[:], in_=tmpc[:, 0:128], pattern=[[-1, 128]], compare_op=ALU.is_equal,
                                               fill=0.0, base=0, channel_multiplier=1), ["tmpc"], ["ident_f"])
        P.op("pool", I("tensor_copy", out=ident_bf[:], in_=ident_f[:]), ["ident_f"], ["ident_bf"])
        P.op("pool", I("affine_select", out=mask01[:], in_=tmpc[:, 0:128], pattern=[[1, 128]], compare_op=ALU.is_ge,
                                               fill=0.0, base=0, channel_multiplier=-1), ["tmpc"], ["mask01"])
        for i4 in range(4):
            P.op("pool", I("tensor_copy", out=mask4[:, i4 * 128:(i4 + 1) * 128], in_=mask01[:]), ["mask01"], ["mask4"])
        P.op("pool", I("tensor_scalar", out=maskb[:], in0=mask01[:], scalar1=-1.0, scalar2=30000.0,
                                               op0=ALU.add, op1=ALU.mult), ["mask01"], ["maskb"])
        P.op("pool", I("memset", bdones[:], 0.0), (), ["bdones"])
        P.op("pool", I("memset", bdones[0:64, 0:64], 1.0), (), ["bdones"])
        P.op("pool", I("memset", bdones[64:128, 64:128], 1.0), (), ["bdones"])
        P.op("pool", I("affine_select", out=sel[:].rearrange("p (h j) -> p h j", j=128),
                                               in_=tmpc[0:4, :].rearrange("p (h j) -> p h j", j=128),
                                               pattern=[[-1, 4], [0, 128]], compare_op=ALU.is_equal,
                                               fill=0.0, base=0, channel_multiplier=1), ["tmpc"], ["sel"])

        cT_sb = SB(st0, "cT_sb", [128, KC, 2], F32)
        sc = SB(st0, "sc", [128, KC, 2], BF16)
        badaT = SB(st0, "badaT", [128, 48], F32)
        n1w = SB(st0, "n1w", [128, KC], F32)
        n2w = SB(st0, "n2w", [128, KC], F32)
        mnw = SB(st0, "mnw", [128, KC], F32)
        bi = SB(st0, "bi", [4, 1], F32)
        nbf20 = SB(st0, "nbf20", [20, 1], F32)
        fqw_s = SB(st0, "fqw_s", [128, 1], F32)
        fkw_s = SB(st0, "fkw_s", [128, 1], F32)
        convw = SB(st0, "convw", [128, 44, 3], F32)
        convb = SB(st0, "convb", [128, 44], F32)
        adaT = SB(st0, "adaT", [128, 48, 2], F32)
        a1 = SB(st0, "a1", [128, KC, 2], F32)
        a2 = SB(st0, "a2", [128, KC, 2], F32)
        grow = SB(st0, "grow", [2, 2048], F32)
        hT = SB(st0, "hT", [128, KC, S], BF16)

        for (t_, src, key) in [(cT_sb, cT, "cT_sb"), (badaT, b_adaT, "badaT"), (n1w, n1wT, "n1w"), (n2w, n2wT, "n2w"),
                               (mnw, mnwT, "mnw"), (bi, b_i, "bi"), (nbf20, b_f20, "nbf20"), (fqw_s, fqw, "fqw_s"),
                               (fkw_s, fkw, "fkw_s"), (convw, convT, "convw"), (convb, convbT, "convb")]:
            P.dma("sp", t_[:], src, "ld_" + key, (), [key])
        P.op("dve", I("tensor_scalar", out=nbf20[:], in0=nbf20[:], scalar1=-1.0, scalar2=None, op0=ALU.mult), ["nbf20"], ["nbf20"])
        P.op("dve", I("tensor_scalar", out=fqw_s[:], in0=fqw_s[:], scalar1=0.125, scalar2=None, op0=ALU.mult), ["fqw_s"], ["fqw_s"])

        with contextlib.ExitStack() as st:
            bgate = SB(st, "bgate", [2, 2048], F32)
            wa = [SB(st, f"wa{i}", [128, KC, 1536], BF16) for i in range(2)]
            for r in range(2):
                P.dma("sp", bgate[r:r + 1, 0:1024], b_ada[2048:3072].rearrange("(o n) -> o n", o=1), "ld_bg", (), ["bgate"])
                P.dma("sp", bgate[r:r + 1, 1024:2048], b_ada[5120:6144].rearrange("(o n) -> o n", o=1), "ld_bg", (), ["bgate"])
            act(sc[:], cT_sb[:], AF.Silu, ["cT_sb"], ["sc"])
            for pc in range(4):
                sl = pc % 2
                P.dma("pool", wa[sl][:], w_ada_v[:, :, pc * 1536:(pc + 1) * 1536], f"ld_wa{sl}", (), [f"wa{sl}"])
                for ch in range(12):
                    gch = pc * 12 + ch
                    mm(bk[0][:, gch * 2:gch * 2 + 2],
                       [(wa[sl][:, k, ch * 128:(ch + 1) * 128], sc[:, k, :]) for k in range(KC)],
                       [f"wa{sl}", "sc"], [BK[0]])
                if pc in (1, 3):
                    for hf in range(2):
                        bi_ = 1 + (pc // 2) * 2 + hf
                        mm(bk[bi_][0:2, :],
                           [(sc[:, k, :], wa[sl][:, k, 512 + hf * 512:1024 + hf * 512]) for k in range(KC)],
                           [f"wa{sl}", "sc"], [BK[bi_]])
            psv = bk[0][:, 0:96].rearrange("p (c b) -> p c b", b=2)
            for b in range(2):
                P.op("dve", I("tensor_tensor", out=adaT[:, :, b], in0=psv[:, :, b], in1=badaT[:], op=ALU.add),
                     [BK[0], "badaT"], ["adaT"])
            for b in range(2):
                P.op("dve", I("scalar_tensor_tensor", out=a1[:, :, b], in0=adaT[:, 8:16, b], scalar=1.0, in1=n1w[:],
                                                                  op0=ALU.add, op1=ALU.mult), ["adaT", "n1w"], ["a1"])
                P.op("dve", I("scalar_tensor_tensor", out=a2[:, :, b], in0=adaT[:, 32:40, b], scalar=1.0, in1=n2w[:],
                                                                  op0=ALU.add, op1=ALU.mult), ["adaT", "n2w"], ["a2"])
            for i in range(4):
                P.op("dve", I("tensor_tensor", out=grow[:, i * 512:(i + 1) * 512], in0=bk[1 + i][0:2, :],
                                                           in1=bgate[:, i * 512:(i + 1) * 512], op=ALU.add),
                     [BK[1 + i], "bgate"], ["grow"])
            P.barrier()

        def build_gate_bc(gbc, key, b, g):
            for hf in range(2):
                mm(bk[6 + hf][:], [(sel[0:2, b * 128:(b + 1) * 128], grow[0:2, g * 1024 + hf * 512:g * 1024 + (hf + 1) * 512])],
                   ["sel", "grow"], [BK[6 + hf]])
                P.op("dve", I("tensor_copy", out=gbc[:, hf * 512:(hf + 1) * 512], in_=bk[6 + hf][:]),
                     [BK[6 + hf]], [key])

        def norm_to_hT(st, b, pre_fn, fin_fn, aX, shcol0, tag):
            xn = [SB(st, f"xn{tag}{i}", [128, D], BF16) for i in range(2)]
            junk = SB(st, f"junk{tag}", [128, D], BF16)
            stt = [SB(st, f"stt{tag}{i}", [128, 4], F32) for i in range(2)]
            held = {}

            def do_block(j):
                if j == 0:
                    pre_fn(0)
                if j + 1 < NB:
                    pre_fn(j + 1)
                if j < NB:
                    xs_ap, xs_keys = fin_fn(j)
                    s2 = j % 2
                    act(junk[:], xs_ap, AF.Square, xs_keys, [f"junk{tag}", f"stt{tag}{s2}"], accum_out=stt[s2][:, 0:1])
                    act(stt[s2][:, 1:2], stt[s2][:, 0:1], AF.Ln, [f"stt{tag}{s2}"], [f"stt{tag}{s2}"], bias=EPS, scale=1.0 / D)
                    act(stt[s2][:, 2:3], stt[s2][:, 1:2], AF.Exp, [f"stt{tag}{s2}"], [f"stt{tag}{s2}"], scale=-0.5)
                if j >= 1:
                    jb = j - 1
                    sb2 = jb % 2
                    tpa, tpb = bkb(4 + sb2), bkb(6 + sb2)
                    for c in range(4):
                        act(hT[:, c, jb * 128:(jb + 1) * 128], tpa[:, c * 128:(c + 1) * 128], AF.Identity,
                            [BK[4 + sb2]], [f"hT{jb}"], bias=adaT[:, shcol0 + c, b:b + 1], scale=aX[:, c, b:b + 1])
                    for c in range(4, 8):
                        P.op("dve", I("tensor_scalar", out=hT[:, c, jb * 128:(jb + 1) * 128], in0=tpb[:, (c - 4) * 128:(c - 3) * 128],
                                      scalar1=aX[:, c, b:b + 1], scalar2=adaT[:, shcol0 + c, b:b + 1], op0=ALU.mult, op1=ALU.add),
                             [BK[6 + sb2]], [f"hT{jb}"])
                if j < NB:
                    P.op("dve", I("tensor_scalar", out=xn[s2][:], in0=xs_ap, scalar1=stt[s2][:, 2:3], scalar2=None, op0=ALU.mult),
                         xs_keys + [f"stt{tag}{s2}"], [f"xn{tag}{s2}"])
                    for hf in range(2):
                        tb = 4 + 2 * hf + s2
                        tp = bkb(tb)
                        P.op("pe", [I("transpose", out=tp[:, c * 128:(c + 1) * 128], in_=xn[s2][:, (4 * hf + c) * 128:(4 * hf + c + 1) * 128],
                                      identity=ident_bf[:]) for c in range(4)], [f"xn{tag}{s2}", "ident_bf"], [BK[tb]])
            return do_block

        hT_all = [f"hT{j}" for j in range(NB)]

        def hT_keys(t):
            return [f"hT{j}" for j in range(4 * t, 4 * t + 4)]

        for b in range(nseq):
            with contextlib.ExitStack() as st:
                xs = [SB(st, f"xs{i}", [128, D], F32) for i in range(3)]

                def ldx1(j):
                    P.dma("sp", xs[j % 3][:], x[b, j * 128:(j + 1) * 128, :], f"ld_xs{j % 3}", (), [f"xs{j % 3}"])

                def fin1(j):
                    return xs[j % 3][:], [f"xs{j % 3}"]
                blk1 = norm_to_hT(st, b, ldx1, fin1, a1, 0, "a")
                for j in range(NB + 1):
                    blk1(j)
                P.barrier()
            if stop_after <= 1:
                continue

            def gates(want_m, want_f, Grow=None, A_tm=None, X_tm=None, NP_=None):
                with contextlib.ExitStack() as st:
                    wi = SB(st, "wi", [128, KC, 4], BF16)
                    wf = SB(st, "wf", [128, KC, 20], BF16)
                    ipre = SB(st, "ipre", [4, S], F32)
                    spl = SB(st, "spl", [20, S], F32)
                    negF = SB(st, "negF", [20, S], F32)
                    P.dma("pool", wi[:], w_in_v[:, :, O_MI:O_MI + 4], "ld_wi", (), ["wi"])
                    P.dma("pool", wf[:, :, 0:4], w_in_v[:, :, O_MF:O_MF + 4], "ld_wf", (), ["wf"])
                    P.dma("pool", wf[:, :, 4:20], w_in_v[:, :, O_FF:O_FF + 16], "ld_wf", (), ["wf"])
                    for t in range(4):
                        ts_ = slice(t * 512, (t + 1) * 512)
                        mm(bk[t % 2][0:4, :], [(wi[:, k, :], hT[:, k, ts_]) for k in range(KC)], ["wi"] + hT_keys(t), [BK[t % 2]])
                        mm(bk[2 + t % 2][0:20, :], [(wf[:, k, :], hT[:, k, ts_]) for k in range(KC)], ["wf"] + hT_keys(t), [BK[2 + t % 2]])
                        act(ipre[:, ts_], bk[t % 2][0:4, :], AF.Identity, [BK[t % 2], "bi"], ["ipre"], bias=bi[:, 0:1])
                        act(spl[:, ts_], bk[2 + t % 2][0:20, :], AF.Exp, [BK[2 + t % 2], "nbf20"], ["spl"], bias=nbf20[:, 0:1], scale=-1.0)
                    act(spl[:], spl[:], AF.Ln, ["spl"], ["spl"], bias=1.0)
                    P.op("dve", I("tensor_tensor_scan", out=negF[:], data0=spl[:], data1=spl[:], initial=0.0, op0=ALU.add, op1=ALU.max),
                         ["spl"], ["negF"])
                    if want_m:
                        Am = SB(st, "Am", [4, S], F32)
                        Xm = SB(st, "Xm", [4, S], F32)
                        P.op("dve", I("tensor_tensor", out=Am[:], in0=ipre[:], in1=negF[0:4, :], op=ALU.add), ["ipre", "negF"], ["Am"])
                        P.op("dve", I("memset", Grow[:, 0:1], 0.0), (), ["Grow"])
                        P.op("dve", I("tensor_tensor_scan", out=Grow[:, 1:S + 1], data0=Am[:], data1=Am[:], initial=0.0, op0=ALU.max, op1=ALU.max),
                             ["Am"], ["Grow"])
                        P.op("dve", I("tensor_tensor", out=Xm[:], in0=negF[0:4, :], in1=Grow[:, 1:S + 1], op=ALU.subtract), ["negF", "Grow"], ["Xm"])
                        P.op("pe", [I("transpose", out=bk[4][:, j * 4:(j + 1) * 4], in_=Am[0:4, j * 128:(j + 1) * 128],
                                                               identity=ident_f[0:4, 0:4]) for j in range(NB)], ["Am", "ident_f"], [BK[4]])
                        P.op("pe", [I("transpose", out=bk[5][:, j * 4:(j + 1) * 4], in_=Xm[0:4, j * 128:(j + 1) * 128],
                                                               identity=ident_f[0:4, 0:4]) for j in range(NB)], ["Xm", "ident_f"], [BK[5]])
                        P.op("dve", I("tensor_copy", out=A_tm[:], in_=bk[4][:, 0:64]), [BK[4]], ["A_tm"])
                        P.op("dve", I("tensor_copy", out=X_tm[:], in_=bk[5][:, 0:64]), [BK[5]], ["X_tm"])
                    if want_f:
                        r1 = SB(st, "r1", [20, S], F32)
                        NP_ = SB(st, "NP", [20, 3, S], BF16)
                        P.op("dve", I("tensor_copy", out=NP_[:, 0, :], in_=negF[:]), ["negF"], ["NP"])
                        P.op("dve", I("tensor_tensor", out=r1[:], in0=negF[:], in1=NP_[:, 0, :], op=ALU.subtract), ["negF", "NP"], ["r1"])
                        P.op("dve", I("tensor_copy", out=NP_[:, 1, :], in_=r1[:]), ["r1"], ["NP"])
                        P.op("dve", I("tensor_tensor", out=r1[:], in0=r1[:], in1=NP_[:, 1, :], op=ALU.subtract), ["r1", "NP"], ["r1"])
                        P.op("dve", I("tensor_copy", out=NP_[:, 2, :], in_=r1[:]), ["r1"], ["NP"])
                        P.dma("sp", npscr.rearrange("p (r t) -> p r t", r=3), NP_[:], "st_np", ["NP"], ["npscr"])
                    P.barrier()

            with contextlib.ExitStack() as st2:
                hmT = SB(st2, "hmT", [128, KC, S], BF16)
                st2b = contextlib.ExitStack()
                Grow = SB(st2b, "Grow", [4, S + 1], F32)
                A_tm = SB(st2b, "A_tm", [128, 64], F32)
                X_tm = SB(st2b, "X_tm", [128, 64], F32)
                gates(True, True, Grow=Grow, A_tm=A_tm, X_tm=X_tm)

                with contextlib.ExitStack() as st:
                    qT = SB(st, "qT", [128, 2, S], BF16)
                    kT = SB(st, "kT", [128, 2, S], BF16)
                    ktm = SB(st, "ktm", [128, NB, 256], BF16)
                    vA = SB(st, "vA", [128, NB, 258], BF16)
                    vwa = SB(st, "vwa", [128, NB, 258], BF16)
                    STb = SB(st, "STb", [128, S], BF16)
                    oS = SB(st, "oS", [128, NB, 256], BF16)
                    Cf = SB(st, "Cf", [128, 2, 257], F32)
                    Cb = [SB(st, f"Cb{i}", [128, 2, 258], BF16) for i in range(2)]
                    sta = SB(st, "sta", [128, 12, 16], F32)
                    hmtm = [SB(st, f"hmtm{i}", [128, 256], BF16) for i in range(2)]
                    junk4 = SB(st, "junk4", [128, 256], BF16)
                    wq_ = [SB(st, f"wq{i}", [128, KC, 256], BF16) for i in range(2)]
                    wk_ = [SB(st, f"wk{i}", [128, KC, 256], BF16) for i in range(2)]
                    wv_ = [SB(st, f"wv{i}", [128, KC, 256], BF16) for i in range(2)]
                    Gbc = SB(st, "Gbc", [128, S + 1], F32)
                    DTa = SB(st, "DTa", [128, S], F32)
                    Eba = SB(st, "Eba", [128, S], F32)
                    tmpS = [SB(st, f"tmpS{i}", [128, 512], F32) for i in range(2)]

                    def ld_m(h):
                        i = h % 2
                        P.dma("pool", wq_[i][:], w_in_v[:, :, O_MQ + 256 * h:O_MQ + 256 * h + 256], f"ld_wq{i}", (), [f"wq{i}"])
                        P.dma("pool", wk_[i][:], w_in_v[:, :, O_MK + 256 * h:O_MK + 256 * h + 256], f"ld_wk{i}", (), [f"wk{i}"])
                        P.dma("pool", wv_[i][:], w_in_v[:, :, O_MV + 256 * h:O_MV + 256 * h + 256], f"ld_wv{i}", (), [f"wv{i}"])
                    ld_m(0)
                    P.op("dve", I("memset", vA[:, :, 256:257], 1.0), (), [f"vA{j}" for j in range(NB)])
                    WALL, DCY, EXA, SS, ND, MDEN, RR, VAR, FAC, TMP, DEN, FACP = range(12)

                    def post_block(h, j):
                        s2 = j % 2
                        js = slice(j * 128, (j + 1) * 128)
                        tb = 2 + s2
                        tp = bkb(tb)
                        act(hmtm[s2][:], oS[:, j, :], AF.Identity, [f"oS{j}", "staFP"], [f"hmtm{s2}"], scale=sta[:, FACP, j:j + 1])
                        P.op("pe", [I("transpose", out=tp[:, c * 128:(c + 1) * 128], in_=hmtm[s2][:, c * 128:(c + 1) * 128],
                                      identity=ident_bf[:]) for c in range(2)], [f"hmtm{s2}", "ident_bf"], [BK[tb]])
                        P.op("dve", I("tensor_copy", out=hmT[:, 2 * h:2 * h + 2, js], in_=tp[:, 0:256].rearrange("p (c t) -> p c t", t=128)),
                             [BK[tb]], ["hmT"])

                    pending_post = []
                    for h in range(4):
                        wq, wk, wv = wq_[h % 2], wk_[h % 2], wv_[h % 2]
                        wqk, wkk, wvk = f"wq{h % 2}", f"wk{h % 2}", f"wv{h % 2}"
                        if h + 1 < 4:
                            ld_m(h + 1)
                        for t5 in range(5):
                            c0, c1 = t5 * 512, min(S + 1, (t5 + 1) * 512)
                            gb_ = 6 + t5 % 2
                            mm(bk[gb_][:, 0:c1 - c0], [(sel[0:4, h * 128:(h + 1) * 128], Grow[0:4, c0:c1])], ["sel", "Grow"], [BK[gb_]])
                            P.op("dve", I("tensor_copy", out=Gbc[:, c0:c1], in_=bk[gb_][:, 0:c1 - c0]), [BK[gb_]], ["Gbc"])
                        g0v = Gbc[:, 0:S].rearrange("p (j t) -> p j t", t=128)[:, :, 0]
                        gEv = Gbc[:, 1:S + 1].rearrange("p (j t) -> p j t", t=128)[:, :, 127]
                        Av = A_tm[:].rearrange("p (j h) -> p j h", h=4)[:, :, h]
                        Xv = X_tm[:].rearrange("p (j h) -> p j h", h=4)[:, :, h]
                        P.op("dve", I("tensor_tensor", out=sta[:, TMP, :], in0=Av, in1=gEv, op=ALU.subtract), ["A_tm", "Gbc"], ["staT"])
                        act(sta[:, WALL, :], sta[:, TMP, :], AF.Exp, ["staT"], ["staW"])
                        P.op("dve", I("tensor_tensor", out=sta[:, ND, :], in0=g0v, in1=gEv, op=ALU.subtract), ["Gbc"], ["staN"])
                        act(sta[:, DCY, :], sta[:, ND, :], AF.Exp, ["staN"], ["staD"])
                        act(sta[:, EXA, :], Xv, AF.Exp, ["X_tm"], ["staE"])
                        exq = []
                        for j in range(NB):
                            js = slice(j * 128, (j + 1) * 128)
                            jg = slice(1 + j * 128, 1 + (j + 1) * 128)
                            exq.append((DTa[:, js], Gbc[:, jg], ["Gbc", "A_tm"], [f"DTa{j // 4}"], A_tm[:, j * 4 + h:j * 4 + h + 1]))
                            exq.append((Eba[:, js], Gbc[:, jg], ["Gbc"], ["Eba"], Gbc[:, j * 128:j * 128 + 1]))
                        pj = 0
                        for (wt, wkey, dst, dkey, scl) in ((wq, wqk, qT, "qT", 1.0 / 16), (wk, wkk, kT, "kT", 1.0)):
                            for c in range(2):
                                for t in range(4):
                                    ts_ = slice(t * 512, (t + 1) * 512)
                                    pb = 4 + pj % 2
                                    mm(bk[pb][:], [(wt[:, k, c * 128:(c + 1) * 128], hT[:, k, ts_]) for k in range(KC)], [wkey] + hT_keys(t), [BK[pb]])
                                    act(dst[:, c, ts_], bk[pb][:], AF.Identity, [BK[pb]], [dkey], scale=scl)
                                    for _ in range(2):
                                        o_, i_, r_, w_, b_ = exq.pop(0)
                                        act(o_, i_, AF.Exp, r_, w_, bias=b_, scale=-1.0)
                                    if pending_post:
                                        post_block(*pending_post.pop(0))
                                    pj += 1
                        for j2 in range(NB // 2):
                            pb = 4 + pj % 2
                            pj += 1
                            for jj in range(2):
                                j = j2 * 2 + jj
                                mm(bk[pb][:, jj * 256:(jj + 1) * 256], [(hT[:, k, j * 128:(j + 1) * 128], wv[:, k, :]) for k in range(KC)],
                                   [wvk, f"hT{j}"], [BK[pb]])
                            act(vA[:, j2 * 2:j2 * 2 + 2, 0:256], bk[pb][:].rearrange("p (j c) -> p j c", c=256), AF.Identity, [BK[pb]],
                                [f"vA{j2 * 2}", f"vA{j2 * 2 + 1}"])
                            for jj in range(2):
                                j = j2 * 2 + jj
                                P.op("dve", I("tensor_scalar", out=vwa[:, j, 0:257], in0=vA[:, j, 0:257], scalar1=sta[:, WALL, j:j + 1], scalar2=None, op0=ALU.mult),
                                     [f"vA{j}", "staW"], [f"vwa{j}"])
                        for g4 in range(4):
                            tb = 6 + g4 % 2
                            tp = bkb(tb)
                            fns = []
                            for jj in range(4):
                                j = g4 * 4 + jj
                                for c in range(2):
                                    fns.append(I("transpose", out=tp[:, jj * 256 + c * 128:jj * 256 + (c + 1) * 128],
                                                 in_=kT[:, c, j * 128:(j + 1) * 128], identity=ident_bf[:]))
                            P.op("pe", fns, ["kT", "ident_bf"], [BK[tb]])
                            P.op("dve", I("tensor_copy", out=ktm[:, g4 * 4:g4 * 4 + 4, :], in_=tp.rearrange("p (j c) -> p j c", c=256)),
                                 [BK[tb]], ["ktm"])
                        while pending_post:
                            post_block(*pending_post.pop(0))
                        for g4 in range(4):
                            gs = slice(g4 * 512, (g4 + 1) * 512)
                            fns = []
                            for jj in range(4):
                                j = g4 * 4 + jj
                                js = slice(j * 128, (j + 1) * 128)
                                for c in range(2):
                                    fns.append(I("matmul", bk[g4][:, jj * 128:(jj + 1) * 128], lhsT=kT[:, c, js], rhs=qT[:, c, js],
                                                 start=(c == 0), stop=(c == 1)))
                            P.op("pe", fns, ["kT", "qT"], [BK[g4]])
                            P.op("dve", I("tensor_tensor", out=tmpS[g4 % 2][:], in0=bk[g4][:], in1=DTa[:, gs], op=ALU.mult),
                                 [BK[g4], f"DTa{g4}"], [f"tmpS{g4 % 2}"])
                            P.op("dve", I("tensor_tensor", out=STb[:, gs], in0=tmpS[g4 % 2][:], in1=mask4[:], op=ALU.mult),
                                 [f"tmpS{g4 % 2}", "mask4"], [f"STb{g4}"])
                        for c in range(2):
                            P.op("dve", I("tensor_tensor", out=qT[:, c, :], in0=qT[:, c, :], in1=Eba[:], op=ALU.mult), ["qT", "Eba"], ["qT"])
                        P.op("dve", I("memset", Cf[:], 0.0), (), ["Cf0", "Cf1"])
                        P.op("dve", I("memset", Cb[0][:], 0.0), (), ["Cb0"])

                        def upd(j):
                            for c in range(2):
                                ub = 4 + (j % 2) * 2 + c
                                mm(bk[ub][:, 0:257], [(ktm[:, j, c * 128:(c + 1) * 128], vwa[:, j, 0:257])], ["ktm", f"vwa{j}"], [BK[ub]])
                        upd(0)
                        for j in range(NB):
                            s2 = j % 2
                            js = slice(j * 128, (j + 1) * 128)
                            if j + 1 < NB:
                                upd(j + 1)
                            ob = s2
                            mm(bk[ob][:, 0:257], [(STb[:, js], vA[:, j, 0:257])] + [(qT[:, c, js], Cb[s2][:, c, 0:257]) for c in range(2)],
                               [f"STb{j // 4}", f"vA{j}", "qT", f"Cb{s2}"], [BK[ob]])
                            for c in range(2):
                                ub = 4 + s2 * 2 + c
                                P.op("dve", I("scalar_tensor_tensor", out=Cf[:, c, :], in0=Cf[:, c, :], scalar=sta[:, DCY, j:j + 1], in1=bk[ub][:, 0:257],
                                              op0=ALU.mult, op1=ALU.add), [f"Cf{c}", "staD", BK[ub]], [f"Cf{c}"])
                            if j + 1 < NB:
                                act(Cb[1 - s2][:, :, 0:257], Cf[:], AF.Identity, ["Cf0", "Cf1"], [f"Cb{1 - s2}"])
                            act(junk4[:], bk[ob][:, 0:256], AF.Square, [BK[ob]], ["junk4", "staS"], accum_out=sta[:, SS, j:j + 1])
                            act(oS[:, j, :], bk[ob][:, 0:256], AF.Identity, [BK[ob]], [f"oS{j}"])
                            act(sta[:, DEN, j:j + 1], bk[ob][:, 256:257], AF.Identity, [BK[ob]], ["staDen"])
                        denv = sta[:, DEN, :]
                        P.op("dve", I("tensor_scalar", out=sta[:, ND, :], in0=denv, scalar1=-1.0, scalar2=None, op0=ALU.mult), ["staDen"], ["staN"])
                        P.op("dve", I("tensor_tensor", out=sta[:, MDEN, :], in0=sta[:, ND, :], in1=sta[:, EXA, :], op=ALU.max), ["staN", "staE"], ["staM"])
                        P.op("dve", I("tensor_tensor", out=sta[:, MDEN, :], in0=sta[:, MDEN, :], in1=denv, op=ALU.max), ["staM", "staDen"], ["staM"])
                        P.op("dve", I("reciprocal", out=sta[:, RR, :], in_=sta[:, MDEN, :]), ["staM"], ["staR"])
                        P.op("dve", I("tensor_tensor", out=sta[:, VAR, :], in0=sta[:, SS, :], in1=sta[:, RR, :], op=ALU.mult), ["staS", "staR"], ["staV"])
                        P.op("dve", I("tensor_tensor", out=sta[:, VAR, :], in0=sta[:, VAR, :], in1=sta[:, RR, :], op=ALU.mult), ["staV", "staR"], ["staV"])
                        act(sta[:, VAR, :], sta[:, VAR, :], AF.Ln, ["staV"], ["staV"], bias=EPS, scale=1.0 / 256)
                        act(sta[:, VAR, :], sta[:, VAR, :], AF.Exp, ["staV"], ["staV"], scale=-0.5)
                        P.op("dve", I("tensor_tensor", out=sta[:, FACP, :], in0=sta[:, VAR, :], in1=sta[:, RR, :], op=ALU.mult), ["staV", "staR"], ["staFP"])
                        for j in range(NB):
                            post_block(h, j)
                    while pending_post:
                        post_block(*pending_post.pop(0))
                    P.barrier()
                st2b.close()

                hfT = SB(st2, "hfT", [128, KC, S], BF16)
                st3b = contextlib.ExitStack()

                with contextlib.ExitStack() as st:
                    qaug = [[SB(st, f"qaug{s}{i}", [128, S], BF16) for i in range(2)] for s in range(2)]
                    kaug = [[SB(st, f"kaug{s}{i}", [128, S], BF16) for i in range(2)] for s in range(2)]
                    vaug = [SB(st, f"vaug{s}", [128, NB, 192], BF16) for s in range(2)]
                    wq2_ = [SB(st, f"wq2{i}", [128, KC, 128], BF16) for i in range(2)]
                    wk2_ = [SB(st, f"wk2{i}", [128, KC, 128], BF16) for i in range(2)]
                    wv2_ = [SB(st, f"wv2{i}", [128, KC, 128], BF16) for i in range(2)]

                    def ld_f(p):
                        i = p % 2
                        P.dma("pool", wq2_[i][:], w_in_v[:, :, O_FQ + 128 * p:O_FQ + 128 * p + 128], f"ld_wq2{i}", (), [f"wq2{i}"])
                        P.dma("pool", wk2_[i][:], w_in_v[:, :, O_FK + 128 * p:O_FK + 128 * p + 128], f"ld_wk2{i}", (), [f"wk2{i}"])
                        P.dma("pool", wv2_[i][:], w_in_v[:, :, O_FV + 128 * p:O_FV + 128 * p + 128], f"ld_wv2{i}", (), [f"wv2{i}"])
                    ld_f(0)
                    sq = [SB(st, f"sq{i}", [128, 512], BF16) for i in range(2)]
                    rs = [SB(st, f"rs{i}", [128, 512], F32) for i in range(2)]
                    pT = [SB(st, f"pT{i}", [128, 512], BF16) for i in range(4)]
                    rden = [SB(st, f"rden{i}", [128, 512], F32) for i in range(2)]
                    for s_ in range(2):
                        for i in range(2):
                            P.op("dve", I("memset", qaug[s_][i][64:128, :], 0.0), (), [f"qaug{s_}{i}"])
                            P.op("dve", I("memset", kaug[s_][i][64:128, :], 0.0), (), [f"kaug{s_}{i}"])
                            P.op("dve", I("memset", qaug[s_][i][64:70, :], 1.0), (), [f"qaug{s_}{i}"])
                            P.op("dve", I("memset", kaug[s_][i][64:70, :], -1.0), (), [f"kaug{s_}{i}"])
                        P.op("dve", I("memset", vaug[s_][:, :, 64:128], 1.0), (), [f"vaug{s_}"])
                    cnt3 = {"sq": 0, "pj": 0}

                    def fox_proj(p):
                        for _ in fox_proj_gen(p):
                            pass

                    def fox_proj_gen(p):
                        s_ = p % 2
                        wq2, wk2, wv2 = wq2_[p % 2], wk2_[p % 2], wv2_[p % 2]
                        if p + 1 < 8:
                            ld_f(p + 1)
                        sched = {}

                        def at(tick, fn):
                            sched.setdefault(tick, []).append(fn)
                        u = 0
                        for (wt, wkey, aug, akey, wcol) in ((wq2, f"wq2{p % 2}", qaug, "qaug", fqw_s), (wk2, f"wk2{p % 2}", kaug, "kaug", fkw_s)):
                            for t in range(4):
                                ts_ = slice(t * 512, (t + 1) * 512)
                                pb = 5 + u % 2
                                i2 = u % 2
                                t0 = 8 * u

                                def s0c(ch, wt=wt, wkey=wkey, ts_=ts_, pb=pb, t=t):
                                    P.op("pe", [I("matmul", bk[pb][:], lhsT=wt[:, k, 0:128], rhs=hT[:, k, ts_], start=(k == 0), stop=(k == KC - 1))
                                                for k in (2 * ch, 2 * ch + 1)], [wkey] + hT_keys(t), [BK[pb]])

                                def s1(pb=pb, i2=i2):
                                    act(sq[i2][:], bk[pb][:], AF.Square, [BK[pb]], [f"sq{i2}"])

                                def s2(i2=i2):
                                    mm(bk[7][:], [(bdones[:], sq[i2][:])], ["bdones", f"sq{i2}"], [BK[7]])

                                def s3(i2=i2):
                                    act(rs[i2][:], bk[7][:], AF.Ln, [BK[7]], [f"rs{i2}"], bias=EPS, scale=1.0 / 64)
                                    act(rs[i2][:], rs[i2][:], AF.Exp, [f"rs{i2}"], [f"rs{i2}"], scale=-0.5)

                                def s4(aug=aug, akey=akey, wcol=wcol, ts_=ts_, pb=pb, i2=i2):
                                    for hh in range(2):
                                        pr = slice(hh * 64, hh * 64 + 64)
                                        P.op("dve", I("scalar_tensor_tensor", out=aug[s_][hh][0:64, ts_], in0=bk[pb][pr, :], scalar=wcol[pr, 0:1],
                                                      in1=rs[i2][pr, :], op0=ALU.mult, op1=ALU.mult),
                                             [BK[pb], f"rs{i2}", "fqw_s", "fkw_s"], [f"{akey}{s_}{hh}"])
                                for ch in range(4):
                                    at(t0 + ch, (lambda ch=ch, f=s0c: f(ch)))
                                for si, fn in enumerate((s1, s2, s3, s4)):
                                    at(t0 + 6 + 3 * si, fn)
                                u += 1
                        for g4, t0 in enumerate((64, 72, 76, 80)):
                            pb = 5 + g4 % 2

                            def v0(jj, g4=g4, pb=pb):
                                j = g4 * 4 + jj
                                mm(bk[pb][:, jj * 128:(jj + 1) * 128], [(hT[:, k, j * 128:(j + 1) * 128], wv2[:, k, 0:128]) for k in range(KC)],
                                   [f"wv2{p % 2}", f"hT{j}"], [BK[pb]])

                            def v1(g4=g4, pb=pb):
                                pv = bk[pb][:].rearrange("p (j c) -> p j c", c=128)
                                act(vaug[s_][:, g4 * 4:g4 * 4 + 4, 0:64], pv[:, :, 0:64], AF.Identity, [BK[pb]], [f"vaug{s_}"])
                                act(vaug[s_][:, g4 * 4:g4 * 4 + 4, 128:192], pv[:, :, 64:128], AF.Identity, [BK[pb]], [f"vaug{s_}"])
                            for jj in range(4):
                                at(t0 + jj, (lambda jj=jj, f=v0: f(jj)))
                            at(t0 + 6, v1)

                        def dmas():
                            for hh in range(2):
                                h = 2 * p + hh
                                for r3 in range(3):
                                    P.dma("sp", kaug[s_][hh][64 + r3:65 + r3, :], npscr[4 + h:5 + h, r3 * S:(r3 + 1) * S], f"ld_ka{s_}{hh}", ["npscr"], [f"kaug{s_}{hh}"])
                                    P.dma("sp", qaug[s_][hh][67 + r3:68 + r3, :], npscr[4 + h:5 + h, r3 * S:(r3 + 1) * S], f"ld_qa{s_}{hh}", ["npscr"], [f"qaug{s_}{hh}"])
                        at(1, dmas)
                        for tick in range(max(sched) + 1):
                            for fn in sched.get(tick, ()):
                                fn()
                            yield

                    cnt_a = {"s": 0, "p": 0, "o": 0}

                    def fox_att(p, hh, gen=None):
                        s_ = p % 2
                        qa, ka, va = qaug[s_][hh], kaug[s_][hh], vaug[s_]
                        qk, kk, vk = f"qaug{s_}{hh}", f"kaug{s_}{hh}", f"vaug{s_}"
                        vlo = 0 if hh == 0 else 64
                        ol = 0 if hh == 0 else 64
                        dl = 64 if hh == 0 else 0
                        for g in range(4):
                            ob = 3 + cnt_a["o"] % 2
                            cnt_a["o"] += 1
                            nkb = 4 * g + 4
                            pend = []

                            def score(i):
                                r = i - 4 * g
                                qo = 128 * r if r >= 0 else 0
                                sb_ = cnt_a["s"] % 3
                                cnt_a["s"] += 1
                                pairs = [(ka[:, i * 128:(i + 1) * 128], qa[:, 512 * g + qo:512 * g + 512])]
                                fns = [I("matmul", bk[sb_][:, qo:512], lhsT=pairs[0][0], rhs=pairs[0][1], start=True, stop=(r < 0))]
                                if r >= 0:
                                    fns.append(I("matmul", bk[sb_][:, qo:qo + 128], lhsT=ident_bf[:], rhs=maskb[:], start=False, stop=True))
                                P.op("pe", fns, [kk, qk, "ident_bf", "maskb"], [BK[sb_]])
                                return (i, qo, sb_)

                            def expo(i, qo, sb_):
                                pi = cnt_a["p"] % 4
                                cnt_a["p"] += 1
                                act(pT[pi][:, qo:512], bk[sb_][:, qo:512], AF.Exp, [BK[sb_]], [f"pT{pi}"])
                                return pi

                            def pv(i, qo, pi):
                                P.op("pe", I("matmul", bk[ob][:, qo:512], lhsT=va[:, i, vlo:vlo + 128], rhs=pT[pi][:, qo:512],
                                                                             start=(i == 0), stop=(i == nkb - 1)), [vk, f"pT{pi}"], [BK[ob]])
                            scq = [score(i) for i in range(min(2, nkb))]
                            for i in range(nkb):
                                if i + 2 < nkb:
                                    scq.append(score(i + 2))
                                sc0 = scq.pop(0)
                                pi = expo(*sc0)
                                pv(sc0[0], sc0[1], pi)
                                if gen is not None:
                                    next(gen, None)
                            rd = rden[g % 2]
                            P.op("dve", I("reciprocal", out=rd[dl:dl + 64, :], in_=bk[ob][dl:dl + 64, :]), [BK[ob]], [f"rden{g % 2}"])
                            P.op("dve", I("tensor_tensor", out=hfT[ol:ol + 64, p, g * 512:(g + 1) * 512], in0=bk[ob][ol:ol + 64, :],
                                                                         in1=rd[dl:dl + 64, :], op=ALU.mult), [BK[ob], f"rden{g % 2}"], ["hfT"])

                    fox_proj(0)
                    for p in range(8):
                        if fox_limit <= 2 * p:
                            break
                        gen = fox_proj_gen(p + 1) if p + 1 < 8 else None
                        fox_att(p, 0, gen)
                        if fox_limit <= 2 * p + 1:
                            break
                        fox_att(p, 1, gen)
                        if gen is not None:
                            for _ in gen:
                                pass
                    P.barrier()
                st3b.close()

                with contextlib.ExitStack() as st:
                    wo = [SB(st, f"wo{i}", [128, KC, 512], BF16) for i in range(2)]
                    sg = [SB(st, f"sg{i}", [128, 512], F32) for i in range(2)]
                    n4 = 0
                    for half in range(2):
                        P.dma("pool", wo[half][:], w_in_v[:, :, O_MO + 512 * half:O_MO + 512 * half + 512], f"ld_wo{half}", (), [f"wo{half}"])
                    for c in range(KC):
                        half, cl = c // 4, (c % 4) * 128
                        for t in range(4):
                            ts_ = slice(t * 512, (t + 1) * 512)
                            pb = n4 % 4
                            i2 = n4 % 2
                            n4 += 1
                            mm(bk[pb][:], [(wo[half][:, k, cl:cl + 128], hT[:, k, ts_]) for k in range(KC)], [f"wo{half}"] + hT_keys(t), [BK[pb]])
                            act(sg[i2][:], bk[pb][:], AF.Sigmoid, [BK[pb]], [f"sg{i2}"])
                            P.op("dve", I("scalar_tensor_tensor", out=hmT[:, c, ts_], in0=hmT[:, c, ts_], scalar=mnw[:, c:c + 1],
                                                                                             in1=sg[i2][:], op0=ALU.mult, op1=ALU.mult),
                                 [f"hmTc{c}", "mnw", f"sg{i2}"], [f"hmTc{c}"])
                    P.barrier()

                st5 = contextlib.ExitStack()
                mgT = SB(st5, "mgT", [128, KC, S], BF16)
                wot = SB(st5, "wot", [128, KC, D], BF16)
                with contextlib.ExitStack() as st:
                    wsl = {nm: [SB(st, f"w5{nm}{i}", [128, KC, 256], BF16) for i in range(2)] for nm in ("bm", "bf", "ga", "gb")}
                    sga = [SB(st, "sga0", [128, 512], F32)] * 2
                    sgb = [SB(st, "sgb0", [128, 512], F32)] * 2
                    t1 = [SB(st, f"t1{i}", [128, 512], F32) for i in range(2)]
                    t2 = [SB(st, f"t2{i}", [128, 512], F32) for i in range(2)]
                    srcs = {"bm": (w_bm_v, 0), "bf": (w_bf_v, 0), "ga": (w_in_v, O_GA), "gb": (w_in_v, O_GB)}
                    n5 = 0
                    def ld_5(c2):
                        sl = c2 % 2
                        for nm in ("bm", "bf", "ga", "gb"):
                            src, off = srcs[nm]
                            P.dma("pool", wsl[nm][sl][:], src[:, :, off + 256 * c2:off + 256 * c2 + 256], f"ld_w5{nm}{sl}", (), [f"w5{nm}{sl}"])
                    ld_5(0)
                    for c2 in range(4):
                        sl = c2 % 2
                        if c2 + 1 < 4:
                            ld_5(c2 + 1)
                        else:
                            for half in range(2):
                                P.dma("pool", wot[:, :, half * 512:(half + 1) * 512], w_out_v[:, :, half * 512:(half + 1) * 512], "ld_wot", (), ["wot"])
                        for cc in range(2):
                            c = c2 * 2 + cc
                            cl = cc * 128
                            for t in range(4):
                                ts_ = slice(t * 512, (t + 1) * 512)
                                i2 = n5 % 2
                                pb0 = (n5 % 2) * 4
                                n5 += 1
                                mm(bk[pb0][:], [(wsl["bm"][sl][:, k, cl:cl + 128], hmT[:, k, ts_]) for k in range(KC)], [f"w5bm{sl}", "hmT"], [BK[pb0]])
                                mm(bk[pb0 + 1][:], [(wsl["bf"][sl][:, k, cl:cl + 128], hfT[:, k, ts_]) for k in range(KC)], [f"w5bf{sl}", "hfT"], [BK[pb0 + 1]])
                                mm(bk[pb0 + 2][:], [(wsl["ga"][sl][:, k, cl:cl + 128], hT[:, k, ts_]) for k in range(KC)], [f"w5ga{sl}"] + hT_keys(t), [BK[pb0 + 2]])
                                mm(bk[pb0 + 3][:], [(wsl["gb"][sl][:, k, cl:cl + 128], hT[:, k, ts_]) for k in range(KC)], [f"w5gb{sl}"] + hT_keys(t), [BK[pb0 + 3]])
                                act(sga[i2][:], bk[pb0 + 2][:], AF.Sigmoid, [BK[pb0 + 2]], ["sga"])
                                act(sgb[i2][:], bk[pb0 + 3][:], AF.Sigmoid, [BK[pb0 + 3]], ["sgb"])
                                P.op("dve", I("tensor_tensor", out=t1[i2][:], in0=bk[pb0][:], in1=sga[i2][:], op=ALU.mult),
                                     [BK[pb0], "sga"], [f"t1{i2}"])
                                P.op("dve", I("tensor_tensor", out=t2[i2][:], in0=bk[pb0 + 1][:], in1=sgb[i2][:], op=ALU.mult),
                                     [BK[pb0 + 1], "sgb"], [f"t2{i2}"])
                                P.op("dve", I("tensor_tensor", out=mgT[:, c, ts_], in0=t1[i2][:], in1=t2[i2][:], op=ALU.add),
                                     [f"t1{i2}", f"t2{i2}"], ["mgT"])
                    P.barrier()

                with contextlib.ExitStack() as st:
                    g1bc = SB(st, "g1bc", [128, D], F32)
                    xs = [SB(st, f"xr{i}", [128, D], F32) for i in range(2)]
                    x1 = [SB(st, f"x1{i}", [128, D], F32) for i in range(3)]
                    build_gate_bc(g1bc, "g1bc", b, 0)

                    def pre2(j):
                        sl = j % 2
                        P.dma("sp", xs[sl][:], x[b, j * 128:(j + 1) * 128, :], f"ld_xr{sl}", (), [f"xr{sl}"])
                        for hf in range(2):
                            pb = 2 * sl + hf
                            cs = slice(hf * 512, (hf + 1) * 512)
                            mm(bk[pb][:], [(mgT[:, k, j * 128:(j + 1) * 128], wot[:, k, cs]) for k in range(KC)], ["mgT", "wot"], [BK[pb]])

                    def fin2(j):
                        sl = j % 2
                        for hf in range(2):
                            pb = 2 * sl + hf
                            cs = slice(hf * 512, (hf + 1) * 512)
                            P.op("dve", I("tensor_tensor", out=x1[j % 3][:, cs], in0=bk[pb][:], in1=g1bc[:, cs], op=ALU.mult),
                                 [BK[pb], "g1bc"], [f"x1{j % 3}"])
                        P.op("dve", I("tensor_tensor", out=x1[j % 3][:], in0=x1[j % 3][:], in1=xs[sl][:], op=ALU.add), [f"x1{j % 3}", f"xr{sl}"], [f"x1{j % 3}"])
                        P.dma("sp", out[b, j * 128:(j + 1) * 128, :], x1[j % 3][:], f"st_x1{j % 3}", [f"x1{j % 3}"], [f"out{j}"])
                        return x1[j % 3][:], [f"x1{j % 3}"]
                    blk2 = norm_to_hT(st, b, pre2, fin2, a2, 24, "b")
                    for j in range(NB + 1):
                        blk2(j)
                    P.barrier()
                st5.close()
            if stop_after <= 5:
                continue

            with contextlib.ExitStack() as st:
                wdn = SB(st, "wdn", [128, NF, D], BF16)
                g2bc = SB(st, "g2bc", [128, D], F32)
                gT = SB(st, "gT", [128, NF, 1024], BF16)
                wug = [SB(st, f"wug{i}", [128, KC, 256], BF16) for i in range(2)]
                wuv = [SB(st, f"wuv{i}", [128, KC, 256], BF16) for i in range(2)]
                ug = [SB(st, f"ug{i}", [128, 1026], F32) for i in range(2)]
                uv = [SB(st, f"uv{i}", [128, 1026], F32) for i in range(2)]
                cg = [SB(st, f"cg{i}", [128, 512], F32) for i in range(3)]
                cv = [SB(st, f"cv{i}", [128, 512], F32) for i in range(3)]
                hal = SB(st, "hal", [128, 44, 2], F32)
                xr = [SB(st, f"xq{i}", [128, D], F32) for i in range(2)]
                ot = [SB(st, f"ot{i}", [128, D], F32) for i in range(2)]
                def ld_u(u):
                    f2_, sl_ = u % (NF // 2), u % 2
                    P.dma("pool", wug[sl_][:], w_up_v[:, :, 256 * f2_:256 * f2_ + 256], f"ld_wug{sl_}", (), [f"wug{sl_}"])
                    P.dma("pool", wuv[sl_][:], w_up_v[:, :, FF + 256 * f2_:FF + 256 * f2_ + 256], f"ld_wuv{sl_}", (), [f"wuv{sl_}"])
                ld_u(0)
                for f4 in range(0, NF, 4):
                    n_ = min(4, NF - f4)
                    P.dma("pool", wdn[:, f4:f4 + n_, :], w_down_v[:, f4:f4 + n_, :], "ld_wdn", (), ["wdn"])
                build_gate_bc(g2bc, "g2bc", b, 1)
                P.op("dve", I("memset", hal[:], 0.0), (), ["hal"])
                n6 = 0
                pend6 = []
                nu6 = [0]
                for half in range(2):
                    for f2 in range(NF // 2):
                        u_ = half * (NF // 2) + f2
                        sl = u_ % 2
                        if u_ + 1 < NF:
                            ld_u(u_ + 1)
                        for ff_ in range(2):
                            f = f2 * 2 + ff_
                            fl = ff_ * 128
                            us = n6 % 2
                            n6 += 1
                            for (ubuf, ukey, hidx) in ((ug[us], f"ug{us}", f), (uv[us], f"uv{us}", 22 + f)):
                                act(ubuf[:, 0:2], hal[:, hidx, :], AF.Identity, ["hal"], [ukey])
                            for tt in range(2):
                                t = half * 2 + tt
                                ts_ = slice(t * 512, (t + 1) * 512)
                                i2 = nu6[0] % 3
                                pbg, pbv = 2 * i2, 2 * i2 + 1
                                nu6[0] += 1
                                mm(bk[pbg][:], [(wug[sl][:, k, fl:fl + 128], hT[:, k, ts_]) for k in range(KC)], [f"wug{sl}"] + hT_keys(t), [BK[pbg]])
                                mm(bk[pbv][:], [(wuv[sl][:, k, fl:fl + 128], hT[:, k, ts_]) for k in range(KC)], [f"wuv{sl}"] + hT_keys(t), [BK[pbv]])
                                for (pb, ubuf, ukey, cbuf, ckey, widx) in ((pbg, ug[us], f"ug{us}", cg[i2], f"cg{i2}", f), (pbv, uv[us], f"uv{us}", cv[i2], f"cv{i2}", 22 + f)):
                                    u0 = 2 + tt * 512
                                    act(ubuf[:, u0:u0 + 512], bk[pb][:], AF.Identity, [BK[pb]], [ukey])
                                    act(cbuf[:], bk[pb][:], AF.Identity, [BK[pb], "convw", "convb"], [ckey], bias=convb[:, widx:widx + 1], scale=convw[:, widx, 2:3])
                                    P.op("dve", I("scalar_tensor_tensor",
                                        out=cbuf[:], in0=ubuf[:, u0 - 1:u0 + 511], scalar=convw[:, widx, 1:2], in1=cbuf[:], op0=ALU.mult, op1=ALU.add),
                                        [ukey, ckey, "convw"], [ckey])
                                    P.op("dve", I("scalar_tensor_tensor",
                                        out=cbuf[:], in0=ubuf[:, u0 - 2:u0 + 510], scalar=convw[:, widx, 0:1], in1=cbuf[:], op0=ALU.mult, op1=ALU.add),
                                        [ukey, ckey, "convw"], [ckey])
                                if pend6:
                                    pend6.pop()()

                                def fin(i2=i2, f=f, tt=tt):
                                    act(cg[i2][:], cg[i2][:], AF.Silu, [f"cg{i2}"], [f"cg{i2}"])
                                    P.op("dve", I("tensor_tensor", out=gT[:, f, tt * 512:(tt + 1) * 512], in0=cg[i2][:], in1=cv[i2][:], op=ALU.mult),
                                         [f"cg{i2}", f"cv{i2}"], ["gT"])
                                pend6.append(fin)
                            for (ubuf, ukey, hidx) in ((ug[us], f"ug{us}", f), (uv[us], f"uv{us}", 22 + f)):
                                act(hal[:, hidx, :], ubuf[:, 1024:1026], AF.Identity, [ukey], ["hal"])
                    if pend6:
                        pend6.pop()()
                    for jj in range(8):
                        j = half * 8 + jj
                        sl = jj % 2
                        P.dma("sp", xr[sl][:], out[b, j * 128:(j + 1) * 128, :], f"ld_xq{sl}", [f"out{j}"], [f"xq{sl}"])
                        for hf in range(2):
                            pb = 6 + hf
                            cs = slice(hf * 512, (hf + 1) * 512)
                            mm(bk[pb][:], [(gT[:, f, jj * 128:(jj + 1) * 128], wdn[:, f, cs]) for f in range(NF)], ["gT", "wdn"], [BK[pb]])
                            P.op("dve", I("tensor_tensor", out=ot[sl][:, cs], in0=bk[pb][:], in1=g2bc[:, cs], op=ALU.mult),
                                 [BK[pb], "g2bc"], [f"ot{sl}"])
                        P.op("dve", I("tensor_tensor", out=ot[sl][:], in0=ot[sl][:], in1=xr[sl][:], op=ALU.add), [f"ot{sl}", f"xq{sl}"], [f"ot{sl}"])
                        P.dma("sp", out[b, j * 128:(j + 1) * 128, :], ot[sl][:], f"st_ot{sl}", [f"ot{sl}"], [f"out{j}"])
                P.barrier()
        P.barrier()
        P.run()
    return nc


def _prep_inputs(inp, core):
    f = lambda a: np.ascontiguousarray(np.asarray(a, dtype=np.float32))
    b0 = 2 * core
    c = np.asarray(inp["c"], dtype=np.float32)[b0:b0 + 2]

    def pk(v, n):
        return f(np.asarray(v, dtype=np.float32).reshape(n, 128).T)
    conv_w = np.asarray(inp["conv_w"], dtype=np.float32)[0]
    m = {
        "x": f(np.asarray(inp["x"])[b0:b0 + 2]),
        "cT": f(c.T.reshape(KC, 128, 2).transpose(1, 0, 2)),
        "w_ada": f(inp["w_ada"][0]),
        "b_ada": f(inp["b_ada"][0]),
        "b_adaT": pk(inp["b_ada"][0], 48),
        "n1wT": pk(inp["norm1_w"][0], KC),
        "n2wT": pk(inp["norm2_w"][0], KC),
        "mnwT": pk(inp["mlstm_norm_w"][0], KC),
        "w_in": f(inp["w_in"][0]),
        "b_i": f(np.asarray(inp["b_mlstm_i"][0]).reshape(4, 1)),
        "b_f20": f(np.concatenate([np.asarray(inp["b_mlstm_f"][0]), np.asarray(inp["b_fox_f"][0])]).reshape(20, 1)),
        "fqw": f(np.tile(np.asarray(inp["fox_q_norm_w"][0]), 2).reshape(128, 1)),
        "fkw": f(np.tile(np.asarray(inp["fox_k_norm_w"][0]), 2).reshape(128, 1)),
        "w_bm": f(inp["w_branch_mlstm"][0]),
        "w_bf": f(inp["w_branch_fox"][0]),
        "w_out": f(inp["w_out"][0]),
        "w_up": f(inp["w_up"][0]),
        "convT": f(conv_w.reshape(3, 44, 128).transpose(2, 1, 0)),
        "convbT": pk(inp["conv_b"][0], 44),
        "w_down": f(inp["w_down"][0]),
    }
    return m


def kernel(**inputs):
    nc = build_program()
    in_maps = [_prep_inputs(inputs, i) for i in range(NCORES)]
    res = run_bass_kernel_spmd(nc, in_maps, core_ids=list(range(NCORES)))
    return np.concatenate([np.asarray(r["out"]) for r in res.results], axis=0).astype(np.float32)
```

```python
import contextlib
import os
import numpy as np
import concourse.bass as bass
import concourse.mybir as mybir
from concourse.bass_utils import run_bass_kernel_spmd

F32 = mybir.dt.float32
BF16 = mybir.dt.bfloat16
AF = mybir.ActivationFunctionType
ALU = mybir.AluOpType

ENGS = ("pe", "act", "dve", "pool", "sp")
NCORES = 8
S = 2048
NB = 16
D = 1024
KC = 8
FF = 2816
NF = 22
EPS = 1e-6
O_MQ, O_MK, O_MV, O_MO, O_MI, O_MF = 0, 1024, 2048, 3072, 4096, 4100
O_FQ, O_FK, O_FV, O_FF, O_GA, O_GB = 4104, 5128, 6152, 7176, 7192, 8216


class Prog:
    def __init__(self, nc, stack):
        self.nc = nc
        self.q = {e: [] for e in ENGS}
        self.cnt = {}
        self.sem = {}
        self.stack = stack
        for e in ENGS:
            self.sem[e] = stack.enter_context(nc.semaphore("s_" + e))
            self.cnt[e] = 0
        self.waited = {e: {} for e in ENGS}
        self.last_w = {}
        self.readers = {}

    def dma_sem(self, key):
        if key not in self.sem:
            self.sem[key] = self.stack.enter_context(self.nc.semaphore("d_" + key))
            self.cnt[key] = 0
        return key

    def _deps(self, eng, reads, writes):
        deps = {}

        def add(d):
            if d is None:
                return
            k, v = d
            if deps.get(k, 0) < v:
                deps[k] = v

        for r in reads:
            add(self.last_w.get(r))
            if r.startswith("bk"):
                for k, v in self.readers.get(r, {}).items():
                    if k != eng:
                        add((k, v))
        for w in writes:
            add(self.last_w.get(w))
            for k, v in self.readers.get(w, {}).items():
                add((k, v))
        return deps

    def _emit_waits(self, eng, deps):
        for k, v in deps.items():
            if self.waited[eng].get(k, 0) >= v:
                continue
            self.waited[eng][k] = v
            sem = self.sem[k]
            self.q[eng].append(lambda e, sem=sem, v=v: e.wait_ge(sem, v))

    def _mark(self, key, val, reads, writes):
        for r in reads:
            self.readers.setdefault(r, {})[key] = val
        for w in writes:
            self.last_w[w] = (key, val)
            self.readers[w] = {}

    def op(self, eng, fns, reads=(), writes=()):
        if not isinstance(fns, (list, tuple)):
            fns = [fns]
        self._emit_waits(eng, self._deps(eng, reads, writes))
        sem = self.sem[eng]
        self.cnt[eng] += 1
        n = self.cnt[eng]
        for f in fns[:-1]:
            self.q[eng].append(f)
        last = fns[-1]
        self.q[eng].append(lambda e, last=last, sem=sem: last(e).then_inc(sem, 1))
        self._mark(eng, n, reads, writes)
        return n

    def dma(self, qeng, out, in_, semkey, reads=(), writes=()):
        self.dma_sem(semkey)
        self._emit_waits(qeng, self._deps(None, reads, writes))
        sem = self.sem[semkey]
        self.cnt[semkey] += 16
        v = self.cnt[semkey]
        self.q[qeng].append(lambda e, out=out, in_=in_, sem=sem: e.dma_start(out=out, in_=in_).then_inc(sem, 16))
        self._mark(semkey, v, reads, writes)
        return v

    def barrier(self):
        allk = {k: v for k, v in self.cnt.items() if v > 0}
        for e in ENGS:
            self._emit_waits(e, dict(allk))
        self.last_w = {}
        self.readers = {}

    def run(self):
        nc = self.nc
        with nc.Block() as block:
            @block.tensor
            def _(e):
                for f in self.q["pe"]:
                    f(e)

            @block.scalar
            def _(e):
                for f in self.q["act"]:
                    f(e)

            @block.vector
            def _(e):
                for f in self.q["dve"]:
                    f(e)

            @block.gpsimd
            def _(e):
                for f in self.q["pool"]:
                    f(e)

            @block.sync
            def _(e):
                for f in self.q["sp"]:
                    f(e)


def build_program(nseq=2, stop_after=99, dbg=None, fox_limit=99):
    nc = bass.Bass("TRN2", target_bir_lowering=False)

    def din(name, shape):
        return nc.dram_tensor(name, list(shape), F32, kind="ExternalInput").ap()

    x = din("x", [2, S, D])
    cT = din("cT", [128, KC, 2])
    w_ada = din("w_ada", [D, 6 * D])
    b_ada = din("b_ada", [6 * D])
    b_adaT = din("b_adaT", [128, 48])
    n1wT = din("n1wT", [128, KC])
    n2wT = din("n2wT", [128, KC])
    mnwT = din("mnwT", [128, KC])
    w_in = din("w_in", [D, 9240])
    b_i = din("b_i", [4, 1])
    b_f20 = din("b_f20", [20, 1])
    fqw = din("fqw", [128, 1])
    fkw = din("fkw", [128, 1])
    w_bm = din("w_bm", [D, D])
    w_bf = din("w_bf", [D, D])
    w_out = din("w_out", [D, D])
    w_up = din("w_up", [D, 2 * FF])
    convT = din("convT", [128, 44, 3])
    convbT = din("convbT", [128, 44])
    w_down = din("w_down", [FF, D])
    out = nc.dram_tensor("out", [2, S, D], F32, kind="ExternalOutput").ap()
    npscr = nc.dram_tensor("npscr", [20, 3 * S], BF16).ap()
    dbg_t = None
    if dbg is not None:
        dbg_t = nc.dram_tensor("dbg", list(dbg), F32, kind="ExternalOutput").ap()

    w_in_v = w_in.rearrange("(k p) n -> p k n", p=128)
    w_ada_v = w_ada.rearrange("(k p) n -> p k n", p=128)
    w_bm_v = w_bm.rearrange("(k p) n -> p k n", p=128)
    w_bf_v = w_bf.rearrange("(k p) n -> p k n", p=128)
    w_out_v = w_out.rearrange("(k p) n -> p k n", p=128)
    w_up_v = w_up.rearrange("(k p) n -> p k n", p=128)
    w_down_v = w_down.rearrange("(k p) n -> p k n", p=128)

    with contextlib.ExitStack() as st0:
        P = Prog(nc, st0)

        uid = [0]

        def SB(st, name, shape, dt):
            uid[0] += 1
            return st.enter_context(nc.sbuf_tensor(f"{name}_{uid[0]}", list(shape), dt))

        bk = [st0.enter_context(nc.psum_tensor(f"bk{i}", [128, 512], F32)) for i in range(8)]
        BK = [f"bk{i}" for i in range(8)]

        def bkb(i):
            return bk[i][:].bitcast(BF16)

        def I(name, *a, **kw):
            return lambda e: getattr(e, name)(*a, **kw)

        def mm(out_ap, pairs, reads, writes):
            n = len(pairs)
            fns = []
            for i, (l, r) in enumerate(pairs):
                fns.append(I("matmul", out_ap, lhsT=l, rhs=r, start=(i == 0), stop=(i == n - 1)))
            P.op("pe", fns, reads, writes)

        def act(out_ap, in_ap, func, reads, writes, bias=None, scale=None, accum_out=None):
            kw = {}
            if bias is not None:
                kw["bias"] = bias
            if scale is not None:
                kw["scale"] = scale
            if accum_out is not None:
                kw["accum_out"] = accum_out
            P.op("act", I("activation", out=out_ap, in_=in_ap, func=func, **kw), reads, writes)

        ident_bf = SB(st0, "ident_bf", [128, 128], BF16)
        ident_f = SB(st0, "ident_f", [128, 128], F32)
        maskb = SB(st0, "maskb", [128, 128], BF16)
        mask01 = SB(st0, "mask01", [128, 128], F32)
        bdones = SB(st0, "bdones", [128, 128], BF16)
        sel = SB(st0, "sel", [4, 512], F32)
        tmpc = SB(st0, "tmpc", [128, 512], F32)
        mask4 = SB(st0, "mask4", [128, 512], F32)

        P.op("pool", I("memset", tmpc[:], 1.0), (), ["tmpc"])
        P.op("pool", I("affine_select", out=ident_f[:], in_=tmpc[:, 0:128], pattern=[[-1, 128]], compare_op=ALU.is_equal,
                                               fill=0.0, base=0, channel_multiplier=1), ["tmpc"], ["ident_f"])
        P.op("pool", I("tensor_copy", out=ident_bf[:], in_=ident_f[:]), ["ident_f"], ["ident_bf"])
        P.op("pool", I("affine_select", out=mask01[:], in_=tmpc[:, 0:128], pattern=[[1, 128]], compare_op=ALU.is_ge,
                                               fill=0.0, base=0, channel_multiplier=-1), ["tmpc"], ["mask01"])
        for i4 in range(4):
            P.op("pool", I("tensor_copy", out=mask4[:, i4 * 128:(i4 + 1) * 128], in_=mask01[:]), ["mask01"], ["mask4"])
        P.op("pool", I("tensor_scalar", out=maskb[:], in0=mask01[:], scalar1=-1.0, scalar2=30000.0,
                                               op0=ALU.add, op1=ALU.mult), ["mask01"], ["maskb"])
        P.op("pool", I("memset", bdones[:], 0.0), (), ["bdones"])
        P.op("pool", I("memset", bdones[0:64, 0:64], 1.0), (), ["bdones"])
        P.op("pool", I("memset", bdones[64:128, 64:128], 1.0), (), ["bdones"])
        P.op("pool", I("affine_select", out=sel[:].rearrange("p (h j) -> p h j", j=128),
                                               in_=tmpc[0:4, :].rearrange("p (h j) -> p h j", j=128),
                                               pattern=[[-1, 4], [0, 128]], compare_op=ALU.is_equal,
                                               fill=0.0, base=0, channel_multiplier=1), ["tmpc"], ["sel"])

        cT_sb = SB(st0, "cT_sb", [128, KC, 2], F32)
        sc = SB(st0, "sc", [128, KC, 2], BF16)
        badaT = SB(st0, "badaT", [128, 48], F32)
        n1w = SB(st0, "n1w", [128, KC], F32)
        n2w = SB(st0, "n2w", [128, KC], F32)
        mnw = SB(st0, "mnw", [128, KC], F32)
        bi = SB(st0, "bi", [4, 1], F32)
        nbf20 = SB(st0, "nbf20", [20, 1], F32)
        fqw_s = SB(st0, "fqw_s", [128, 1], F32)
        fkw_s = SB(st0, "fkw_s", [128, 1], F32)
        convw = SB(st0, "convw", [128, 44, 3], F32)
        convb = SB(st0, "convb", [128, 44], F32)
        adaT = SB(st0, "adaT", [128, 48, 2], F32)
        a1 = SB(st0, "a1", [128, KC, 2], F32)
        a2 = SB(st0, "a2", [128, KC, 2], F32)
        grow = SB(st0, "grow", [2, 2048], F32)
        hT = SB(st0, "hT", [128, KC, S], BF16)

        for (t_, src, key) in [(cT_sb, cT, "cT_sb"), (badaT, b_adaT, "badaT"), (n1w, n1wT, "n1w"), (n2w, n2wT, "n2w"),
                               (mnw, mnwT, "mnw"), (bi, b_i, "bi"), (nbf20, b_f20, "nbf20"), (fqw_s, fqw, "fqw_s"),
                               (fkw_s, fkw, "fkw_s"), (convw, convT, "convw"), (convb, convbT, "convb")]:
            P.dma("sp", t_[:], src, "ld_" + key, (), [key])
        P.op("dve", I("tensor_scalar", out=nbf20[:], in0=nbf20[:], scalar1=-1.0, scalar2=None, op0=ALU.mult), ["nbf20"], ["nbf20"])
        P.op("dve", I("tensor_scalar", out=fqw_s[:], in0=fqw_s[:], scalar1=0.125, scalar2=None, op0=ALU.mult), ["fqw_s"], ["fqw_s"])

        with contextlib.ExitStack() as st:
            bgate = SB(st, "bgate", [2, 2048], F32)
            wa = [SB(st, f"wa{i}", [128, KC, 1536], BF16) for i in range(2)]
            for r in range(2):
                P.dma("sp", bgate[r:r + 1, 0:1024], b_ada[2048:3072].rearrange("(o n) -> o n", o=1), "ld_bg", (), ["bgate"])
                P.dma("sp", bgate[r:r + 1, 1024:2048], b_ada[5120:6144].rearrange("(o n) -> o n", o=1), "ld_bg", (), ["bgate"])
            act(sc[:], cT_sb[:], AF.Silu, ["cT_sb"], ["sc"])
            for pc in range(4):
                sl = pc % 2
                P.dma("pool", wa[sl][:], w_ada_v[:, :, pc * 1536:(pc + 1) * 1536], f"ld_wa{sl}", (), [f"wa{sl}"])
                for ch in range(12):
                    gch = pc * 12 + ch
                    mm(bk[0][:, gch * 2:gch * 2 + 2],
                       [(wa[sl][:, k, ch * 128:(ch + 1) * 128], sc[:, k, :]) for k in range(KC)],
                       [f"wa{sl}", "sc"], [BK[0]])
                if pc in (1, 3):
                    for hf in range(2):
                        bi_ = 1 + (pc // 2) * 2 + hf
                        mm(bk[bi_][0:2, :],
                           [(sc[:, k, :], wa[sl][:, k, 512 + hf * 512:1024 + hf * 512]) for k in range(KC)],
                           [f"wa{sl}", "sc"], [BK[bi_]])
            psv = bk[0][:, 0:96].rearrange("p (c b) -> p c b", b=2)
            for b in range(2):
                P.op("dve", I("tensor_tensor", out=adaT[:, :, b], in0=psv[:, :, b], in1=badaT[:], op=ALU.add),
                     [BK[0], "badaT"], ["adaT"])
            for b in range(2):
                P.op("dve", I("scalar_tensor_tensor", out=a1[:, :, b], in0=adaT[:, 8:16, b], scalar=1.0, in1=n1w[:],
                                                                  op0=ALU.add, op1=ALU.mult), ["adaT", "n1w"], ["a1"])
                P.op("dve", I("scalar_tensor_tensor", out=a2[:, :, b], in0=adaT[:, 32:40, b], scalar=1.0, in1=n2w[:],
                                                                  op0=ALU.add, op1=ALU.mult), ["adaT", "n2w"], ["a2"])
            for i in range(4):
                P.op("dve", I("tensor_tensor", out=grow[:, i * 512:(i + 1) * 512], in0=bk[1 + i][0:2, :],
                                                           in1=bgate[:, i * 512:(i + 1) * 512], op=ALU.add),
                     [BK[1 + i], "bgate"], ["grow"])
            P.barrier()

        def build_gate_bc(gbc, key, b, g):
            for hf in range(2):
                mm(bk[6 + hf][:], [(sel[0:2, b * 128:(b + 1) * 128], grow[0:2, g * 1024 + hf * 512:g * 1024 + (hf + 1) * 512])],
                   ["sel", "grow"], [BK[6 + hf]])
                P.op("dve", I("tensor_copy", out=gbc[:, hf * 512:(hf + 1) * 512], in_=bk[6 + hf][:]),
                     [BK[6 + hf]], [key])

        def norm_to_hT(st, b, pre_fn, fin_fn, aX, shcol0, tag):
            xn = [SB(st, f"xn{tag}{i}", [128, D], BF16) for i in range(2)]
            junk = SB(st, f"junk{tag}", [128, D], BF16)
            stt = [SB(st, f"stt{tag}{i}", [128, 4], F32) for i in range(2)]
            held = {}

            def do_block(j):
                if j == 0:
                    pre_fn(0)
                if j + 1 < NB:
                    pre_fn(j + 1)
                if j < NB:
                    xs_ap, xs_keys = fin_fn(j)
                    s2 = j % 2
                    act(junk[:], xs_ap, AF.Square, xs_keys, [f"junk{tag}", f"stt{tag}{s2}"], accum_out=stt[s2][:, 0:1])
                    act(stt[s2][:, 1:2], stt[s2][:, 0:1], AF.Ln, [f"stt{tag}{s2}"], [f"stt{tag}{s2}"], bias=EPS, scale=1.0 / D)
                    act(stt[s2][:, 2:3], stt[s2][:, 1:2], AF.Exp, [f"stt{tag}{s2}"], [f"stt{tag}{s2}"], scale=-0.5)
                    P.op("dve", I("tensor_scalar", out=xn[s2][:], in0=xs_ap, scalar1=stt[s2][:, 2:3], scalar2=None, op0=ALU.mult),
                         xs_keys + [f"stt{tag}{s2}"], [f"xn{tag}{s2}"])
                    for hf in range(2):
                        tb = 4 + 2 * hf + s2
                        tp = bkb(tb)
                        P.op("pe", [I("transpose", out=tp[:, c * 128:(c + 1) * 128], in_=xn[s2][:, (4 * hf + c) * 128:(4 * hf + c + 1) * 128],
                                      identity=ident_bf[:]) for c in range(4)], [f"xn{tag}{s2}", "ident_bf"], [BK[tb]])
                if j >= 1:
                    jb = j - 1
                    s2 = jb % 2
                    tpa, tpb = bkb(4 + s2), bkb(6 + s2)
                    for c in range(4):
                        act(hT[:, c, jb * 128:(jb + 1) * 128], tpa[:, c * 128:(c + 1) * 128], AF.Identity,
                            [BK[4 + s2]], [f"hT{jb}"], bias=adaT[:, shcol0 + c, b:b + 1], scale=aX[:, c, b:b + 1])
                    for c in range(4, 8):
                        P.op("dve", I("tensor_scalar", out=hT[:, c, jb * 128:(jb + 1) * 128], in0=tpb[:, (c - 4) * 128:(c - 3) * 128],
                                      scalar1=aX[:, c, b:b + 1], scalar2=adaT[:, shcol0 + c, b:b + 1], op0=ALU.mult, op1=ALU.add),
                             [BK[6 + s2]], [f"hT{jb}"])
            return do_block

        hT_all = [f"hT{j}" for j in range(NB)]

        def hT_keys(t):
            return [f"hT{j}" for j in range(4 * t, 4 * t + 4)]

        for b in range(nseq):
            with contextlib.ExitStack() as st:
                xs = [SB(st, f"xs{i}", [128, D], F32) for i in range(3)]

                def ldx1(j):
                    P.dma("sp", xs[j % 3][:], x[b, j * 128:(j + 1) * 128, :], f"ld_xs{j % 3}", (), [f"xs{j % 3}"])

                def fin1(j):
                    return xs[j % 3][:], [f"xs{j % 3}"]
                blk1 = norm_to_hT(st, b, ldx1, fin1, a1, 0, "a")
                for j in range(NB + 1):
                    blk1(j)
                P.barrier()
            if stop_after <= 1:
                continue

            def gates(want_m, want_f, Grow=None, A_tm=None, X_tm=None, NP_=None):
                with contextlib.ExitStack() as st:
                    wi = SB(st, "wi", [128, KC, 4], BF16)
                    wf = SB(st, "wf", [128, KC, 20], BF16)
                    ipre = SB(st, "ipre", [4, S], F32)
                    spl = SB(st, "spl", [20, S], F32)
                    negF = SB(st, "negF", [20, S], F32)
                    P.dma("pool", wi[:], w_in_v[:, :, O_MI:O_MI + 4], "ld_wi", (), ["wi"])
                    P.dma("pool", wf[:, :, 0:4], w_in_v[:, :, O_MF:O_MF + 4], "ld_wf", (), ["wf"])
                    P.dma("pool", wf[:, :, 4:20], w_in_v[:, :, O_FF:O_FF + 16], "ld_wf", (), ["wf"])
                    for t in range(4):
                        ts_ = slice(t * 512, (t + 1) * 512)
                        mm(bk[t % 2][0:4, :], [(wi[:, k, :], hT[:, k, ts_]) for k in range(KC)], ["wi"] + hT_keys(t), [BK[t % 2]])
                        mm(bk[2 + t % 2][0:20, :], [(wf[:, k, :], hT[:, k, ts_]) for k in range(KC)], ["wf"] + hT_keys(t), [BK[2 + t % 2]])
                        act(ipre[:, ts_], bk[t % 2][0:4, :], AF.Identity, [BK[t % 2], "bi"], ["ipre"], bias=bi[:, 0:1])
                        act(spl[:, ts_], bk[2 + t % 2][0:20, :], AF.Exp, [BK[2 + t % 2], "nbf20"], ["spl"], bias=nbf20[:, 0:1], scale=-1.0)
                    act(spl[:], spl[:], AF.Ln, ["spl"], ["spl"], bias=1.0)
                    P.op("dve", I("tensor_tensor_scan", out=negF[:], data0=spl[:], data1=spl[:], initial=0.0, op0=ALU.add, op1=ALU.max),
                         ["spl"], ["negF"])
                    if want_m:
                        Am = SB(st, "Am", [4, S], F32)
                        Xm = SB(st, "Xm", [4, S], F32)
                        P.op("dve", I("tensor_tensor", out=Am[:], in0=ipre[:], in1=negF[0:4, :], op=ALU.add), ["ipre", "negF"], ["Am"])
                        P.op("dve", I("memset", Grow[:, 0:1], 0.0), (), ["Grow"])
                        P.op("dve", I("tensor_tensor_scan", out=Grow[:, 1:S + 1], data0=Am[:], data1=Am[:], initial=0.0, op0=ALU.max, op1=ALU.max),
                             ["Am"], ["Grow"])
                        P.op("dve", I("tensor_tensor", out=Xm[:], in0=negF[0:4, :], in1=Grow[:, 1:S + 1], op=ALU.subtract), ["negF", "Grow"], ["Xm"])
                        P.op("pe", [I("transpose", out=bk[4][:, j * 4:(j + 1) * 4], in_=Am[0:4, j * 128:(j + 1) * 128],
                                                               identity=ident_f[0:4, 0:4]) for j in range(NB)], ["Am", "ident_f"], [BK[4]])
                        P.op("pe", [I("transpose", out=bk[5][:, j * 4:(j + 1) * 4], in_=Xm[0:4, j * 128:(j + 1) * 128],
                                                               identity=ident_f[0:4, 0:4]) for j in range(NB)], ["Xm", "ident_f"], [BK[5]])
                        P.op("dve", I("tensor_copy", out=A_tm[:], in_=bk[4][:, 0:64]), [BK[4]], ["A_tm"])
                        P.op("dve", I("tensor_copy", out=X_tm[:], in_=bk[5][:, 0:64]), [BK[5]], ["X_tm"])
                    if want_f:
                        r1 = SB(st, "r1", [20, S], F32)
                        NP_ = SB(st, "NP", [20, 3, S], BF16)
                        P.op("dve", I("tensor_copy", out=NP_[:, 0, :], in_=negF[:]), ["negF"], ["NP"])
                        P.op("dve", I("tensor_tensor", out=r1[:], in0=negF[:], in1=NP_[:, 0, :], op=ALU.subtract), ["negF", "NP"], ["r1"])
                        P.op("dve", I("tensor_copy", out=NP_[:, 1, :], in_=r1[:]), ["r1"], ["NP"])
                        P.op("dve", I("tensor_tensor", out=r1[:], in0=r1[:], in1=NP_[:, 1, :], op=ALU.subtract), ["r1", "NP"], ["r1"])
                        P.op("dve", I("tensor_copy", out=NP_[:, 2, :], in_=r1[:]), ["r1"], ["NP"])
                        P.dma("sp", npscr.rearrange("p (r t) -> p r t", r=3), NP_[:], "st_np", ["NP"], ["npscr"])
                    P.barrier()

            with contextlib.ExitStack() as st2:
                hmT = SB(st2, "hmT", [128, KC, S], BF16)
                st2b = contextlib.ExitStack()
                Grow = SB(st2b, "Grow", [4, S + 1], F32)
                A_tm = SB(st2b, "A_tm", [128, 64], F32)
                X_tm = SB(st2b, "X_tm", [128, 64], F32)
                gates(True, True, Grow=Grow, A_tm=A_tm, X_tm=X_tm)

                with contextlib.ExitStack() as st:
                    qT = SB(st, "qT", [128, 2, S], BF16)
                    kT = SB(st, "kT", [128, 2, S], BF16)
                    ktm = SB(st, "ktm", [128, NB, 256], BF16)
                    vA = SB(st, "vA", [128, NB, 258], BF16)
                    vwa = SB(st, "vwa", [128, NB, 258], BF16)
                    STb = SB(st, "STb", [128, S], BF16)
                    oS = SB(st, "oS", [128, NB, 256], BF16)
                    Cf = SB(st, "Cf", [128, 2, 257], F32)
                    Cb = [SB(st, f"Cb{i}", [128, 2, 258], BF16) for i in range(2)]
                    sta = SB(st, "sta", [128, 12, 16], F32)
                    hmtm = [SB(st, f"hmtm{i}", [128, 256], BF16) for i in range(2)]
                    junk4 = SB(st, "junk4", [128, 256], BF16)
                    wq_ = [SB(st, f"wq{i}", [128, KC, 256], BF16) for i in range(2)]
                    wk_ = [SB(st, f"wk{i}", [128, KC, 256], BF16) for i in range(2)]
                    wv_ = [SB(st, f"wv{i}", [128, KC, 256], BF16) for i in range(2)]
                    Gbc = SB(st, "Gbc", [128, S + 1], F32)
                    DTa = SB(st, "DTa", [128, S], F32)
                    Eba = SB(st, "Eba", [128, S], F32)
                    tmpS = [SB(st, f"tmpS{i}", [128, 512], F32) for i in range(2)]

                    def ld_m(h):
                        i = h % 2
                        P.dma("pool", wq_[i][:], w_in_v[:, :, O_MQ + 256 * h:O_MQ + 256 * h + 256], f"ld_wq{i}", (), [f"wq{i}"])
                        P.dma("pool", wk_[i][:], w_in_v[:, :, O_MK + 256 * h:O_MK + 256 * h + 256], f"ld_wk{i}", (), [f"wk{i}"])
                        P.dma("pool", wv_[i][:], w_in_v[:, :, O_MV + 256 * h:O_MV + 256 * h + 256], f"ld_wv{i}", (), [f"wv{i}"])
                    ld_m(0)
                    P.op("dve", I("memset", vA[:, :, 256:257], 1.0), (), [f"vA{j}" for j in range(NB)])
                    WALL, DCY, EXA, SS, ND, MDEN, RR, VAR, FAC, TMP, DEN, FACP = range(12)

                    def post_block(h, j):
                        s2 = j % 2
                        js = slice(j * 128, (j + 1) * 128)
                        tb = 2 + s2
                        tp = bkb(tb)
                        act(hmtm[s2][:], oS[:, j, :], AF.Identity, [f"oS{j}", "staFP"], [f"hmtm{s2}"], scale=sta[:, FACP, j:j + 1])
                        P.op("pe", [I("transpose", out=tp[:, c * 128:(c + 1) * 128], in_=hmtm[s2][:, c * 128:(c + 1) * 128],
                                      identity=ident_bf[:]) for c in range(2)], [f"hmtm{s2}", "ident_bf"], [BK[tb]])
                        P.op("dve", I("tensor_copy", out=hmT[:, 2 * h:2 * h + 2, js], in_=tp[:, 0:256].rearrange("p (c t) -> p c t", t=128)),
                             [BK[tb]], ["hmT"])

                    pending_post = []
                    for h in range(4):
                        wq, wk, wv = wq_[h % 2], wk_[h % 2], wv_[h % 2]
                        wqk, wkk, wvk = f"wq{h % 2}", f"wk{h % 2}", f"wv{h % 2}"
                        if h + 1 < 4:
                            ld_m(h + 1)
                        for t5 in range(5):
                            c0, c1 = t5 * 512, min(S + 1, (t5 + 1) * 512)
                            gb_ = 6 + t5 % 2
                            mm(bk[gb_][:, 0:c1 - c0], [(sel[0:4, h * 128:(h + 1) * 128], Grow[0:4, c0:c1])], ["sel", "Grow"], [BK[gb_]])
                            P.op("dve", I("tensor_copy", out=Gbc[:, c0:c1], in_=bk[gb_][:, 0:c1 - c0]), [BK[gb_]], ["Gbc"])
                        g0v = Gbc[:, 0:S].rearrange("p (j t) -> p j t", t=128)[:, :, 0]
                        gEv = Gbc[:, 1:S + 1].rearrange("p (j t) -> p j t", t=128)[:, :, 127]
                        Av = A_tm[:].rearrange("p (j h) -> p j h", h=4)[:, :, h]
                        Xv = X_tm[:].rearrange("p (j h) -> p j h", h=4)[:, :, h]
                        P.op("dve", I("tensor_tensor", out=sta[:, TMP, :], in0=Av, in1=gEv, op=ALU.subtract), ["A_tm", "Gbc"], ["staT"])
                        act(sta[:, WALL, :], sta[:, TMP, :], AF.Exp, ["staT"], ["staW"])
                        P.op("dve", I("tensor_tensor", out=sta[:, ND, :], in0=g0v, in1=gEv, op=ALU.subtract), ["Gbc"], ["staN"])
                        act(sta[:, DCY, :], sta[:, ND, :], AF.Exp, ["staN"], ["staD"])
                        act(sta[:, EXA, :], Xv, AF.Exp, ["X_tm"], ["staE"])
                        exq = []
                        for j in range(NB):
                            js = slice(j * 128, (j + 1) * 128)
                            jg = slice(1 + j * 128, 1 + (j + 1) * 128)
                            exq.append((DTa[:, js], Gbc[:, jg], ["Gbc", "A_tm"], [f"DTa{j // 4}"], A_tm[:, j * 4 + h:j * 4 + h + 1]))
                            exq.append((Eba[:, js], Gbc[:, jg], ["Gbc"], ["Eba"], Gbc[:, j * 128:j * 128 + 1]))
                        pj = 0
                        for (wt, wkey, dst, dkey, scl) in ((wq, wqk, qT, "qT", 1.0 / 16), (wk, wkk, kT, "kT", 1.0)):
                            for c in range(2):
                                for t in range(4):
                                    ts_ = slice(t * 512, (t + 1) * 512)
                                    pb = 4 + pj % 2
                                    mm(bk[pb][:], [(wt[:, k, c * 128:(c + 1) * 128], hT[:, k, ts_]) for k in range(KC)], [wkey] + hT_keys(t), [BK[pb]])
                                    act(dst[:, c, ts_], bk[pb][:], AF.Identity, [BK[pb]], [dkey], scale=scl)
                                    for _ in range(2):
                                        o_, i_, r_, w_, b_ = exq.pop(0)
                                        act(o_, i_, AF.Exp, r_, w_, bias=b_, scale=-1.0)
                                    if pending_post:
                                        post_block(*pending_post.pop(0))
                                    pj += 1
                        for j2 in range(NB // 2):
                            pb = 4 + pj % 2
                            pj += 1
                            for jj in range(2):
                                j = j2 * 2 + jj
                                mm(bk[pb][:, jj * 256:(jj + 1) * 256], [(hT[:, k, j * 128:(j + 1) * 128], wv[:, k, :]) for k in range(KC)],
                                   [wvk, f"hT{j}"], [BK[pb]])
                            act(vA[:, j2 * 2:j2 * 2 + 2, 0:256], bk[pb][:].rearrange("p (j c) -> p j c", c=256), AF.Identity, [BK[pb]],
                                [f"vA{j2 * 2}", f"vA{j2 * 2 + 1}"])
                            for jj in range(2):
                                j = j2 * 2 + jj
                                P.op("dve", I("tensor_scalar", out=vwa[:, j, 0:257], in0=vA[:, j, 0:257], scalar1=sta[:, WALL, j:j + 1], scalar2=None, op0=ALU.mult),
                                     [f"vA{j}", "staW"], [f"vwa{j}"])
                        for g4 in range(4):
                            tb = 6 + g4 % 2
                            tp = bkb(tb)
                            fns = []
                            for jj in range(4):
                                j = g4 * 4 + jj
                                for c in range(2):
                                    fns.append(I("transpose", out=tp[:, jj * 256 + c * 128:jj * 256 + (c + 1) * 128],
                                                 in_=kT[:, c, j * 128:(j + 1) * 128], identity=ident_bf[:]))
                            P.op("pe", fns, ["kT", "ident_bf"], [BK[tb]])
                            P.op("dve", I("tensor_copy", out=ktm[:, g4 * 4:g4 * 4 + 4, :], in_=tp.rearrange("p (j c) -> p j c", c=256)),
                                 [BK[tb]], ["ktm"])
                        while pending_post:
                            post_block(*pending_post.pop(0))
                        for g4 in range(4):
                            gs = slice(g4 * 512, (g4 + 1) * 512)
                            fns = []
                            for jj in range(4):
                                j = g4 * 4 + jj
                                js = slice(j * 128, (j + 1) * 128)
                                for c in range(2):
                                    fns.append(I("matmul", bk[g4][:, jj * 128:(jj + 1) * 128], lhsT=kT[:, c, js], rhs=qT[:, c, js],
                                                 start=(c == 0), stop=(c == 1)))
                            P.op("pe", fns, ["kT", "qT"], [BK[g4]])
                            P.op("dve", I("tensor_tensor", out=tmpS[g4 % 2][:], in0=bk[g4][:], in1=DTa[:, gs], op=ALU.mult),
                                 [BK[g4], f"DTa{g4}"], [f"tmpS{g4 % 2}"])
                            P.op("dve", I("tensor_tensor", out=STb[:, gs], in0=tmpS[g4 % 2][:], in1=mask4[:], op=ALU.mult),
                                 [f"tmpS{g4 % 2}", "mask4"], [f"STb{g4}"])
                        for c in range(2):
                            P.op("dve", I("tensor_tensor", out=qT[:, c, :], in0=qT[:, c, :], in1=Eba[:], op=ALU.mult), ["qT", "Eba"], ["qT"])
                        P.op("dve", I("memset", Cf[:], 0.0), (), ["Cf0", "Cf1"])
                        P.op("dve", I("memset", Cb[0][:], 0.0), (), ["Cb0a", "Cb0b"])

                        def upd(j):
                            for c in range(2):
                                ub = 4 + (j % 2) * 2 + c
                                mm(bk[ub][:, 0:257], [(ktm[:, j, c * 128:(c + 1) * 128], vwa[:, j, 0:257])], ["ktm", f"vwa{j}"], [BK[ub]])
                        upd(0)
                        for j in range(NB):
                            s2 = j % 2
                            js = slice(j * 128, (j + 1) * 128)
                            if j + 1 < NB:
                                upd(j + 1)
                            ob = s2
                            mm(bk[ob][:, 0:257], [(STb[:, js], vA[:, j, 0:257])] + [(qT[:, c, js], Cb[s2][:, c, 0:257]) for c in range(2)],
                               [f"STb{j // 4}", f"vA{j}", "qT", f"Cb{s2}a", f"Cb{s2}b"], [BK[ob]])
                            for c in range(2):
                                ub = 4 + s2 * 2 + c
                                P.op("dve", I("scalar_tensor_tensor", out=Cf[:, c, :], in0=Cf[:, c, :], scalar=sta[:, DCY, j:j + 1], in1=bk[ub][:, 0:257],
                                              op0=ALU.mult, op1=ALU.add), [f"Cf{c}", "staD", BK[ub]], [f"Cf{c}"])
                            if j + 1 < NB:
                                act(Cb[1 - s2][:, 0, 0:257], Cf[:, 0, :], AF.Identity, ["Cf0"], [f"Cb{1 - s2}a"])
                                P.op("dve", I("tensor_copy", out=Cb[1 - s2][:, 1, 0:257], in_=Cf[:, 1, :]), ["Cf1"], [f"Cb{1 - s2}b"])
                            act(junk4[:], bk[ob][:, 0:256], AF.Square, [BK[ob]], ["junk4", "staS"], accum_out=sta[:, SS, j:j + 1])
                            act(oS[:, j, :], bk[ob][:, 0:256], AF.Identity, [BK[ob]], [f"oS{j}"])
                            act(sta[:, DEN, j:j + 1], bk[ob][:, 256:257], AF.Identity, [BK[ob]], ["staDen"])
                        denv = sta[:, DEN, :]
                        P.op("dve", I("tensor_scalar", out=sta[:, ND, :], in0=denv, scalar1=-1.0, scalar2=None, op0=ALU.mult), ["staDen"], ["staN"])
                        P.op("dve", I("tensor_tensor", out=sta[:, MDEN, :], in0=sta[:, ND, :], in1=sta[:, EXA, :], op=ALU.max), ["staN", "staE"], ["staM"])
                        P.op("dve", I("tensor_tensor", out=sta[:, MDEN, :], in0=sta[:, MDEN, :], in1=denv, op=ALU.max), ["staM", "staDen"], ["staM"])
                        P.op("dve", I("reciprocal", out=sta[:, RR, :], in_=sta[:, MDEN, :]), ["staM"], ["staR"])
                        P.op("dve", I("tensor_tensor", out=sta[:, VAR, :], in0=sta[:, SS, :], in1=sta[:, RR, :], op=ALU.mult), ["staS", "staR"], ["staV"])
                        P.op("dve", I("tensor_tensor", out=sta[:, VAR, :], in0=sta[:, VAR, :], in1=sta[:, RR, :], op=ALU.mult), ["staV", "staR"], ["staV"])
                        act(sta[:, VAR, :], sta[:, VAR, :], AF.Ln, ["staV"], ["staV"], bias=EPS, scale=1.0 / 256)
                        act(sta[:, VAR, :], sta[:, VAR, :], AF.Exp, ["staV"], ["staV"], scale=-0.5)
                        P.op("dve", I("tensor_tensor", out=sta[:, FACP, :], in0=sta[:, VAR, :], in1=sta[:, RR, :], op=ALU.mult), ["staV", "staR"], ["staFP"])
                        for j in range(NB):
                            post_block(h, j)
                    while pending_post:
                        post_block(*pending_post.pop(0))
                    P.barrier()
                st2b.close()

                hfT = SB(st2, "hfT", [128, KC, S], BF16)
                st3b = contextlib.ExitStack()

                with contextlib.ExitStack() as st:
                    qaug = [[SB(st, f"qaug{s}{i}", [128, S], BF16) for i in range(2)] for s in range(2)]
                    kaug = [[SB(st, f"kaug{s}{i}", [128, S], BF16) for i in range(2)] for s in range(2)]
                    vaug = [SB(st, f"vaug{s}", [128, NB, 192], BF16) for s in range(2)]
                    wq2_ = [SB(st, f"wq2{i}", [128, KC, 128], BF16) for i in range(2)]
                    wk2_ = [SB(st, f"wk2{i}", [128, KC, 128], BF16) for i in range(2)]
                    wv2_ = [SB(st, f"wv2{i}", [128, KC, 128], BF16) for i in range(2)]

                    def ld_f(p):
                        i = p % 2
                        P.dma("pool", wq2_[i][:], w_in_v[:, :, O_FQ + 128 * p:O_FQ + 128 * p + 128], f"ld_wq2{i}", (), [f"wq2{i}"])
                        P.dma("pool", wk2_[i][:], w_in_v[:, :, O_FK + 128 * p:O_FK + 128 * p + 128], f"ld_wk2{i}", (), [f"wk2{i}"])
                        P.dma("pool", wv2_[i][:], w_in_v[:, :, O_FV + 128 * p:O_FV + 128 * p + 128], f"ld_wv2{i}", (), [f"wv2{i}"])
                    ld_f(0)
                    sq = [SB(st, f"sq{i}", [128, 512], BF16) for i in range(2)]
                    rs = [SB(st, f"rs{i}", [128, 512], F32) for i in range(2)]
                    pT = [SB(st, f"pT{i}", [128, 512], BF16) for i in range(4)]
                    rden = [SB(st, f"rden{i}", [128, 512], F32) for i in range(2)]
                    for s_ in range(2):
                        for i in range(2):
                            P.op("dve", I("memset", qaug[s_][i][64:128, :], 0.0), (), [f"qaug{s_}{i}"])
                            P.op("dve", I("memset", kaug[s_][i][64:128, :], 0.0), (), [f"kaug{s_}{i}"])
                            P.op("dve", I("memset", qaug[s_][i][64:70, :], 1.0), (), [f"qaug{s_}{i}"])
                            P.op("dve", I("memset", kaug[s_][i][64:70, :], -1.0), (), [f"kaug{s_}{i}"])
                        P.op("dve", I("memset", vaug[s_][:, :, 64:128], 1.0), (), [f"vaug{s_}"])
                    cnt3 = {"sq": 0, "pj": 0}

                    def fox_proj(p):
                        for _ in fox_proj_gen(p):
                            pass

                    def fox_proj_gen(p):
                        s_ = p % 2
                        wq2, wk2, wv2 = wq2_[p % 2], wk2_[p % 2], wv2_[p % 2]
                        if p + 1 < 8:
                            ld_f(p + 1)
                        sched = {}

                        def at(tick, fn):
                            sched.setdefault(tick, []).append(fn)
                        u = 0
                        for (wt, wkey, aug, akey, wcol) in ((wq2, f"wq2{p % 2}", qaug, "qaug", fqw_s), (wk2, f"wk2{p % 2}", kaug, "kaug", fkw_s)):
                            for t in range(4):
                                ts_ = slice(t * 512, (t + 1) * 512)
                                pb = 5 + u % 2
                                i2 = u % 2
                                t0 = 8 * u

                                def s0c(ch, wt=wt, wkey=wkey, ts_=ts_, pb=pb, t=t):
                                    P.op("pe", [I("matmul", bk[pb][:], lhsT=wt[:, k, 0:128], rhs=hT[:, k, ts_], start=(k == 0), stop=(k == KC - 1))
                                                for k in (2 * ch, 2 * ch + 1)], [wkey] + hT_keys(t), [BK[pb]])

                                def s1(pb=pb, i2=i2):
                                    act(sq[i2][:], bk[pb][:], AF.Square, [BK[pb]], [f"sq{i2}"])

                                def s2(i2=i2):
                                    mm(bk[7][:], [(bdones[:], sq[i2][:])], ["bdones", f"sq{i2}"], [BK[7]])

                                def s3(i2=i2):
                                    act(rs[i2][:], bk[7][:], AF.Ln, [BK[7]], [f"rs{i2}"], bias=EPS, scale=1.0 / 64)
                                    act(rs[i2][:], rs[i2][:], AF.Exp, [f"rs{i2}"], [f"rs{i2}"], scale=-0.5)

                                def s4(aug=aug, akey=akey, wcol=wcol, ts_=ts_, pb=pb, i2=i2):
                                    for hh in range(2):
                                        pr = slice(hh * 64, hh * 64 + 64)
                                        P.op("dve", I("scalar_tensor_tensor", out=aug[s_][hh][0:64, ts_], in0=bk[pb][pr, :], scalar=wcol[pr, 0:1],
                                                      in1=rs[i2][pr, :], op0=ALU.mult, op1=ALU.mult),
                                             [BK[pb], f"rs{i2}", "fqw_s", "fkw_s"], [f"{akey}{s_}{hh}"])
                                for ch in range(4):
                                    at(t0 + ch, (lambda ch=ch, f=s0c: f(ch)))
                                for si, fn in enumerate((s1, s2, s3, s4)):
                                    at(t0 + 6 + 3 * si, fn)
                                u += 1
                        for g4, t0 in enumerate((64, 72, 76, 80)):
                            pb = 5 + g4 % 2

                            def v0(jj, g4=g4, pb=pb):
                                j = g4 * 4 + jj
                                mm(bk[pb][:, jj * 128:(jj + 1) * 128], [(hT[:, k, j * 128:(j + 1) * 128], wv2[:, k, 0:128]) for k in range(KC)],
                                   [f"wv2{p % 2}", f"hT{j}"], [BK[pb]])

                            def v1(g4=g4, pb=pb):
                                pv = bk[pb][:].rearrange("p (j c) -> p j c", c=128)
                                act(vaug[s_][:, g4 * 4:g4 * 4 + 4, 0:64], pv[:, :, 0:64], AF.Identity, [BK[pb]], [f"vaug{s_}"])
                                act(vaug[s_][:, g4 * 4:g4 * 4 + 4, 128:192], pv[:, :, 64:128], AF.Identity, [BK[pb]], [f"vaug{s_}"])
                            for jj in range(4):
                                at(t0 + jj, (lambda jj=jj, f=v0: f(jj)))
                            at(t0 + 6, v1)

                        def dmas():
                            for hh in range(2):
                                h = 2 * p + hh
                                for r3 in range(3):
                                    P.dma("sp", kaug[s_][hh][64 + r3:65 + r3, :], npscr[4 + h:5 + h, r3 * S:(r3 + 1) * S], f"ld_ka{s_}{hh}", ["npscr"], [f"kaug{s_}{hh}"])
                                    P.dma("sp", qaug[s_][hh][67 + r3:68 + r3, :], npscr[4 + h:5 + h, r3 * S:(r3 + 1) * S], f"ld_qa{s_}{hh}", ["npscr"], [f"qaug{s_}{hh}"])
                        at(1, dmas)
                        for tick in range(max(sched) + 1):
                            for fn in sched.get(tick, ()):
                                fn()
                            yield

                    cnt_a = {"s": 0, "p": 0, "o": 0}

                    def fox_att(p, hh, gen=None):
                        s_ = p % 2
                        qa, ka, va = qaug[s_][hh], kaug[s_][hh], vaug[s_]
                        qk, kk, vk = f"qaug{s_}{hh}", f"kaug{s_}{hh}", f"vaug{s_}"
                        vlo = 0 if hh == 0 else 64
                        ol = 0 if hh == 0 else 64
                        dl = 64 if hh == 0 else 0
                        for g in range(4):
                            ob = 3 + cnt_a["o"] % 2
                            cnt_a["o"] += 1
                            nkb = 4 * g + 4
                            pend = []

                            def score(i):
                                r = i - 4 * g
                                qo = 128 * r if r >= 0 else 0
                                sb_ = cnt_a["s"] % 3
                                cnt_a["s"] += 1
                                pairs = [(ka[:, i * 128:(i + 1) * 128], qa[:, 512 * g + qo:512 * g + 512])]
                                fns = [I("matmul", bk[sb_][:, qo:512], lhsT=pairs[0][0], rhs=pairs[0][1], start=True, stop=(r < 0))]
                                if r >= 0:
                                    fns.append(I("matmul", bk[sb_][:, qo:qo + 128], lhsT=ident_bf[:], rhs=maskb[:], start=False, stop=True))
                                P.op("pe", fns, [kk, qk, "ident_bf", "maskb"], [BK[sb_]])
                                return (i, qo, sb_)

                            def expo(i, qo, sb_):
                                pi = cnt_a["p"] % 4
                                cnt_a["p"] += 1
                                act(pT[pi][:, qo:512], bk[sb_][:, qo:512], AF.Exp, [BK[sb_]], [f"pT{pi}"])
                                return pi

                            def pv(i, qo, pi):
                                P.op("pe", I("matmul", bk[ob][:, qo:512], lhsT=va[:, i, vlo:vlo + 128], rhs=pT[pi][:, qo:512],
                                                                             start=(i == 0), stop=(i == nkb - 1)), [vk, f"pT{pi}"], [BK[ob]])
                            scq = [score(i) for i in range(min(2, nkb))]
                            for i in range(nkb):
                                if i + 2 < nkb:
                                    scq.append(score(i + 2))
                                sc0 = scq.pop(0)
                                pi = expo(*sc0)
                                pv(sc0[0], sc0[1], pi)
                                if gen is not None:
                                    next(gen, None)
                            rd = rden[g % 2]
                            P.op("dve", I("reciprocal", out=rd[dl:dl + 64, :], in_=bk[ob][dl:dl + 64, :]), [BK[ob]], [f"rden{g % 2}"])
                            P.op("dve", I("tensor_tensor", out=hfT[ol:ol + 64, p, g * 512:(g + 1) * 512], in0=bk[ob][ol:ol + 64, :],
                                                                         in1=rd[dl:dl + 64, :], op=ALU.mult), [BK[ob], f"rden{g % 2}"], ["hfT"])

                    fox_proj(0)
                    for p in range(8):
                        if fox_limit <= 2 * p:
                            break
                        gen = fox_proj_gen(p + 1) if p + 1 < 8 else None
                        fox_att(p, 0, gen)
                        if fox_limit <= 2 * p + 1:
                            break
                        fox_att(p, 1, gen)
                        if gen is not None:
                            for _ in gen:
                                pass
                    P.barrier()
                st3b.close()

                with contextlib.ExitStack() as st:
                    wo = [SB(st, f"wo{i}", [128, KC, 512], BF16) for i in range(2)]
                    sg = [SB(st, f"sg{i}", [128, 512], F32) for i in range(2)]
                    n4 = 0
                    for half in range(2):
                        P.dma("pool", wo[half][:], w_in_v[:, :, O_MO + 512 * half:O_MO + 512 * half + 512], f"ld_wo{half}", (), [f"wo{half}"])
                    for c in range(KC):
                        half, cl = c // 4, (c % 4) * 128
                        for t in range(4):
                            ts_ = slice(t * 512, (t + 1) * 512)
                            pb = n4 % 4
                            i2 = n4 % 2
                            n4 += 1
                            mm(bk[pb][:], [(wo[half][:, k, cl:cl + 128], hT[:, k, ts_]) for k in range(KC)], [f"wo{half}"] + hT_keys(t), [BK[pb]])
                            act(sg[i2][:], bk[pb][:], AF.Sigmoid, [BK[pb]], [f"sg{i2}"])
                            P.op("dve", I("scalar_tensor_tensor", out=hmT[:, c, ts_], in0=hmT[:, c, ts_], scalar=mnw[:, c:c + 1],
                                                                                             in1=sg[i2][:], op0=ALU.mult, op1=ALU.mult),
                                 [f"hmTc{c}", "mnw", f"sg{i2}"], [f"hmTc{c}"])
                    P.barrier()

                st5 = contextlib.ExitStack()
                mgT = SB(st5, "mgT", [128, KC, S], BF16)
                wot = SB(st5, "wot", [128, KC, D], BF16)
                with contextlib.ExitStack() as st:
                    wsl = {nm: [SB(st, f"w5{nm}{i}", [128, KC, 256], BF16) for i in range(2)] for nm in ("bm", "bf", "ga", "gb")}
                    sga = [SB(st, "sga0", [128, 512], F32)] * 2
                    sgb = [SB(st, "sgb0", [128, 512], F32)] * 2
                    t1 = [SB(st, f"t1{i}", [128, 512], F32) for i in range(2)]
                    t2 = [SB(st, f"t2{i}", [128, 512], F32) for i in range(2)]
                    srcs = {"bm": (w_bm_v, 0), "bf": (w_bf_v, 0), "ga": (w_in_v, O_GA), "gb": (w_in_v, O_GB)}
                    n5 = 0
                    def ld_5(c2):
                        sl = c2 % 2
                        for nm in ("bm", "bf", "ga", "gb"):
                            src, off = srcs[nm]
                            P.dma("pool", wsl[nm][sl][:], src[:, :, off + 256 * c2:off + 256 * c2 + 256], f"ld_w5{nm}{sl}", (), [f"w5{nm}{sl}"])
                    ld_5(0)
                    for c2 in range(4):
                        sl = c2 % 2
                        if c2 + 1 < 4:
                            ld_5(c2 + 1)
                        else:
                            for half in range(2):
                                P.dma("pool", wot[:, :, half * 512:(half + 1) * 512], w_out_v[:, :, half * 512:(half + 1) * 512], "ld_wot", (), ["wot"])
                        for cc in range(2):
                            c = c2 * 2 + cc
                            cl = cc * 128
                            for t in range(4):
                                ts_ = slice(t * 512, (t + 1) * 512)
                                i2 = n5 % 2
                                pb0 = (n5 % 2) * 4
                                n5 += 1
                                mm(bk[pb0][:], [(wsl["bm"][sl][:, k, cl:cl + 128], hmT[:, k, ts_]) for k in range(KC)], [f"w5bm{sl}", "hmT"], [BK[pb0]])
                                mm(bk[pb0 + 1][:], [(wsl["bf"][sl][:, k, cl:cl + 128], hfT[:, k, ts_]) for k in range(KC)], [f"w5bf{sl}", "hfT"], [BK[pb0 + 1]])
                                mm(bk[pb0 + 2][:], [(wsl["ga"][sl][:, k, cl:cl + 128], hT[:, k, ts_]) for k in range(KC)], [f"w5ga{sl}"] + hT_keys(t), [BK[pb0 + 2]])
                                mm(bk[pb0 + 3][:], [(wsl["gb"][sl][:, k, cl:cl + 128], hT[:, k, ts_]) for k in range(KC)], [f"w5gb{sl}"] + hT_keys(t), [BK[pb0 + 3]])
                                act(sga[i2][:], bk[pb0 + 2][:], AF.Sigmoid, [BK[pb0 + 2]], ["sga"])
                                act(sgb[i2][:], bk[pb0 + 3][:], AF.Sigmoid, [BK[pb0 + 3]], ["sgb"])
                                P.op("dve", I("tensor_tensor", out=t1[i2][:], in0=bk[pb0][:], in1=sga[i2][:], op=ALU.mult),
                                     [BK[pb0], "sga"], [f"t1{i2}"])
                                P.op("dve", I("tensor_tensor", out=t2[i2][:], in0=bk[pb0 + 1][:], in1=sgb[i2][:], op=ALU.mult),
                                     [BK[pb0 + 1], "sgb"], [f"t2{i2}"])
                                P.op("dve", I("tensor_tensor", out=mgT[:, c, ts_], in0=t1[i2][:], in1=t2[i2][:], op=ALU.add),
                                     [f"t1{i2}", f"t2{i2}"], ["mgT"])
                    P.barrier()

                with contextlib.ExitStack() as st:
                    g1bc = SB(st, "g1bc", [128, D], F32)
                    xs = [SB(st, f"xr{i}", [128, D], F32) for i in range(2)]
                    x1 = [SB(st, f"x1{i}", [128, D], F32) for i in range(3)]
                    build_gate_bc(g1bc, "g1bc", b, 0)

                    def pre2(j):
                        sl = j % 2
                        P.dma("sp", xs[sl][:], x[b, j * 128:(j + 1) * 128, :], f"ld_xr{sl}", (), [f"xr{sl}"])
                        for hf in range(2):
                            pb = 2 * sl + hf
                            cs = slice(hf * 512, (hf + 1) * 512)
                            mm(bk[pb][:], [(mgT[:, k, j * 128:(j + 1) * 128], wot[:, k, cs]) for k in range(KC)], ["mgT", "wot"], [BK[pb]])

                    def fin2(j):
                        sl = j % 2
                        for hf in range(2):
                            pb = 2 * sl + hf
                            cs = slice(hf * 512, (hf + 1) * 512)
                            P.op("dve", I("tensor_tensor", out=x1[j % 3][:, cs], in0=bk[pb][:], in1=g1bc[:, cs], op=ALU.mult),
                                 [BK[pb], "g1bc"], [f"x1{j % 3}"])
                        P.op("dve", I("tensor_tensor", out=x1[j % 3][:], in0=x1[j % 3][:], in1=xs[sl][:], op=ALU.add), [f"x1{j % 3}", f"xr{sl}"], [f"x1{j % 3}"])
                        P.dma("sp", out[b, j * 128:(j + 1) * 128, :], x1[j % 3][:], f"st_x1{j % 3}", [f"x1{j % 3}"], [f"out{j}"])
                        return x1[j % 3][:], [f"x1{j % 3}"]
                    blk2 = norm_to_hT(st, b, pre2, fin2, a2, 24, "b")
                    for j in range(NB + 1):
                        blk2(j)
                    P.barrier()
                st5.close()
            if stop_after <= 5:
                continue

            with contextlib.ExitStack() as st:
                wdn = SB(st, "wdn", [128, NF, D], BF16)
                g2bc = SB(st, "g2bc", [128, D], F32)
                gT = SB(st, "gT", [128, NF, 1024], BF16)
                wug = [SB(st, f"wug{i}", [128, KC, 256], BF16) for i in range(2)]
                wuv = [SB(st, f"wuv{i}", [128, KC, 256], BF16) for i in range(2)]
                ug = [SB(st, f"ug{i}", [128, 1026], F32) for i in range(2)]
                uv = [SB(st, f"uv{i}", [128, 1026], F32) for i in range(2)]
                cg = [SB(st, f"cg{i}", [128, 512], F32) for i in range(3)]
                cv = [SB(st, f"cv{i}", [128, 512], F32) for i in range(3)]
                hal = SB(st, "hal", [128, 44, 2], F32)
                xr = [SB(st, f"xq{i}", [128, D], F32) for i in range(2)]
                ot = [SB(st, f"ot{i}", [128, D], F32) for i in range(2)]
                def ld_u(u):
                    f2_, sl_ = u % (NF // 2), u % 2
                    P.dma("pool", wug[sl_][:], w_up_v[:, :, 256 * f2_:256 * f2_ + 256], f"ld_wug{sl_}", (), [f"wug{sl_}"])
                    P.dma("pool", wuv[sl_][:], w_up_v[:, :, FF + 256 * f2_:FF + 256 * f2_ + 256], f"ld_wuv{sl_}", (), [f"wuv{sl_}"])
                ld_u(0)
                for f4 in range(0, NF, 4):
                    n_ = min(4, NF - f4)
                    P.dma("pool", wdn[:, f4:f4 + n_, :], w_down_v[:, f4:f4 + n_, :], "ld_wdn", (), ["wdn"])
                build_gate_bc(g2bc, "g2bc", b, 1)
                P.op("dve", I("memset", hal[:], 0.0), (), ["hal"])
                n6 = 0
                pend6 = []
                nu6 = [0]
                for half in range(2):
                    for f2 in range(NF // 2):
                        u_ = half * (NF // 2) + f2
                        sl = u_ % 2
                        if u_ + 1 < NF:
                            ld_u(u_ + 1)
                        for ff_ in range(2):
                            f = f2 * 2 + ff_
                            fl = ff_ * 128
                            us = n6 % 2
                            n6 += 1
                            for (ubuf, ukey, hidx) in ((ug[us], f"ug{us}", f), (uv[us], f"uv{us}", 22 + f)):
                                act(ubuf[:, 0:2], hal[:, hidx, :], AF.Identity, ["hal"], [ukey])
                            for tt in range(2):
                                t = half * 2 + tt
                                ts_ = slice(t * 512, (t + 1) * 512)
                                i2 = nu6[0] % 3
                                pbg, pbv = 2 * i2, 2 * i2 + 1
                                nu6[0] += 1
                                mm(bk[pbg][:], [(wug[sl][:, k, fl:fl + 128], hT[:, k, ts_]) for k in range(KC)], [f"wug{sl}"] + hT_keys(t), [BK[pbg]])
                                mm(bk[pbv][:], [(wuv[sl][:, k, fl:fl + 128], hT[:, k, ts_]) for k in range(KC)], [f"wuv{sl}"] + hT_keys(t), [BK[pbv]])
                                for (pb, ubuf, ukey, cbuf, ckey, widx) in ((pbg, ug[us], f"ug{us}", cg[i2], f"cg{i2}", f), (pbv, uv[us], f"uv{us}", cv[i2], f"cv{i2}", 22 + f)):
                                    u0 = 2 + tt * 512
                                    act(ubuf[:, u0:u0 + 512], bk[pb][:], AF.Identity, [BK[pb]], [ukey])
                                    act(cbuf[:], bk[pb][:], AF.Identity, [BK[pb], "convw", "convb"], [ckey], bias=convb[:, widx:widx + 1], scale=convw[:, widx, 2:3])
                                    P.op("dve", I("scalar_tensor_tensor",
                                        out=cbuf[:], in0=ubuf[:, u0 - 1:u0 + 511], scalar=convw[:, widx, 1:2], in1=cbuf[:], op0=ALU.mult, op1=ALU.add),
                                        [ukey, ckey, "convw"], [ckey])
                                    P.op("dve", I("scalar_tensor_tensor",
                                        out=cbuf[:], in0=ubuf[:, u0 - 2:u0 + 510], scalar=convw[:, widx, 0:1], in1=cbuf[:], op0=ALU.mult, op1=ALU.add),
                                        [ukey, ckey, "convw"], [ckey])
                                if pend6:
                                    pend6.pop()()

                                def fin(i2=i2, f=f, tt=tt):
                                    act(cg[i2][:], cg[i2][:], AF.Silu, [f"cg{i2}"], [f"cg{i2}"])
                                    P.op("dve", I("tensor_tensor", out=gT[:, f, tt * 512:(tt + 1) * 512], in0=cg[i2][:], in1=cv[i2][:], op=ALU.mult),
                                         [f"cg{i2}", f"cv{i2}"], ["gT"])
                                pend6.append(fin)
                            for (ubuf, ukey, hidx) in ((ug[us], f"ug{us}", f), (uv[us], f"uv{us}", 22 + f)):
                                act(hal[:, hidx, :], ubuf[:, 1024:1026], AF.Identity, [ukey], ["hal"])
                    if pend6:
                        pend6.pop()()
                    for jj in range(8):
                        j = half * 8 + jj
                        sl = jj % 2
                        P.dma("sp", xr[sl][:], out[b, j * 128:(j + 1) * 128, :], f"ld_xq{sl}", [f"out{j}"], [f"xq{sl}"])
                        for hf in range(2):
                            pb = 6 + hf
                            cs = slice(hf * 512, (hf + 1) * 512)
                            mm(bk[pb][:], [(gT[:, f, jj * 128:(jj + 1) * 128], wdn[:, f, cs]) for f in range(NF)], ["gT", "wdn"], [BK[pb]])
                            P.op("dve", I("tensor_tensor", out=ot[sl][:, cs], in0=bk[pb][:], in1=g2bc[:, cs], op=ALU.mult),
                                 [BK[pb], "g2bc"], [f"ot{sl}"])
                        P.op("dve", I("tensor_tensor", out=ot[sl][:], in0=ot[sl][:], in1=xr[sl][:], op=ALU.add), [f"ot{sl}", f"xq{sl}"], [f"ot{sl}"])
                        P.dma("sp", out[b, j * 128:(j + 1) * 128, :], ot[sl][:], f"st_ot{sl}", [f"ot{sl}"], [f"out{j}"])
                P.barrier()
        P.barrier()
        P.run()
    return nc


def _prep_inputs(inp, core):
    f = lambda a: np.ascontiguousarray(np.asarray(a, dtype=np.float32))
    b0 = 2 * core
    c = np.asarray(inp["c"], dtype=np.float32)[b0:b0 + 2]

    def pk(v, n):
        return f(np.asarray(v, dtype=np.float32).reshape(n, 128).T)
    conv_w = np.asarray(inp["conv_w"], dtype=np.float32)[0]
    m = {
        "x": f(np.asarray(inp["x"])[b0:b0 + 2]),
        "cT": f(c.T.reshape(KC, 128, 2).transpose(1, 0, 2)),
        "w_ada": f(inp["w_ada"][0]),
        "b_ada": f(inp["b_ada"][0]),
        "b_adaT": pk(inp["b_ada"][0], 48),
        "n1wT": pk(inp["norm1_w"][0], KC),
        "n2wT": pk(inp["norm2_w"][0], KC),
        "mnwT": pk(inp["mlstm_norm_w"][0], KC),
        "w_in": f(inp["w_in"][0]),
        "b_i": f(np.asarray(inp["b_mlstm_i"][0]).reshape(4, 1)),
        "b_f20": f(np.concatenate([np.asarray(inp["b_mlstm_f"][0]), np.asarray(inp["b_fox_f"][0])]).reshape(20, 1)),
        "fqw": f(np.tile(np.asarray(inp["fox_q_norm_w"][0]), 2).reshape(128, 1)),
        "fkw": f(np.tile(np.asarray(inp["fox_k_norm_w"][0]), 2).reshape(128, 1)),
        "w_bm": f(inp["w_branch_mlstm"][0]),
        "w_bf": f(inp["w_branch_fox"][0]),
        "w_out": f(inp["w_out"][0]),
        "w_up": f(inp["w_up"][0]),
        "convT": f(conv_w.reshape(3, 44, 128).transpose(2, 1, 0)),
        "convbT": pk(inp["conv_b"][0], 44),
        "w_down": f(inp["w_down"][0]),
    }
    return m


def kernel(**inputs):
    nc = build_program()
    in_maps = [_prep_inputs(inputs, i) for i in range(NCORES)]
    res = run_bass_kernel_spmd(nc, in_maps, core_ids=list(range(NCORES)))
    return np.concatenate([np.asarray(r["out"]) for r in res.results], axis=0).astype(np.float32)
```

```python
import contextlib
import os
import numpy as np
import concourse.bass as bass
import concourse.mybir as mybir
from concourse.bass_utils import run_bass_kernel_spmd

F32 = mybir.dt.float32
BF16 = mybir.dt.bfloat16
AF = mybir.ActivationFunctionType
ALU = mybir.AluOpType

ENGS = ("pe", "act", "dve", "pool", "sp")
NCORES = 8
S = 2048
NB = 16
D = 1024
KC = 8
FF = 2816
NF = 22
EPS = 1e-6
O_MQ, O_MK, O_MV, O_MO, O_MI, O_MF = 0, 1024, 2048, 3072, 4096, 4100
O_FQ, O_FK, O_FV, O_FF, O_GA, O_GB = 4104, 5128, 6152, 7176, 7192, 8216


class Prog:
    def __init__(self, nc, stack):
        self.nc = nc
        self.q = {e: [] for e in ENGS}
        self.cnt = {}
        self.sem = {}
        self.stack = stack
        for e in ENGS:
            self.sem[e] = stack.enter_context(nc.semaphore("s_" + e))
            self.cnt[e] = 0
        self.waited = {e: {} for e in ENGS}
        self.last_w = {}
        self.readers = {}

    def dma_sem(self, key):
        if key not in self.sem:
            self.sem[key] = self.stack.enter_context(self.nc.semaphore("d_" + key))
            self.cnt[key] = 0
        return key

    def _deps(self, eng, reads, writes):
        deps = {}

        def add(d):
            if d is None:
                return
            k, v = d
            if deps.get(k, 0) < v:
                deps[k] = v

        for r in reads:
            add(self.last_w.get(r))
            if r.startswith("bk"):
                for k, v in self.readers.get(r, {}).items():
                    if k != eng:
                        add((k, v))
        for w in writes:
            add(self.last_w.get(w))
            for k, v in self.readers.get(w, {}).items():
                add((k, v))
        return deps

    def _emit_waits(self, eng, deps):
        for k, v in deps.items():
            if self.waited[eng].get(k, 0) >= v:
                continue
            self.waited[eng][k] = v
            sem = self.sem[k]
            self.q[eng].append(lambda e, sem=sem, v=v: e.wait_ge(sem, v))

    def _mark(self, key, val, reads, writes):
        for r in reads:
            self.readers.setdefault(r, {})[key] = val
        for w in writes:
            self.last_w[w] = (key, val)
            self.readers[w] = {}

    def op(self, eng, fns, reads=(), writes=()):
        if not isinstance(fns, (list, tuple)):
            fns = [fns]
        self._emit_waits(eng, self._deps(eng, reads, writes))
        sem = self.sem[eng]
        self.cnt[eng] += 1
        n = self.cnt[eng]
        for f in fns[:-1]:
            self.q[eng].append(f)
        last = fns[-1]
        self.q[eng].append(lambda e, last=last, sem=sem: last(e).then_inc(sem, 1))
        self._mark(eng, n, reads, writes)
        return n

    def dma(self, qeng, out, in_, semkey, reads=(), writes=()):
        self.dma_sem(semkey)
        self._emit_waits(qeng, self._deps(None, reads, writes))
        sem = self.sem[semkey]
        self.cnt[semkey] += 16
        v = self.cnt[semkey]
        self.q[qeng].append(lambda e, out=out, in_=in_, sem=sem: e.dma_start(out=out, in_=in_).then_inc(sem, 16))
        self._mark(semkey, v, reads, writes)
        return v

    def barrier(self):
        allk = {k: v for k, v in self.cnt.items() if v > 0}
        for e in ENGS:
            self._emit_waits(e, dict(allk))
        self.last_w = {}
        self.readers = {}

    def run(self):
        nc = self.nc
        with nc.Block() as block:
            @block.tensor
            def _(e):
                for f in self.q["pe"]:
                    f(e)

            @block.scalar
            def _(e):
                for f in self.q["act"]:
                    f(e)

            @block.vector
            def _(e):
                for f in self.q["dve"]:
                    f(e)

            @block.gpsimd
            def _(e):
                for f in self.q["pool"]:
                    f(e)

            @block.sync
            def _(e):
                for f in self.q["sp"]:
                    f(e)


def build_program(nseq=2, stop_after=99, dbg=None, fox_limit=99):
    nc = bass.Bass("TRN2", target_bir_lowering=False)

    def din(name, shape):
        return nc.dram_tensor(name, list(shape), F32, kind="ExternalInput").ap()

    x = din("x", [2, S, D])
    cT = din("cT", [128, KC, 2])
    w_ada = din("w_ada", [D, 6 * D])
    b_ada = din("b_ada", [6 * D])
    b_adaT = din("b_adaT", [128, 48])
    n1wT = din("n1wT", [128, KC])
    n2wT = din("n2wT", [128, KC])
    mnwT = din("mnwT", [128, KC])
    w_in = din("w_in", [D, 9240])
    b_i = din("b_i", [4, 1])
    b_f20 = din("b_f20", [20, 1])
    fqw = din("fqw", [128, 1])
    fkw = din("fkw", [128, 1])
    w_bm = din("w_bm", [D, D])
    w_bf = din("w_bf", [D, D])
    w_out = din("w_out", [D, D])
    w_up = din("w_up", [D, 2 * FF])
    convT = din("convT", [128, 44, 3])
    convbT = din("convbT", [128, 44])
    w_down = din("w_down", [FF, D])
    out = nc.dram_tensor("out", [2, S, D], F32, kind="ExternalOutput").ap()
    npscr = nc.dram_tensor("npscr", [20, 3 * S], BF16).ap()
    dbg_t = None
    if dbg is not None:
        dbg_t = nc.dram_tensor("dbg", list(dbg), F32, kind="ExternalOutput").ap()

    w_in_v = w_in.rearrange("(k p) n -> p k n", p=128)
    w_ada_v = w_ada.rearrange("(k p) n -> p k n", p=128)
    w_bm_v = w_bm.rearrange("(k p) n -> p k n", p=128)
    w_bf_v = w_bf.rearrange("(k p) n -> p k n", p=128)
    w_out_v = w_out.rearrange("(k p) n -> p k n", p=128)
    w_up_v = w_up.rearrange("(k p) n -> p k n", p=128)
    w_down_v = w_down.rearrange("(k p) n -> p k n", p=128)

    with contextlib.ExitStack() as st0:
        P = Prog(nc, st0)

        uid = [0]

        def SB(st, name, shape, dt):
            uid[0] += 1
            return st.enter_context(nc.sbuf_tensor(f"{name}_{uid[0]}", list(shape), dt))

        bk = [st0.enter_context(nc.psum_tensor(f"bk{i}", [128, 512], F32)) for i in range(8)]
        BK = [f"bk{i}" for i in range(8)]

        def bkb(i):
            return bk[i][:].bitcast(BF16)

        def I(name, *a, **kw):
            return lambda e: getattr(e, name)(*a, **kw)

        def mm(out_ap, pairs, reads, writes):
            n = len(pairs)
            fns = []
            for i, (l, r) in enumerate(pairs):
                fns.append(I("matmul", out_ap, lhsT=l, rhs=r, start=(i == 0), stop=(i == n - 1)))
            P.op("pe", fns, reads, writes)

        def act(out_ap, in_ap, func, reads, writes, bias=None, scale=None, accum_out=None):
            kw = {}
            if bias is not None:
                kw["bias"] = bias
            if scale is not None:
                kw["scale"] = scale
            if accum_out is not None:
                kw["accum_out"] = accum_out
            P.op("act", I("activation", out=out_ap, in_=in_ap, func=func, **kw), reads, writes)

        ident_bf = SB(st0, "ident_bf", [128, 128], BF16)
        ident_f = SB(st0, "ident_f", [128, 128], F32)
        maskb = SB(st0, "maskb", [128, 128], BF16)
        mask01 = SB(st0, "mask01", [128, 128], F32)
        bdones = SB(st0, "bdones", [128, 128], BF16)
        sel = SB(st0, "sel", [4, 512], F32)
        tmpc = SB(st0, "tmpc", [128, 512], F32)
        mask4 = SB(st0, "mask4", [128, 512], F32)

        P.op("pool", I("memset", tmpc[:], 1.0), (), ["tmpc"])
        P.op("pool", I("affine_select", out=ident_f[:], in_=tmpc[:, 0:128], pattern=[[-1, 128]], compare_op=ALU.is_equal,
                                               fill=0.0, base=0, channel_multiplier=1), ["tmpc"], ["ident_f"])
        P.op("pool", I("tensor_copy", out=ident_bf[:], in_=ident_f[:]), ["ident_f"], ["ident_bf"])
        P.op("pool", I("affine_select", out=mask01[:], in_=tmpc[:, 0:128], pattern=[[1, 128]], compare_op=ALU.is_ge,
                                               fill=0.0, base=0, channel_multiplier=-1), ["tmpc"], ["mask01"])
        for i4 in range(4):
            P.op("pool", I("tensor_copy", out=mask4[:, i4 * 128:(i4 + 1) * 128], in_=mask01[:]), ["mask01"], ["mask4"])
        P.op("pool", I("tensor_scalar", out=maskb[:], in0=mask01[:], scalar1=-1.0, scalar2=30000.0,
                                               op0=ALU.add, op1=ALU.mult), ["mask01"], ["maskb"])
        P.op("pool", I("memset", bdones[:], 0.0), (), ["bdones"])
        P.op("pool", I("memset", bdones[0:64, 0:64], 1.0), (), ["bdones"])
        P.op("pool", I("memset", bdones[64:128, 64:128], 1.0), (), ["bdones"])
        P.op("pool", I("affine_select", out=sel[:].rearrange("p (h j) -> p h j", j=128),
                                               in_=tmpc[0:4, :].rearrange("p (h j) -> p h j", j=128),
                                               pattern=[[-1, 4], [0, 128]], compare_op=ALU.is_equal,
                                               fill=0.0, base=0, channel_multiplier=1), ["tmpc"], ["sel"])

        cT_sb = SB(st0, "cT_sb", [128, KC, 2], F32)
        sc = SB(st0, "sc", [128, KC, 2], BF16)
        badaT = SB(st0, "badaT", [128, 48], F32)
        n1w = SB(st0, "n1w", [128, KC], F32)
        n2w = SB(st0, "n2w", [128, KC], F32)
        mnw = SB(st0, "mnw", [128, KC], F32)
        bi = SB(st0, "bi", [4, 1], F32)
        nbf20 = SB(st0, "nbf20", [20, 1], F32)
        fqw_s = SB(st0, "fqw_s", [128, 1], F32)
        fkw_s = SB(st0, "fkw_s", [128, 1], F32)
        convw = SB(st0, "convw", [128, 44, 3], F32)
        convb = SB(st0, "convb", [128, 44], F32)
        adaT = SB(st0, "adaT", [128, 48, 2], F32)
        a1 = SB(st0, "a1", [128, KC, 2], F32)
        a2 = SB(st0, "a2", [128, KC, 2], F32)
        grow = SB(st0, "grow", [2, 2048], F32)
        hT = SB(st0, "hT", [128, KC, S], BF16)

        for (t_, src, key) in [(cT_sb, cT, "cT_sb"), (badaT, b_adaT, "badaT"), (n1w, n1wT, "n1w"), (n2w, n2wT, "n2w"),
                               (mnw, mnwT, "mnw"), (bi, b_i, "bi"), (nbf20, b_f20, "nbf20"), (fqw_s, fqw, "fqw_s"),
                               (fkw_s, fkw, "fkw_s"), (convw, convT, "convw"), (convb, convbT, "convb")]:
            P.dma("sp", t_[:], src, "ld_" + key, (), [key])
        P.op("dve", I("tensor_scalar", out=nbf20[:], in0=nbf20[:], scalar1=-1.0, scalar2=None, op0=ALU.mult), ["nbf20"], ["nbf20"])
        P.op("dve", I("tensor_scalar", out=fqw_s[:], in0=fqw_s[:], scalar1=0.125, scalar2=None, op0=ALU.mult), ["fqw_s"], ["fqw_s"])

        with contextlib.ExitStack() as st:
            bgate = SB(st, "bgate", [2, 2048], F32)
            wa = [SB(st, f"wa{i}", [128, KC, 1536], BF16) for i in range(2)]
            for r in range(2):
                P.dma("sp", bgate[r:r + 1, 0:1024], b_ada[2048:3072].rearrange("(o n) -> o n", o=1), "ld_bg", (), ["bgate"])
                P.dma("sp", bgate[r:r + 1, 1024:2048], b_ada[5120:6144].rearrange("(o n) -> o n", o=1), "ld_bg", (), ["bgate"])
            act(sc[:], cT_sb[:], AF.Silu, ["cT_sb"], ["sc"])
            for pc in range(4):
                sl = pc % 2
                P.dma("pool", wa[sl][:], w_ada_v[:, :, pc * 1536:(pc + 1) * 1536], f"ld_wa{sl}", (), [f"wa{sl}"])
                for ch in range(12):
                    gch = pc * 12 + ch
                    mm(bk[0][:, gch * 2:gch * 2 + 2],
                       [(wa[sl][:, k, ch * 128:(ch + 1) * 128], sc[:, k, :]) for k in range(KC)],
                       [f"wa{sl}", "sc"], [BK[0]])
                if pc in (1, 3):
                    for hf in range(2):
                        bi_ = 1 + (pc // 2) * 2 + hf
                        mm(bk[bi_][0:2, :],
                           [(sc[:, k, :], wa[sl][:, k, 512 + hf * 512:1024 + hf * 512]) for k in range(KC)],
                           [f"wa{sl}", "sc"], [BK[bi_]])
            psv = bk[0][:, 0:96].rearrange("p (c b) -> p c b", b=2)
            for b in range(2):
                P.op("dve", I("tensor_tensor", out=adaT[:, :, b], in0=psv[:, :, b], in1=badaT[:], op=ALU.add),
                     [BK[0], "badaT"], ["adaT"])
            for b in range(2):
                P.op("dve", I("scalar_tensor_tensor", out=a1[:, :, b], in0=adaT[:, 8:16, b], scalar=1.0, in1=n1w[:],
                                                                  op0=ALU.add, op1=ALU.mult), ["adaT", "n1w"], ["a1"])
                P.op("dve", I("scalar_tensor_tensor", out=a2[:, :, b], in0=adaT[:, 32:40, b], scalar=1.0, in1=n2w[:],
                                                                  op0=ALU.add, op1=ALU.mult), ["adaT", "n2w"], ["a2"])
            for i in range(4):
                P.op("dve", I("tensor_tensor", out=grow[:, i * 512:(i + 1) * 512], in0=bk[1 + i][0:2, :],
                                                           in1=bgate[:, i * 512:(i + 1) * 512], op=ALU.add),
                     [BK[1 + i], "bgate"], ["grow"])
            P.barrier()

        def build_gate_bc(gbc, key, b, g):
            for hf in range(2):
                mm(bk[6 + hf][:], [(sel[0:2, b * 128:(b + 1) * 128], grow[0:2, g * 1024 + hf * 512:g * 1024 + (hf + 1) * 512])],
                   ["sel", "grow"], [BK[6 + hf]])
                P.op("dve", I("tensor_copy", out=gbc[:, hf * 512:(hf + 1) * 512], in_=bk[6 + hf][:]),
                     [BK[6 + hf]], [key])

        def norm_to_hT(st, b, pre_fn, fin_fn, aX, shcol0, tag):
            xn = [SB(st, f"xn{tag}{i}", [128, D], BF16) for i in range(2)]
            junk = SB(st, f"junk{tag}", [128, D], BF16)
            stt = [SB(st, f"stt{tag}{i}", [128, 4], F32) for i in range(2)]
            held = {}

            def do_block(j):
                if j == 0:
                    pre_fn(0)
                if j + 1 < NB:
                    pre_fn(j + 1)
                if j < NB:
                    xs_ap, xs_keys = fin_fn(j)
                    s2 = j % 2
                    act(junk[:], xs_ap, AF.Square, xs_keys, [f"junk{tag}", f"stt{tag}{s2}"], accum_out=stt[s2][:, 0:1])
                    act(stt[s2][:, 1:2], stt[s2][:, 0:1], AF.Ln, [f"stt{tag}{s2}"], [f"stt{tag}{s2}"], bias=EPS, scale=1.0 / D)
                    act(stt[s2][:, 2:3], stt[s2][:, 1:2], AF.Exp, [f"stt{tag}{s2}"], [f"stt{tag}{s2}"], scale=-0.5)
                    P.op("dve", I("tensor_scalar", out=xn[s2][:], in0=xs_ap, scalar1=stt[s2][:, 2:3], scalar2=None, op0=ALU.mult),
                         xs_keys + [f"stt{tag}{s2}"], [f"xn{tag}{s2}"])
                    for hf in range(2):
                        tb = 4 + 2 * hf + s2
                        tp = bkb(tb)
                        P.op("pe", [I("transpose", out=tp[:, c * 128:(c + 1) * 128], in_=xn[s2][:, (4 * hf + c) * 128:(4 * hf + c + 1) * 128],
                                      identity=ident_bf[:]) for c in range(4)], [f"xn{tag}{s2}", "ident_bf"], [BK[tb]])
                if j >= 1:
                    jb = j - 1
                    s2 = jb % 2
                    tpa, tpb = bkb(4 + s2), bkb(6 + s2)
                    for c in range(4):
                        act(hT[:, c, jb * 128:(jb + 1) * 128], tpa[:, c * 128:(c + 1) * 128], AF.Identity,
                            [BK[4 + s2]], [f"hT{jb}"], bias=adaT[:, shcol0 + c, b:b + 1], scale=aX[:, c, b:b + 1])
                    for c in range(4, 8):
                        P.op("dve", I("tensor_scalar", out=hT[:, c, jb * 128:(jb + 1) * 128], in0=tpb[:, (c - 4) * 128:(c - 3) * 128],
                                      scalar1=aX[:, c, b:b + 1], scalar2=adaT[:, shcol0 + c, b:b + 1], op0=ALU.mult, op1=ALU.add),
                             [BK[6 + s2]], [f"hT{jb}"])
            return do_block

        hT_all = [f"hT{j}" for j in range(NB)]

        def hT_keys(t):
            return [f"hT{j}" for j in range(4 * t, 4 * t + 4)]

        for b in range(nseq):
            with contextlib.ExitStack() as st:
                xs = [SB(st, f"xs{i}", [128, D], F32) for i in range(3)]

                def ldx1(j):
                    P.dma("sp", xs[j % 3][:], x[b, j * 128:(j + 1) * 128, :], f"ld_xs{j % 3}", (), [f"xs{j % 3}"])

                def fin1(j):
                    return xs[j % 3][:], [f"xs{j % 3}"]
                blk1 = norm_to_hT(st, b, ldx1, fin1, a1, 0, "a")
                for j in range(NB + 1):
                    blk1(j)
                P.barrier()
            if stop_after <= 1:
                continue

            def gates(want_m, want_f, Grow=None, A_tm=None, X_tm=None, NP_=None):
                with contextlib.ExitStack() as st:
                    wi = SB(st, "wi", [128, KC, 4], BF16)
                    wf = SB(st, "wf", [128, KC, 20], BF16)
                    ipre = SB(st, "ipre", [4, S], F32)
                    spl = SB(st, "spl", [20, S], F32)
                    negF = SB(st, "negF", [20, S], F32)
                    P.dma("pool", wi[:], w_in_v[:, :, O_MI:O_MI + 4], "ld_wi", (), ["wi"])
                    P.dma("pool", wf[:, :, 0:4], w_in_v[:, :, O_MF:O_MF + 4], "ld_wf", (), ["wf"])
                    P.dma("pool", wf[:, :, 4:20], w_in_v[:, :, O_FF:O_FF + 16], "ld_wf", (), ["wf"])
                    for t in range(4):
                        ts_ = slice(t * 512, (t + 1) * 512)
                        mm(bk[t % 2][0:4, :], [(wi[:, k, :], hT[:, k, ts_]) for k in range(KC)], ["wi"] + hT_keys(t), [BK[t % 2]])
                        mm(bk[2 + t % 2][0:20, :], [(wf[:, k, :], hT[:, k, ts_]) for k in range(KC)], ["wf"] + hT_keys(t), [BK[2 + t % 2]])
                        act(ipre[:, ts_], bk[t % 2][0:4, :], AF.Identity, [BK[t % 2], "bi"], ["ipre"], bias=bi[:, 0:1])
                        act(spl[:, ts_], bk[2 + t % 2][0:20, :], AF.Exp, [BK[2 + t % 2], "nbf20"], ["spl"], bias=nbf20[:, 0:1], scale=-1.0)
                    act(spl[:], spl[:], AF.Ln, ["spl"], ["spl"], bias=1.0)
                    P.op("dve", I("tensor_tensor_scan", out=negF[:], data0=spl[:], data1=spl[:], initial=0.0, op0=ALU.add, op1=ALU.max),
                         ["spl"], ["negF"])
                    if want_m:
                        Am = SB(st, "Am", [4, S], F32)
                        Xm = SB(st, "Xm", [4, S], F32)
                        P.op("dve", I("tensor_tensor", out=Am[:], in0=ipre[:], in1=negF[0:4, :], op=ALU.add), ["ipre", "negF"], ["Am"])
                        P.op("dve", I("memset", Grow[:, 0:1], 0.0), (), ["Grow"])
                        P.op("dve", I("tensor_tensor_scan", out=Grow[:, 1:S + 1], data0=Am[:], data1=Am[:], initial=0.0, op0=ALU.max, op1=ALU.max),
                             ["Am"], ["Grow"])
                        P.op("dve", I("tensor_tensor", out=Xm[:], in0=negF[0:4, :], in1=Grow[:, 1:S + 1], op=ALU.subtract), ["negF", "Grow"], ["Xm"])
                        P.op("pe", [I("transpose", out=bk[4][:, j * 4:(j + 1) * 4], in_=Am[0:4, j * 128:(j + 1) * 128],
                                                               identity=ident_f[0:4, 0:4]) for j in range(NB)], ["Am", "ident_f"], [BK[4]])
                        P.op("pe", [I("transpose", out=bk[5][:, j * 4:(j + 1) * 4], in_=Xm[0:4, j * 128:(j + 1) * 128],
                                                               identity=ident_f[0:4, 0:4]) for j in range(NB)], ["Xm", "ident_f"], [BK[5]])
                        P.op("dve", I("tensor_copy", out=A_tm[:], in_=bk[4][:, 0:64]), [BK[4]], ["A_tm"])
                        P.op("dve", I("tensor_copy", out=X_tm[:], in_=bk[5][:, 0:64]), [BK[5]], ["X_tm"])
                    if want_f:
                        r1 = SB(st, "r1", [20, S], F32)
                        NP_ = SB(st, "NP", [20, 3, S], BF16)
                        P.op("dve", I("tensor_copy", out=NP_[:, 0, :], in_=negF[:]), ["negF"], ["NP"])
                        P.op("dve", I("tensor_tensor", out=r1[:], in0=negF[:], in1=NP_[:, 0, :], op=ALU.subtract), ["negF", "NP"], ["r1"])
                        P.op("dve", I("tensor_copy", out=NP_[:, 1, :], in_=r1[:]), ["r1"], ["NP"])
                        P.op("dve", I("tensor_tensor", out=r1[:], in0=r1[:], in1=NP_[:, 1, :], op=ALU.subtract), ["r1", "NP"], ["r1"])
                        P.op("dve", I("tensor_copy", out=NP_[:, 2, :], in_=r1[:]), ["r1"], ["NP"])
                        P.dma("sp", npscr.rearrange("p (r t) -> p r t", r=3), NP_[:], "st_np", ["NP"], ["npscr"])
                    P.barrier()

            with contextlib.ExitStack() as st2:
                hmT = SB(st2, "hmT", [128, KC, S], BF16)
                st2b = contextlib.ExitStack()
                Grow = SB(st2b, "Grow", [4, S + 1], F32)
                A_tm = SB(st2b, "A_tm", [128, 64], F32)
                X_tm = SB(st2b, "X_tm", [128, 64], F32)
                gates(True, True, Grow=Grow, A_tm=A_tm, X_tm=X_tm)

                with contextlib.ExitStack() as st:
                    qT = SB(st, "qT", [128, 2, S], BF16)
                    kT = SB(st, "kT", [128, 2, S], BF16)
                    ktm = SB(st, "ktm", [128, NB, 256], BF16)
                    vA = SB(st, "vA", [128, NB, 258], BF16)
                    vwa = SB(st, "vwa", [128, NB, 258], BF16)
                    STb = SB(st, "STb", [128, S], BF16)
                    oS = SB(st, "oS", [128, NB, 256], BF16)
                    Cf = SB(st, "Cf", [128, 2, 257], F32)
                    Cb = [SB(st, f"Cb{i}", [128, 2, 258], BF16) for i in range(2)]
                    sta = SB(st, "sta", [128, 12, 16], F32)
                    hmtm = [SB(st, f"hmtm{i}", [128, 256], BF16) for i in range(2)]
                    junk4 = SB(st, "junk4", [128, 256], BF16)
                    wq_ = [SB(st, f"wq{i}", [128, KC, 256], BF16) for i in range(2)]
                    wk_ = [SB(st, f"wk{i}", [128, KC, 256], BF16) for i in range(2)]
                    wv_ = [SB(st, f"wv{i}", [128, KC, 256], BF16) for i in range(2)]
                    Gbc = SB(st, "Gbc", [128, S + 1], F32)
                    DTa = SB(st, "DTa", [128, S], F32)
                    Eba = SB(st, "Eba", [128, S], F32)
                    tmpS = [SB(st, f"tmpS{i}", [128, 512], F32) for i in range(2)]

                    def ld_m(h):
                        i = h % 2
                        P.dma("pool", wq_[i][:], w_in_v[:, :, O_MQ + 256 * h:O_MQ + 256 * h + 256], f"ld_wq{i}", (), [f"wq{i}"])
                        P.dma("pool", wk_[i][:], w_in_v[:, :, O_MK + 256 * h:O_MK + 256 * h + 256], f"ld_wk{i}", (), [f"wk{i}"])
                        P.dma("pool", wv_[i][:], w_in_v[:, :, O_MV + 256 * h:O_MV + 256 * h + 256], f"ld_wv{i}", (), [f"wv{i}"])
                    ld_m(0)
                    P.op("dve", I("memset", vA[:, :, 256:257], 1.0), (), [f"vA{j}" for j in range(NB)])
                    WALL, DCY, EXA, SS, ND, MDEN, RR, VAR, FAC, TMP, DEN, FACP = range(12)

                    def post_block(h, j):
                        s2 = j % 2
                        js = slice(j * 128, (j + 1) * 128)
                        tb = 2 + s2
                        tp = bkb(tb)
                        act(hmtm[s2][:], oS[:, j, :], AF.Identity, [f"oS{j}", "staFP"], [f"hmtm{s2}"], scale=sta[:, FACP, j:j + 1])
                        P.op("pe", [I("transpose", out=tp[:, c * 128:(c + 1) * 128], in_=hmtm[s2][:, c * 128:(c + 1) * 128],
                                      identity=ident_bf[:]) for c in range(2)], [f"hmtm{s2}", "ident_bf"], [BK[tb]])
                        P.op("dve", I("tensor_copy", out=hmT[:, 2 * h:2 * h + 2, js], in_=tp[:, 0:256].rearrange("p (c t) -> p c t", t=128)),
                             [BK[tb]], ["hmT"])

                    pending_post = []
                    for h in range(4):
                        wq, wk, wv = wq_[h % 2], wk_[h % 2], wv_[h % 2]
                        wqk, wkk, wvk = f"wq{h % 2}", f"wk{h % 2}", f"wv{h % 2}"
                        if h + 1 < 4:
                            ld_m(h + 1)
                        for t5 in range(5):
                            c0, c1 = t5 * 512, min(S + 1, (t5 + 1) * 512)
                            gb_ = 6 + t5 % 2
                            mm(bk[gb_][:, 0:c1 - c0], [(sel[0:4, h * 128:(h + 1) * 128], Grow[0:4, c0:c1])], ["sel", "Grow"], [BK[gb_]])
                            P.op("dve", I("tensor_copy", out=Gbc[:, c0:c1], in_=bk[gb_][:, 0:c1 - c0]), [BK[gb_]], ["Gbc"])
                        g0v = Gbc[:, 0:S].rearrange("p (j t) -> p j t", t=128)[:, :, 0]
                        gEv = Gbc[:, 1:S + 1].rearrange("p (j t) -> p j t", t=128)[:, :, 127]
                        Av = A_tm[:].rearrange("p (j h) -> p j h", h=4)[:, :, h]
                        Xv = X_tm[:].rearrange("p (j h) -> p j h", h=4)[:, :, h]
                        P.op("dve", I("tensor_tensor", out=sta[:, TMP, :], in0=Av, in1=gEv, op=ALU.subtract), ["A_tm", "Gbc"], ["staT"])
                        act(sta[:, WALL, :], sta[:, TMP, :], AF.Exp, ["staT"], ["staW"])
                        P.op("dve", I("tensor_tensor", out=sta[:, ND, :], in0=g0v, in1=gEv, op=ALU.subtract), ["Gbc"], ["staN"])
                        act(sta[:, DCY, :], sta[:, ND, :], AF.Exp, ["staN"], ["staD"])
                        act(sta[:, EXA, :], Xv, AF.Exp, ["X_tm"], ["staE"])
                        exq = []
                        for j in range(NB):
                            js = slice(j * 128, (j + 1) * 128)
                            jg = slice(1 + j * 128, 1 + (j + 1) * 128)
                            exq.append((DTa[:, js], Gbc[:, jg], ["Gbc", "A_tm"], [f"DTa{j // 4}"], A_tm[:, j * 4 + h:j * 4 + h + 1]))
                            exq.append((Eba[:, js], Gbc[:, jg], ["Gbc"], ["Eba"], Gbc[:, j * 128:j * 128 + 1]))
                        pj = 0
                        for (wt, wkey, dst, dkey, scl) in ((wq, wqk, qT, "qT", 1.0 / 16), (wk, wkk, kT, "kT", 1.0)):
                            for c in range(2):
                                for t in range(4):
                                    ts_ = slice(t * 512, (t + 1) * 512)
                                    pb = 4 + pj % 2
                                    mm(bk[pb][:], [(wt[:, k, c * 128:(c + 1) * 128], hT[:, k, ts_]) for k in range(KC)], [wkey] + hT_keys(t), [BK[pb]])
                                    act(dst[:, c, ts_], bk[pb][:], AF.Identity, [BK[pb]], [dkey], scale=scl)
                                    for _ in range(2):
                                        o_, i_, r_, w_, b_ = exq.pop(0)
                                        act(o_, i_, AF.Exp, r_, w_, bias=b_, scale=-1.0)
                                    if pending_post:
                                        post_block(*pending_post.pop(0))
                                    pj += 1
                        for g4 in range(4):
                            gs = slice(g4 * 512, (g4 + 1) * 512)
                            P.op("pool", I("tensor_tensor", out=DTa[:, gs], in0=DTa[:, gs], in1=mask4[:], op=ALU.mult),
                                 [f"DTa{g4}", "mask4"], [f"DTm{g4}", f"DTa{g4}"])
                        for j2 in range(NB // 2):
                            pb = 4 + pj % 2
                            pj += 1
                            for jj in range(2):
                                j = j2 * 2 + jj
                                mm(bk[pb][:, jj * 256:(jj + 1) * 256], [(hT[:, k, j * 128:(j + 1) * 128], wv[:, k, :]) for k in range(KC)],
                                   [wvk, f"hT{j}"], [BK[pb]])
                            act(vA[:, j2 * 2:j2 * 2 + 2, 0:256], bk[pb][:].rearrange("p (j c) -> p j c", c=256), AF.Identity, [BK[pb]],
                                [f"vA{j2 * 2}", f"vA{j2 * 2 + 1}"])
                            for jj in range(2):
                                j = j2 * 2 + jj
                                P.op("dve", I("tensor_scalar", out=vwa[:, j, 0:257], in0=vA[:, j, 0:257], scalar1=sta[:, WALL, j:j + 1], scalar2=None, op0=ALU.mult),
                                     [f"vA{j}", "staW"], [f"vwa{j}"])
                        for g4 in range(4):
                            tb = 6 + g4 % 2
                            tp = bkb(tb)
                            fns = []
                            for jj in range(4):
                                j = g4 * 4 + jj
                                for c in range(2):
                                    fns.append(I("transpose", out=tp[:, jj * 256 + c * 128:jj * 256 + (c + 1) * 128],
                                                 in_=kT[:, c, j * 128:(j + 1) * 128], identity=ident_bf[:]))
                            P.op("pe", fns, ["kT", "ident_bf"], [BK[tb]])
                            P.op("dve", I("tensor_copy", out=ktm[:, g4 * 4:g4 * 4 + 4, :], in_=tp.rearrange("p (j c) -> p j c", c=256)),
                                 [BK[tb]], ["ktm"])
                        while pending_post:
                            post_block(*pending_post.pop(0))
                        for g4 in range(4):
                            gs = slice(g4 * 512, (g4 + 1) * 512)
                            fns = []
                            for jj in range(4):
                                j = g4 * 4 + jj
                                js = slice(j * 128, (j + 1) * 128)
                                for c in range(2):
                                    fns.append(I("matmul", bk[g4][:, jj * 128:(jj + 1) * 128], lhsT=kT[:, c, js], rhs=qT[:, c, js],
                                                 start=(c == 0), stop=(c == 1)))
                            P.op("pe", fns, ["kT", "qT"], [BK[g4]])
                            P.op("dve", I("tensor_tensor", out=STb[:, gs], in0=bk[g4][:], in1=DTa[:, gs], op=ALU.mult),
                                 [BK[g4], f"DTm{g4}"], [f"STb{g4}"])
                        for c in range(2):
                            P.op("dve", I("tensor_tensor", out=qT[:, c, :], in0=qT[:, c, :], in1=Eba[:], op=ALU.mult), ["qT", "Eba"], ["qT"])
                        P.op("dve", I("memset", Cf[:], 0.0), (), ["Cf0", "Cf1"])
                        P.op("dve", I("memset", Cb[0][:], 0.0), (), ["Cb0a", "Cb0b"])

                        def upd(j):
                            for c in range(2):
                                ub = 4 + (j % 2) * 2 + c
                                mm(bk[ub][:, 0:257], [(ktm[:, j, c * 128:(c + 1) * 128], vwa[:, j, 0:257])], ["ktm", f"vwa{j}"], [BK[ub]])
                        upd(0)
                        for j in range(NB):
                            s2 = j % 2
                            js = slice(j * 128, (j + 1) * 128)
                            if j + 1 < NB:
                                upd(j + 1)
                            ob = s2
                            mm(bk[ob][:, 0:257], [(STb[:, js], vA[:, j, 0:257])] + [(qT[:, c, js], Cb[s2][:, c, 0:257]) for c in range(2)],
                               [f"STb{j // 4}", f"vA{j}", "qT", f"Cb{s2}a", f"Cb{s2}b"], [BK[ob]])
                            for c in range(2):
                                ub = 4 + s2 * 2 + c
                                P.op("dve", I("scalar_tensor_tensor", out=Cf[:, c, :], in0=Cf[:, c, :], scalar=sta[:, DCY, j:j + 1], in1=bk[ub][:, 0:257],
                                              op0=ALU.mult, op1=ALU.add), [f"Cf{c}", "staD", BK[ub]], [f"Cf{c}"])
                            if j + 1 < NB:
                                act(Cb[1 - s2][:, 0, 0:257], Cf[:, 0, :], AF.Identity, ["Cf0"], [f"Cb{1 - s2}a"])
                                P.op("dve", I("tensor_copy", out=Cb[1 - s2][:, 1, 0:257], in_=Cf[:, 1, :]), ["Cf1"], [f"Cb{1 - s2}b"])
                            act(junk4[:], bk[ob][:, 0:256], AF.Square, [BK[ob]], ["junk4", "staS"], accum_out=sta[:, SS, j:j + 1])
                            act(oS[:, j, :], bk[ob][:, 0:256], AF.Identity, [BK[ob]], [f"oS{j}"])
                            act(sta[:, DEN, j:j + 1], bk[ob][:, 256:257], AF.Identity, [BK[ob]], ["staDen"])
                        denv = sta[:, DEN, :]
                        P.op("dve", I("tensor_scalar", out=sta[:, ND, :], in0=denv, scalar1=-1.0, scalar2=None, op0=ALU.mult), ["staDen"], ["staN"])
                        P.op("dve", I("tensor_tensor", out=sta[:, MDEN, :], in0=sta[:, ND, :], in1=sta[:, EXA, :], op=ALU.max), ["staN", "staE"], ["staM"])
                        P.op("dve", I("tensor_tensor", out=sta[:, MDEN, :], in0=sta[:, MDEN, :], in1=denv, op=ALU.max), ["staM", "staDen"], ["staM"])
                        P.op("dve", I("reciprocal", out=sta[:, RR, :], in_=sta[:, MDEN, :]), ["staM"], ["staR"])
                        P.op("dve", I("tensor_tensor", out=sta[:, VAR, :], in0=sta[:, SS, :], in1=sta[:, RR, :], op=ALU.mult), ["staS", "staR"], ["staV"])
                        P.op("dve", I("tensor_tensor", out=sta[:, VAR, :], in0=sta[:, VAR, :], in1=sta[:, RR, :], op=ALU.mult), ["staV", "staR"], ["staV"])
                        act(sta[:, VAR, :], sta[:, VAR, :], AF.Ln, ["staV"], ["staV"], bias=EPS, scale=1.0 / 256)
                        act(sta[:, VAR, :], sta[:, VAR, :], AF.Exp, ["staV"], ["staV"], scale=-0.5)
                        P.op("dve", I("tensor_tensor", out=sta[:, FACP, :], in0=sta[:, VAR, :], in1=sta[:, RR, :], op=ALU.mult), ["staV", "staR"], ["staFP"])
                        for j in range(NB):
                            post_block(h, j)
                    while pending_post:
                        post_block(*pending_post.pop(0))
                    P.barrier()
                st2b.close()

                hfT = SB(st2, "hfT", [128, KC, S], BF16)
                st3b = contextlib.ExitStack()

                with contextlib.ExitStack() as st:
                    qaug = [[SB(st, f"qaug{s}{i}", [128, S], BF16) for i in range(2)] for s in range(2)]
                    kaug = [[SB(st, f"kaug{s}{i}", [128, S], BF16) for i in range(2)] for s in range(2)]
                    vaug = [SB(st, f"vaug{s}", [128, NB, 192], BF16) for s in range(2)]
                    wq2_ = [SB(st, f"wq2{i}", [128, KC, 128], BF16) for i in range(2)]
                    wk2_ = [SB(st, f"wk2{i}", [128, KC, 128], BF16) for i in range(2)]
                    wv2_ = [SB(st, f"wv2{i}", [128, KC, 128], BF16) for i in range(2)]

                    def ld_f(p):
                        i = p % 2
                        P.dma("pool", wq2_[i][:], w_in_v[:, :, O_FQ + 128 * p:O_FQ + 128 * p + 128], f"ld_wq2{i}", (), [f"wq2{i}"])
                        P.dma("pool", wk2_[i][:], w_in_v[:, :, O_FK + 128 * p:O_FK + 128 * p + 128], f"ld_wk2{i}", (), [f"wk2{i}"])
                        P.dma("pool", wv2_[i][:], w_in_v[:, :, O_FV + 128 * p:O_FV + 128 * p + 128], f"ld_wv2{i}", (), [f"wv2{i}"])
                    ld_f(0)
                    sq = [SB(st, f"sq{i}", [128, 512], BF16) for i in range(2)]
                    rs = [SB(st, f"rs{i}", [128, 512], F32) for i in range(2)]
                    pT = [SB(st, f"pT{i}", [128, 512], BF16) for i in range(4)]
                    rden = [SB(st, f"rden{i}", [128, 512], F32) for i in range(2)]
                    for s_ in range(2):
                        for i in range(2):
                            P.op("dve", I("memset", qaug[s_][i][64:128, :], 0.0), (), [f"qaug{s_}{i}"])
                            P.op("dve", I("memset", kaug[s_][i][64:128, :], 0.0), (), [f"kaug{s_}{i}"])
                            P.op("dve", I("memset", qaug[s_][i][64:70, :], 1.0), (), [f"qaug{s_}{i}"])
                            P.op("dve", I("memset", kaug[s_][i][64:70, :], -1.0), (), [f"kaug{s_}{i}"])
                        P.op("dve", I("memset", vaug[s_][:, :, 64:128], 1.0), (), [f"vaug{s_}"])
                    cnt3 = {"sq": 0, "pj": 0}

                    def fox_proj(p):
                        for _ in fox_proj_gen(p):
                            pass

                    def fox_proj_gen(p):
                        s_ = p % 2
                        wq2, wk2, wv2 = wq2_[p % 2], wk2_[p % 2], wv2_[p % 2]
                        if p + 1 < 8:
                            ld_f(p + 1)
                        sched = {}

                        def at(tick, fn):
                            sched.setdefault(tick, []).append(fn)
                        u = 0
                        for (wt, wkey, aug, akey, wcol) in ((wq2, f"wq2{p % 2}", qaug, "qaug", fqw_s), (wk2, f"wk2{p % 2}", kaug, "kaug", fkw_s)):
                            for t in range(4):
                                ts_ = slice(t * 512, (t + 1) * 512)
                                pb = 5 + u % 2
                                i2 = u % 2
                                t0 = 8 * u

                                def s0c(ch, wt=wt, wkey=wkey, ts_=ts_, pb=pb, t=t):
                                    P.op("pe", [I("matmul", bk[pb][:], lhsT=wt[:, k, 0:128], rhs=hT[:, k, ts_], start=(k == 0), stop=(k == KC - 1))
                                                for k in (2 * ch, 2 * ch + 1)], [wkey] + hT_keys(t), [BK[pb]])

                                def s1(pb=pb, i2=i2):
                                    act(sq[i2][:], bk[pb][:], AF.Square, [BK[pb]], [f"sq{i2}"])

                                def s2(i2=i2):
                                    mm(bk[7][:], [(bdones[:], sq[i2][:])], ["bdones", f"sq{i2}"], [BK[7]])

                                def s3(i2=i2):
                                    act(rs[i2][:], bk[7][:], AF.Ln, [BK[7]], [f"rs{i2}"], bias=EPS, scale=1.0 / 64)
                                    act(rs[i2][:], rs[i2][:], AF.Exp, [f"rs{i2}"], [f"rs{i2}"], scale=-0.5)

                                def s4(aug=aug, akey=akey, wcol=wcol, ts_=ts_, pb=pb, i2=i2):
                                    for hh in range(2):
                                        pr = slice(hh * 64, hh * 64 + 64)
                                        P.op("dve", I("scalar_tensor_tensor", out=aug[s_][hh][0:64, ts_], in0=bk[pb][pr, :], scalar=wcol[pr, 0:1],
                                                      in1=rs[i2][pr, :], op0=ALU.mult, op1=ALU.mult),
                                             [BK[pb], f"rs{i2}", "fqw_s", "fkw_s"], [f"{akey}{s_}{hh}"])
                                for ch in range(4):
                                    at(t0 + ch, (lambda ch=ch, f=s0c: f(ch)))
                                for si, fn in enumerate((s1, s2, s3, s4)):
                                    at(t0 + 6 + 3 * si, fn)
                                u += 1
                        for g4, t0 in enumerate((64, 72, 76, 80)):
                            pb = 5 + g4 % 2

                            def v0(jj, g4=g4, pb=pb):
                                j = g4 * 4 + jj
                                mm(bk[pb][:, jj * 128:(jj + 1) * 128], [(hT[:, k, j * 128:(j + 1) * 128], wv2[:, k, 0:128]) for k in range(KC)],
                                   [f"wv2{p % 2}", f"hT{j}"], [BK[pb]])

                            def v1(g4=g4, pb=pb):
                                pv = bk[pb][:].rearrange("p (j c) -> p j c", c=128)
                                act(vaug[s_][:, g4 * 4:g4 * 4 + 4, 0:64], pv[:, :, 0:64], AF.Identity, [BK[pb]], [f"vaug{s_}"])
                                act(vaug[s_][:, g4 * 4:g4 * 4 + 4, 128:192], pv[:, :, 64:128], AF.Identity, [BK[pb]], [f"vaug{s_}"])
                            for jj in range(4):
                                at(t0 + jj, (lambda jj=jj, f=v0: f(jj)))
                            at(t0 + 6, v1)

                        def dmas():
                            for hh in range(2):
                                h = 2 * p + hh
                                for r3 in range(3):
                                    P.dma("sp", kaug[s_][hh][64 + r3:65 + r3, :], npscr[4 + h:5 + h, r3 * S:(r3 + 1) * S], f"ld_ka{s_}{hh}", ["npscr"], [f"kaug{s_}{hh}"])
                                    P.dma("sp", qaug[s_][hh][67 + r3:68 + r3, :], npscr[4 + h:5 + h, r3 * S:(r3 + 1) * S], f"ld_qa{s_}{hh}", ["npscr"], [f"qaug{s_}{hh}"])
                        at(1, dmas)
                        for tick in range(max(sched) + 1):
                            for fn in sched.get(tick, ()):
                                fn()
                            yield

                    cnt_a = {"s": 0, "p": 0, "o": 0}

                    def fox_att(p, hh, gen=None):
                        s_ = p % 2
                        qa, ka, va = qaug[s_][hh], kaug[s_][hh], vaug[s_]
                        qk, kk, vk = f"qaug{s_}{hh}", f"kaug{s_}{hh}", f"vaug{s_}"
                        vlo = 0 if hh == 0 else 64
                        ol = 0 if hh == 0 else 64
                        dl = 64 if hh == 0 else 0
                        for g in range(4):
                            ob = 3 + cnt_a["o"] % 2
                            cnt_a["o"] += 1
                            nkb = 4 * g + 4
                            pend = []

                            def score(i):
                                r = i - 4 * g
                                qo = 128 * r if r >= 0 else 0
                                sb_ = cnt_a["s"] % 3
                                cnt_a["s"] += 1
                                pairs = [(ka[:, i * 128:(i + 1) * 128], qa[:, 512 * g + qo:512 * g + 512])]
                                fns = [I("matmul", bk[sb_][:, qo:512], lhsT=pairs[0][0], rhs=pairs[0][1], start=True, stop=(r < 0))]
                                if r >= 0:
                                    fns.append(I("matmul", bk[sb_][:, qo:qo + 128], lhsT=ident_bf[:], rhs=maskb[:], start=False, stop=True))
                                P.op("pe", fns, [kk, qk, "ident_bf", "maskb"], [BK[sb_]])
                                return (i, qo, sb_)

                            def expo(i, qo, sb_):
                                pi = cnt_a["p"] % 4
                                cnt_a["p"] += 1
                                act(pT[pi][:, qo:512], bk[sb_][:, qo:512], AF.Exp, [BK[sb_]], [f"pT{pi}"])
                                return pi

                            def pv(i, qo, pi):
                                P.op("pe", I("matmul", bk[ob][:, qo:512], lhsT=va[:, i, vlo:vlo + 128], rhs=pT[pi][:, qo:512],
                                                                             start=(i == 0), stop=(i == nkb - 1)), [vk, f"pT{pi}"], [BK[ob]])
                            scq = [score(i) for i in range(min(2, nkb))]
                            for i in range(nkb):
                                if i + 2 < nkb:
                                    scq.append(score(i + 2))
                                sc0 = scq.pop(0)
                                pi = expo(*sc0)
                                pv(sc0[0], sc0[1], pi)
                                if gen is not None:
                                    next(gen, None)
                            rd = rden[g % 2]
                            P.op("dve", I("reciprocal", out=rd[dl:dl + 64, :], in_=bk[ob][dl:dl + 64, :]), [BK[ob]], [f"rden{g % 2}"])
                            P.op("dve", I("tensor_tensor", out=hfT[ol:ol + 64, p, g * 512:(g + 1) * 512], in0=bk[ob][ol:ol + 64, :],
                                                                         in1=rd[dl:dl + 64, :], op=ALU.mult), [BK[ob], f"rden{g % 2}"], ["hfT"])

                    fox_proj(0)
                    for p in range(8):
                        if fox_limit <= 2 * p:
                            break
                        gen = fox_proj_gen(p + 1) if p + 1 < 8 else None
                        fox_att(p, 0, gen)
                        if fox_limit <= 2 * p + 1:
                            break
                        fox_att(p, 1, gen)
                        if gen is not None:
                            for _ in gen:
                                pass
                    P.barrier()
                st3b.close()

                with contextlib.ExitStack() as st:
                    wo = [SB(st, f"wo{i}", [128, KC, 512], BF16) for i in range(2)]
                    sg = [SB(st, f"sg{i}", [128, 512], F32) for i in range(2)]
                    n4 = 0
                    for half in range(2):
                        P.dma("pool", wo[half][:], w_in_v[:, :, O_MO + 512 * half:O_MO + 512 * half + 512], f"ld_wo{half}", (), [f"wo{half}"])
                    for c in range(KC):
                        half, cl = c // 4, (c % 4) * 128
                        for t in range(4):
                            ts_ = slice(t * 512, (t + 1) * 512)
                            pb = n4 % 4
                            i2 = n4 % 2
                            n4 += 1
                            mm(bk[pb][:], [(wo[half][:, k, cl:cl + 128], hT[:, k, ts_]) for k in range(KC)], [f"wo{half}"] + hT_keys(t), [BK[pb]])
                            act(sg[i2][:], bk[pb][:], AF.Sigmoid, [BK[pb]], [f"sg{i2}"])
                            P.op("dve", I("scalar_tensor_tensor", out=hmT[:, c, ts_], in0=hmT[:, c, ts_], scalar=mnw[:, c:c + 1],
                                                                                             in1=sg[i2][:], op0=ALU.mult, op1=ALU.mult),
                                 [f"hmTc{c}", "mnw", f"sg{i2}"], [f"hmTc{c}"])
                    P.barrier()

                st5 = contextlib.ExitStack()
                mgT = SB(st5, "mgT", [128, KC, S], BF16)
                wot = SB(st5, "wot", [128, KC, D], BF16)
                with contextlib.ExitStack() as st:
                    wsl = {nm: [SB(st, f"w5{nm}{i}", [128, KC, 256], BF16) for i in range(2)] for nm in ("bm", "bf", "ga", "gb")}
                    sga = [SB(st, "sga0", [128, 512], F32)] * 2
                    sgb = [SB(st, "sgb0", [128, 512], F32)] * 2
                    t1 = [SB(st, f"t1{i}", [128, 512], F32) for i in range(2)]
                    t2 = [SB(st, f"t2{i}", [128, 512], F32) for i in range(2)]
                    srcs = {"bm": (w_bm_v, 0), "bf": (w_bf_v, 0), "ga": (w_in_v, O_GA), "gb": (w_in_v, O_GB)}
                    n5 = 0
                    def ld_5(c2):
                        sl = c2 % 2
                        for nm in ("bm", "bf", "ga", "gb"):
                            src, off = srcs[nm]
                            P.dma("pool", wsl[nm][sl][:], src[:, :, off + 256 * c2:off + 256 * c2 + 256], f"ld_w5{nm}{sl}", (), [f"w5{nm}{sl}"])
                    ld_5(0)
                    for c2 in range(4):
                        sl = c2 % 2
                        if c2 + 1 < 4:
                            ld_5(c2 + 1)
                        else:
                            for half in range(2):
                                P.dma("pool", wot[:, :, half * 512:(half + 1) * 512], w_out_v[:, :, half * 512:(half + 1) * 512], "ld_wot", (), ["wot"])
                        for cc in range(2):
                            c = c2 * 2 + cc
                            cl = cc * 128
                            for t in range(4):
                                ts_ = slice(t * 512, (t + 1) * 512)
                                i2 = n5 % 2
                                pb0 = (n5 % 2) * 4
                                n5 += 1
                                mm(bk[pb0][:], [(wsl["bm"][sl][:, k, cl:cl + 128], hmT[:, k, ts_]) for k in range(KC)], [f"w5bm{sl}", "hmT"], [BK[pb0]])
                                mm(bk[pb0 + 1][:], [(wsl["bf"][sl][:, k, cl:cl + 128], hfT[:, k, ts_]) for k in range(KC)], [f"w5bf{sl}", "hfT"], [BK[pb0 + 1]])
                                mm(bk[pb0 + 2][:], [(wsl["ga"][sl][:, k, cl:cl + 128], hT[:, k, ts_]) for k in range(KC)], [f"w5ga{sl}"] + hT_keys(t), [BK[pb0 + 2]])
                                mm(bk[pb0 + 3][:], [(wsl["gb"][sl][:, k, cl:cl + 128], hT[:, k, ts_]) for k in range(KC)], [f"w5gb{sl}"] + hT_keys(t), [BK[pb0 + 3]])
                                act(sga[i2][:], bk[pb0 + 2][:], AF.Sigmoid, [BK[pb0 + 2]], ["sga"])
                                act(sgb[i2][:], bk[pb0 + 3][:], AF.Sigmoid, [BK[pb0 + 3]], ["sgb"])
                                P.op("dve", I("tensor_tensor", out=t1[i2][:], in0=bk[pb0][:], in1=sga[i2][:], op=ALU.mult),
                                     [BK[pb0], "sga"], [f"t1{i2}"])
                                P.op("dve", I("tensor_tensor", out=t2[i2][:], in0=bk[pb0 + 1][:], in1=sgb[i2][:], op=ALU.mult),
                                     [BK[pb0 + 1], "sgb"], [f"t2{i2}"])
                                P.op("dve", I("tensor_tensor", out=mgT[:, c, ts_], in0=t1[i2][:], in1=t2[i2][:], op=ALU.add),
                                     [f"t1{i2}", f"t2{i2}"], ["mgT"])
                    P.barrier()

                with contextlib.ExitStack() as st:
                    g1bc = SB(st, "g1bc", [128, D], F32)
                    xs = [SB(st, f"xr{i}", [128, D], F32) for i in range(2)]
                    x1 = [SB(st, f"x1{i}", [128, D], F32) for i in range(3)]
                    build_gate_bc(g1bc, "g1bc", b, 0)

                    def pre2(j):
                        sl = j % 2
                        P.dma("sp", xs[sl][:], x[b, j * 128:(j + 1) * 128, :], f"ld_xr{sl}", (), [f"xr{sl}"])
                        for hf in range(2):
                            pb = 2 * sl + hf
                            cs = slice(hf * 512, (hf + 1) * 512)
                            mm(bk[pb][:], [(mgT[:, k, j * 128:(j + 1) * 128], wot[:, k, cs]) for k in range(KC)], ["mgT", "wot"], [BK[pb]])

                    def fin2(j):
                        sl = j % 2
                        for hf in range(2):
                            pb = 2 * sl + hf
                            cs = slice(hf * 512, (hf + 1) * 512)
                            P.op("dve", I("tensor_tensor", out=x1[j % 3][:, cs], in0=bk[pb][:], in1=g1bc[:, cs], op=ALU.mult),
                                 [BK[pb], "g1bc"], [f"x1{j % 3}"])
                        P.op("dve", I("tensor_tensor", out=x1[j % 3][:], in0=x1[j % 3][:], in1=xs[sl][:], op=ALU.add), [f"x1{j % 3}", f"xr{sl}"], [f"x1{j % 3}"])
                        P.dma("sp", out[b, j * 128:(j + 1) * 128, :], x1[j % 3][:], f"st_x1{j % 3}", [f"x1{j % 3}"], [f"out{j}"])
                        return x1[j % 3][:], [f"x1{j % 3}"]
                    blk2 = norm_to_hT(st, b, pre2, fin2, a2, 24, "b")
                    for j in range(NB + 1):
                        blk2(j)
                    P.barrier()
                st5.close()
            if stop_after <= 5:
                continue

            with contextlib.ExitStack() as st:
                wdn = SB(st, "wdn", [128, NF, D], BF16)
                g2bc = SB(st, "g2bc", [128, D], F32)
                gT = SB(st, "gT", [128, NF, 1024], BF16)
                wug = [SB(st, f"wug{i}", [128, KC, 256], BF16) for i in range(2)]
                wuv = [SB(st, f"wuv{i}", [128, KC, 256], BF16) for i in range(2)]
                ug = [SB(st, f"ug{i}", [128, 1026], F32) for i in range(2)]
                uv = [SB(st, f"uv{i}", [128, 1026], F32) for i in range(2)]
                cg = [SB(st, f"cg{i}", [128, 512], F32) for i in range(3)]
                cv = [SB(st, f"cv{i}", [128, 512], F32) for i in range(3)]
                hal = SB(st, "hal", [128, 44, 2], F32)
                xr = [SB(st, f"xq{i}", [128, D], F32) for i in range(2)]
                ot = [SB(st, f"ot{i}", [128, D], F32) for i in range(2)]
                def ld_u(u):
                    f2_, sl_ = u % (NF // 2), u % 2
                    P.dma("pool", wug[sl_][:], w_up_v[:, :, 256 * f2_:256 * f2_ + 256], f"ld_wug{sl_}", (), [f"wug{sl_}"])
                    P.dma("pool", wuv[sl_][:], w_up_v[:, :, FF + 256 * f2_:FF + 256 * f2_ + 256], f"ld_wuv{sl_}", (), [f"wuv{sl_}"])
                ld_u(0)
                for f4 in range(0, NF, 4):
                    n_ = min(4, NF - f4)
                    P.dma("pool", wdn[:, f4:f4 + n_, :], w_down_v[:, f4:f4 + n_, :], "ld_wdn", (), ["wdn"])
                build_gate_bc(g2bc, "g2bc", b, 1)
                P.op("dve", I("memset", hal[:], 0.0), (), ["hal"])
                n6 = 0
                pend6 = []
                nu6 = [0]
                for half in range(2):
                    for f2 in range(NF // 2):
                        u_ = half * (NF // 2) + f2
                        sl = u_ % 2
                        if u_ + 1 < NF:
                            ld_u(u_ + 1)
                        for ff_ in range(2):
                            f = f2 * 2 + ff_
                            fl = ff_ * 128
                            us = n6 % 2
                            n6 += 1
                            for (ubuf, ukey, hidx) in ((ug[us], f"ug{us}", f), (uv[us], f"uv{us}", 22 + f)):
                                act(ubuf[:, 0:2], hal[:, hidx, :], AF.Identity, ["hal"], [ukey])
                            for tt in range(2):
                                t = half * 2 + tt
                                ts_ = slice(t * 512, (t + 1) * 512)
                                i2 = nu6[0] % 3
                                pbg, pbv = 2 * i2, 2 * i2 + 1
                                nu6[0] += 1
                                mm(bk[pbg][:], [(wug[sl][:, k, fl:fl + 128], hT[:, k, ts_]) for k in range(KC)], [f"wug{sl}"] + hT_keys(t), [BK[pbg]])
                                mm(bk[pbv][:], [(wuv[sl][:, k, fl:fl + 128], hT[:, k, ts_]) for k in range(KC)], [f"wuv{sl}"] + hT_keys(t), [BK[pbv]])
                                for (pb, ubuf, ukey, cbuf, ckey, widx) in ((pbg, ug[us], f"ug{us}", cg[i2], f"cg{i2}", f), (pbv, uv[us], f"uv{us}", cv[i2], f"cv{i2}", 22 + f)):
                                    u0 = 2 + tt * 512
                                    act(ubuf[:, u0:u0 + 512], bk[pb][:], AF.Identity, [BK[pb]], [ukey])
                                    act(cbuf[:], bk[pb][:], AF.Identity, [BK[pb], "convw", "convb"], [ckey], bias=convb[:, widx:widx + 1], scale=convw[:, widx, 2:3])
                                    P.op("dve", I("scalar_tensor_tensor",
                                        out=cbuf[:], in0=ubuf[:, u0 - 1:u0 + 511], scalar=convw[:, widx, 1:2], in1=cbuf[:], op0=ALU.mult, op1=ALU.add),
                                        [ukey, ckey, "convw"], [ckey])
                                    P.op("dve", I("scalar_tensor_tensor",
                                        out=cbuf[:], in0=ubuf[:, u0 - 2:u0 + 510], scalar=convw[:, widx, 0:1], in1=cbuf[:], op0=ALU.mult, op1=ALU.add),
                                        [ukey, ckey, "convw"], [ckey])
                                if pend6:
                                    pend6.pop()()

                                def fin(i2=i2, f=f, tt=tt):
                                    act(cg[i2][:], cg[i2][:], AF.Silu, [f"cg{i2}"], [f"cg{i2}"])
                                    P.op("dve", I("tensor_tensor", out=gT[:, f, tt * 512:(tt + 1) * 512], in0=cg[i2][:], in1=cv[i2][:], op=ALU.mult),
                                         [f"cg{i2}", f"cv{i2}"], ["gT"])
                                pend6.append(fin)
                            for (ubuf, ukey, hidx) in ((ug[us], f"ug{us}", f), (uv[us], f"uv{us}", 22 + f)):
                                act(hal[:, hidx, :], ubuf[:, 1024:1026], AF.Identity, [ukey], ["hal"])
                    if pend6:
                        pend6.pop()()
                    for jj in range(8):
                        j = half * 8 + jj
                        sl = jj % 2
                        P.dma("sp", xr[sl][:], out[b, j * 128:(j + 1) * 128, :], f"ld_xq{sl}", [f"out{j}"], [f"xq{sl}"])
                        for hf in range(2):
                            pb = 6 + hf
                            cs = slice(hf * 512, (hf + 1) * 512)
                            mm(bk[pb][:], [(gT[:, f, jj * 128:(jj + 1) * 128], wdn[:, f, cs]) for f in range(NF)], ["gT", "wdn"], [BK[pb]])
                            P.op("dve", I("tensor_tensor", out=ot[sl][:, cs], in0=bk[pb][:], in1=g2bc[:, cs], op=ALU.mult),
                                 [BK[pb], "g2bc"], [f"ot{sl}"])
                        P.op("dve", I("tensor_tensor", out=ot[sl][:], in0=ot[sl][:], in1=xr[sl][:], op=ALU.add), [f"ot{sl}", f"xq{sl}"], [f"ot{sl}"])
                        P.dma("sp", out[b, j * 128:(j + 1) * 128, :], ot[sl][:], f"st_ot{sl}", [f"ot{sl}"], [f"out{j}"])
                P.barrier()
        P.barrier()
        P.run()
    return nc


def _prep_inputs(inp, core):
    f = lambda a: np.ascontiguousarray(np.asarray(a, dtype=np.float32))
    b0 = 2 * core
    c = np.asarray(inp["c"], dtype=np.float32)[b0:b0 + 2]

    def pk(v, n):
        return f(np.asarray(v, dtype=np.float32).reshape(n, 128).T)
    conv_w = np.asarray(inp["conv_w"], dtype=np.float32)[0]
    m = {
        "x": f(np.asarray(inp["x"])[b0:b0 + 2]),
        "cT": f(c.T.reshape(KC, 128, 2).transpose(1, 0, 2)),
        "w_ada": f(inp["w_ada"][0]),
        "b_ada": f(inp["b_ada"][0]),
        "b_adaT": pk(inp["b_ada"][0], 48),
        "n1wT": pk(inp["norm1_w"][0], KC),
        "n2wT": pk(inp["norm2_w"][0], KC),
        "mnwT": pk(inp["mlstm_norm_w"][0], KC),
        "w_in": f(inp["w_in"][0]),
        "b_i": f(np.asarray(inp["b_mlstm_i"][0]).reshape(4, 1)),
        "b_f20": f(np.concatenate([np.asarray(inp["b_mlstm_f"][0]), np.asarray(inp["b_fox_f"][0])]).reshape(20, 1)),
        "fqw": f(np.tile(np.asarray(inp["fox_q_norm_w"][0]), 2).reshape(128, 1)),
        "fkw": f(np.tile(np.asarray(inp["fox_k_norm_w"][0]), 2).reshape(128, 1)),
        "w_bm": f(inp["w_branch_mlstm"][0]),
        "w_bf": f(inp["w_branch_fox"][0]),
        "w_out": f(inp["w_out"][0]),
        "w_up": f(inp["w_up"][0]),
        "convT": f(conv_w.reshape(3, 44, 128).transpose(2, 1, 0)),
        "convbT": pk(inp["conv_b"][0], 44),
        "w_down": f(inp["w_down"][0]),
    }
    return m


def kernel(**inputs):
    nc = build_program()
    in_maps = [_prep_inputs(inputs, i) for i in range(NCORES)]
    res = run_bass_kernel_spmd(nc, in_maps, core_ids=list(range(NCORES)))
    return np.concatenate([np.asarray(r["out"]) for r in res.results], axis=0).astype(np.float32)
```

```python
import contextlib
import os
import numpy as np
import concourse.bass as bass
import concourse.mybir as mybir
from concourse.bass_utils import run_bass_kernel_spmd

F32 = mybir.dt.float32
BF16 = mybir.dt.bfloat16
AF = mybir.ActivationFunctionType
ALU = mybir.AluOpType

ENGS = ("pe", "act", "dve", "pool", "sp")
NCORES = 8
S = 2048
NB = 16
D = 1024
KC = 8
FF = 2816
NF = 22
EPS = 1e-6
O_MQ, O_MK, O_MV, O_MO, O_MI, O_MF = 0, 1024, 2048, 3072, 4096, 4100
O_FQ, O_FK, O_FV, O_FF, O_GA, O_GB = 4104, 5128, 6152, 7176, 7192, 8216


class Prog:
    def __init__(self, nc, stack):
        self.nc = nc
        self.q = {e: [] for e in ENGS}
        self.cnt = {}
        self.sem = {}
        self.stack = stack
        for e in ENGS:
            self.sem[e] = stack.enter_context(nc.semaphore("s_" + e))
            self.cnt[e] = 0
        self.waited = {e: {} for e in ENGS}
        self.last_w = {}
        self.readers = {}

    def dma_sem(self, key):
        if key not in self.sem:
            self.sem[key] = self.stack.enter_context(self.nc.semaphore("d_" + key))
            self.cnt[key] = 0
        return key

    def _deps(self, eng, reads, writes):
        deps = {}

        def add(d):
            if d is None:
                return
            k, v = d
            if deps.get(k, 0) < v:
                deps[k] = v

        for r in reads:
            add(self.last_w.get(r))
            if r.startswith("bk"):
                for k, v in self.readers.get(r, {}).items():
                    if k != eng:
                        add((k, v))
        for w in writes:
            add(self.last_w.get(w))
            for k, v in self.readers.get(w, {}).items():
                add((k, v))
        return deps

    def _emit_waits(self, eng, deps):
        for k, v in deps.items():
            if self.waited[eng].get(k, 0) >= v:
                continue
            self.waited[eng][k] = v
            sem = self.sem[k]
            self.q[eng].append(lambda e, sem=sem, v=v: e.wait_ge(sem, v))

    def _mark(self, key, val, reads, writes):
        for r in reads:
            self.readers.setdefault(r, {})[key] = val
        for w in writes:
            self.last_w[w] = (key, val)
            self.readers[w] = {}

    def op(self, eng, fns, reads=(), writes=()):
        if not isinstance(fns, (list, tuple)):
            fns = [fns]
        self._emit_waits(eng, self._deps(eng, reads, writes))
        sem = self.sem[eng]
        self.cnt[eng] += 1
        n = self.cnt[eng]
        for f in fns[:-1]:
            self.q[eng].append(f)
        last = fns[-1]
        self.q[eng].append(lambda e, last=last, sem=sem: last(e).then_inc(sem, 1))
        self._mark(eng, n, reads, writes)
        return n

    def dma(self, qeng, out, in_, semkey, reads=(), writes=()):
        self.dma_sem(semkey)
        self._emit_waits(qeng, self._deps(None, reads, writes))
        sem = self.sem[semkey]
        self.cnt[semkey] += 16
        v = self.cnt[semkey]
        self.q[qeng].append(lambda e, out=out, in_=in_, sem=sem: e.dma_start(out=out, in_=in_).then_inc(sem, 16))
        self._mark(semkey, v, reads, writes)
        return v

    def barrier(self):
        allk = {k: v for k, v in self.cnt.items() if v > 0}
        for e in ENGS:
            self._emit_waits(e, dict(allk))
        self.last_w = {}
        self.readers = {}

    def run(self):
        nc = self.nc
        with nc.Block() as block:
            @block.tensor
            def _(e):
                for f in self.q["pe"]:
                    f(e)

            @block.scalar
            def _(e):
                for f in self.q["act"]:
                    f(e)

            @block.vector
            def _(e):
                for f in self.q["dve"]:
                    f(e)

            @block.gpsimd
            def _(e):
                for f in self.q["pool"]:
                    f(e)

            @block.sync
            def _(e):
                for f in self.q["sp"]:
                    f(e)


def build_program(nseq=2, stop_after=99, dbg=None, fox_limit=99):
    nc = bass.Bass("TRN2", target_bir_lowering=False)

    def din(name, shape):
        return nc.dram_tensor(name, list(shape), F32, kind="ExternalInput").ap()

    x = din("x", [2, S, D])
    cT = din("cT", [128, KC, 2])
    w_ada = din("w_ada", [D, 6 * D])
    b_ada = din("b_ada", [6 * D])
    b_adaT = din("b_adaT", [128, 48])
    n1wT = din("n1wT", [128, KC])
    n2wT = din("n2wT", [128, KC])
    mnwT = din("mnwT", [128, KC])
    w_in = din("w_in", [D, 9240])
    b_i = din("b_i", [4, 1])
    b_f20 = din("b_f20", [20, 1])
    fqw = din("fqw", [128, 1])
    fkw = din("fkw", [128, 1])
    w_bm = din("w_bm", [D, D])
    w_bf = din("w_bf", [D, D])
    w_out = din("w_out", [D, D])
    w_up = din("w_up", [D, 2 * FF])
    convT = din("convT", [128, 44, 3])
    convbT = din("convbT", [128, 44])
    w_down = din("w_down", [FF, D])
    out = nc.dram_tensor("out", [2, S, D], F32, kind="ExternalOutput").ap()
    npscr = nc.dram_tensor("npscr", [20, 3 * S], BF16).ap()
    dbg_t = None
    if dbg is not None:
        dbg_t = nc.dram_tensor("dbg", list(dbg), F32, kind="ExternalOutput").ap()

    w_in_v = w_in.rearrange("(k p) n -> p k n", p=128)
    w_ada_v = w_ada.rearrange("(k p) n -> p k n", p=128)
    w_bm_v = w_bm.rearrange("(k p) n -> p k n", p=128)
    w_bf_v = w_bf.rearrange("(k p) n -> p k n", p=128)
    w_out_v = w_out.rearrange("(k p) n -> p k n", p=128)
    w_up_v = w_up.rearrange("(k p) n -> p k n", p=128)
    w_down_v = w_down.rearrange("(k p) n -> p k n", p=128)

    with contextlib.ExitStack() as st0:
        P = Prog(nc, st0)

        uid = [0]

        def SB(st, name, shape, dt):
            uid[0] += 1
            return st.enter_context(nc.sbuf_tensor(f"{name}_{uid[0]}", list(shape), dt))

        bk = [st0.enter_context(nc.psum_tensor(f"bk{i}", [128, 512], F32)) for i in range(8)]
        BK = [f"bk{i}" for i in range(8)]

        def bkb(i):
            return bk[i][:].bitcast(BF16)

        def I(name, *a, **kw):
            return lambda e: getattr(e, name)(*a, **kw)

        def mm(out_ap, pairs, reads, writes):
            n = len(pairs)
            fns = []
            for i, (l, r) in enumerate(pairs):
                fns.append(I("matmul", out_ap, lhsT=l, rhs=r, start=(i == 0), stop=(i == n - 1)))
            P.op("pe", fns, reads, writes)

        def act(out_ap, in_ap, func, reads, writes, bias=None, scale=None, accum_out=None):
            kw = {}
            if bias is not None:
                kw["bias"] = bias
            if scale is not None:
                kw["scale"] = scale
            if accum_out is not None:
                kw["accum_out"] = accum_out
            P.op("act", I("activation", out=out_ap, in_=in_ap, func=func, **kw), reads, writes)

        ident_bf = SB(st0, "ident_bf", [128, 128], BF16)
        ident_f = SB(st0, "ident_f", [128, 128], F32)
        maskb = SB(st0, "maskb", [128, 128], BF16)
        mask01 = SB(st0, "mask01", [128, 128], F32)
        bdones = SB(st0, "bdones", [128, 128], BF16)
        sel = SB(st0, "sel", [4, 512], F32)
        tmpc = SB(st0, "tmpc", [128, 512], F32)
        mask4 = SB(st0, "mask4", [128, 512], F32)

        P.op("pool", I("memset", tmpc[:], 1.0), (), ["tmpc"])
        P.op("pool", I("affine_select", out=ident_f[:], in_=tmpc[:, 0:128], pattern=[[-1, 128]], compare_op=ALU.is_equal,
                                               fill=0.0, base=0, channel_multiplier=1), ["tmpc"], ["ident_f"])
        P.op("pool", I("tensor_copy", out=ident_bf[:], in_=ident_f[:]), ["ident_f"], ["ident_bf"])
        P.op("pool", I("affine_select", out=mask01[:], in_=tmpc[:, 0:128], pattern=[[1, 128]], compare_op=ALU.is_ge,
                                               fill=0.0, base=0, channel_multiplier=-1), ["tmpc"], ["mask01"])
        for i4 in range(4):
            P.op("pool", I("tensor_copy", out=mask4[:, i4 * 128:(i4 + 1) * 128], in_=mask01[:]), ["mask01"], ["mask4"])
        P.op("pool", I("tensor_scalar", out=maskb[:], in0=mask01[:], scalar1=-1.0, scalar2=30000.0,
                                               op0=ALU.add, op1=ALU.mult), ["mask01"], ["maskb"])
        P.op("pool", I("memset", bdones[:], 0.0), (), ["bdones"])
        P.op("pool", I("memset", bdones[0:64, 0:64], 1.0), (), ["bdones"])
        P.op("pool", I("memset", bdones[64:128, 64:128], 1.0), (), ["bdones"])
        P.op("pool", I("affine_select", out=sel[:].rearrange("p (h j) -> p h j", j=128),
                                               in_=tmpc[0:4, :].rearrange("p (h j) -> p h j", j=128),
                                               pattern=[[-1, 4], [0, 128]], compare_op=ALU.is_equal,
                                               fill=0.0, base=0, channel_multiplier=1), ["tmpc"], ["sel"])

        cT_sb = SB(st0, "cT_sb", [128, KC, 2], F32)
        sc = SB(st0, "sc", [128, KC, 2], BF16)
        badaT = SB(st0, "badaT", [128, 48], F32)
        n1w = SB(st0, "n1w", [128, KC], F32)
        n2w = SB(st0, "n2w", [128, KC], F32)
        mnw = SB(st0, "mnw", [128, KC], F32)
        bi = SB(st0, "bi", [4, 1], F32)
        nbf20 = SB(st0, "nbf20", [20, 1], F32)
        fqw_s = SB(st0, "fqw_s", [128, 1], F32)
        fkw_s = SB(st0, "fkw_s", [128, 1], F32)
        convw = SB(st0, "convw", [128, 44, 3], F32)
        convb = SB(st0, "convb", [128, 44], F32)
        adaT = SB(st0, "adaT", [128, 48, 2], F32)
        a1 = SB(st0, "a1", [128, KC, 2], F32)
        a2 = SB(st0, "a2", [128, KC, 2], F32)
        grow = SB(st0, "grow", [2, 2048], F32)
        hT = SB(st0, "hT", [128, KC, S], BF16)

        for (t_, src, key) in [(cT_sb, cT, "cT_sb"), (badaT, b_adaT, "badaT"), (n1w, n1wT, "n1w"), (n2w, n2wT, "n2w"),
                               (mnw, mnwT, "mnw"), (bi, b_i, "bi"), (nbf20, b_f20, "nbf20"), (fqw_s, fqw, "fqw_s"),
                               (fkw_s, fkw, "fkw_s"), (convw, convT, "convw"), (convb, convbT, "convb")]:
            P.dma("sp", t_[:], src, "ld_" + key, (), [key])
        P.op("dve", I("tensor_scalar", out=nbf20[:], in0=nbf20[:], scalar1=-1.0, scalar2=None, op0=ALU.mult), ["nbf20"], ["nbf20"])
        P.op("dve", I("tensor_scalar", out=fqw_s[:], in0=fqw_s[:], scalar1=0.125, scalar2=None, op0=ALU.mult), ["fqw_s"], ["fqw_s"])

        st_ada = contextlib.ExitStack()
        bgate = SB(st_ada, "bgate", [2, 2048], F32)
        wa = [SB(st_ada, f"wa{i}", [128, KC, 1536], BF16) for i in range(2)]
        for r in range(2):
            P.dma("sp", bgate[r:r + 1, 0:1024], b_ada[2048:3072].rearrange("(o n) -> o n", o=1), "ld_bg", (), ["bgate"])
            P.dma("sp", bgate[r:r + 1, 1024:2048], b_ada[5120:6144].rearrange("(o n) -> o n", o=1), "ld_bg", (), ["bgate"])
        act(sc[:], cT_sb[:], AF.Silu, ["cT_sb"], ["sc"])
        psv = bk[0][:, 0:96].rearrange("p (c b) -> p c b", b=2)

        def ada_load(pc):
            P.dma("pool", wa[pc % 2][:], w_ada_v[:, :, pc * 1536:(pc + 1) * 1536], f"ld_wa{pc % 2}", (), [f"wa{pc % 2}"])

        def ada_part(part):
            for pc in (2 * part, 2 * part + 1):
                sl = pc % 2
                for ch in range(12):
                    gch = pc * 12 + ch
                    mm(bk[0][:, gch * 2:gch * 2 + 2],
                       [(wa[sl][:, k, ch * 128:(ch + 1) * 128], sc[:, k, :]) for k in range(KC)],
                       [f"wa{sl}", "sc"], [BK[0]])
                if pc in (1, 3):
                    for hf in range(2):
                        bi_ = 1 + (pc // 2) * 2 + hf
                        mm(bk[bi_][0:2, :],
                           [(sc[:, k, :], wa[sl][:, k, 512 + hf * 512:1024 + hf * 512]) for k in range(KC)],
                           [f"wa{sl}", "sc"], [BK[bi_]])
            c0, c1 = 24 * part, 24 * part + 24
            for b in range(2):
                P.op("dve", I("tensor_tensor", out=adaT[:, c0:c1, b], in0=psv[:, c0:c1, b], in1=badaT[:, c0:c1], op=ALU.add),
                     [BK[0], "badaT"], ["adaT"])
            for b in range(2):
                if part == 0:
                    P.op("dve", I("scalar_tensor_tensor", out=a1[:, :, b], in0=adaT[:, 8:16, b], scalar=1.0, in1=n1w[:],
                                  op0=ALU.add, op1=ALU.mult), ["adaT", "n1w"], ["a1"])
                else:
                    P.op("dve", I("scalar_tensor_tensor", out=a2[:, :, b], in0=adaT[:, 32:40, b], scalar=1.0, in1=n2w[:],
                                  op0=ALU.add, op1=ALU.mult), ["adaT", "n2w"], ["a2"])
            for i in (2 * part, 2 * part + 1):
                P.op("dve", I("tensor_tensor", out=grow[:, i * 512:(i + 1) * 512], in0=bk[1 + i][0:2, :],
                              in1=bgate[:, i * 512:(i + 1) * 512], op=ALU.add), [BK[1 + i], "bgate"], ["grow"])
            P.barrier()
        ada_load(0)
        ada_load(1)
        ada_part(0)
        ada_load(2)
        ada_load(3)

        def build_gate_bc(gbc, key, b, g):
            for hf in range(2):
                mm(bk[6 + hf][:], [(sel[0:2, b * 128:(b + 1) * 128], grow[0:2, g * 1024 + hf * 512:g * 1024 + (hf + 1) * 512])],
                   ["sel", "grow"], [BK[6 + hf]])
                P.op("dve", I("tensor_copy", out=gbc[:, hf * 512:(hf + 1) * 512], in_=bk[6 + hf][:]),
                     [BK[6 + hf]], [key])

        def norm_to_hT(st, b, pre_fn, fin_fn, aX, shcol0, tag):
            xn = [SB(st, f"xn{tag}{i}", [128, D], BF16) for i in range(2)]
            junk = SB(st, f"junk{tag}", [128, D], BF16)
            stt = [SB(st, f"stt{tag}{i}", [128, 4], F32) for i in range(2)]
            held = {}

            def do_block(j):
                if j == 0:
                    pre_fn(0)
                if j + 1 < NB:
                    pre_fn(j + 1)
                if j < NB:
                    xs_ap, xs_keys = fin_fn(j)
                    s2 = j % 2
                    act(junk[:], xs_ap, AF.Square, xs_keys, [f"junk{tag}", f"stt{tag}{s2}"], accum_out=stt[s2][:, 0:1])
                    act(stt[s2][:, 1:2], stt[s2][:, 0:1], AF.Ln, [f"stt{tag}{s2}"], [f"stt{tag}{s2}"], bias=EPS, scale=1.0 / D)
                    act(stt[s2][:, 2:3], stt[s2][:, 1:2], AF.Exp, [f"stt{tag}{s2}"], [f"stt{tag}{s2}"], scale=-0.5)
                    P.op("dve", I("tensor_scalar", out=xn[s2][:], in0=xs_ap, scalar1=stt[s2][:, 2:3], scalar2=None, op0=ALU.mult),
                         xs_keys + [f"stt{tag}{s2}"], [f"xn{tag}{s2}"])
                    for hf in range(2):
                        tb = 4 + 2 * hf + s2
                        tp = bkb(tb)
                        P.op("pe", [I("transpose", out=tp[:, c * 128:(c + 1) * 128], in_=xn[s2][:, (4 * hf + c) * 128:(4 * hf + c + 1) * 128],
                                      identity=ident_bf[:]) for c in range(4)], [f"xn{tag}{s2}", "ident_bf"], [BK[tb]])
                if j >= 1:
                    jb = j - 1
                    s2 = jb % 2
                    tpa, tpb = bkb(4 + s2), bkb(6 + s2)
                    for c in range(4):
                        act(hT[:, c, jb * 128:(jb + 1) * 128], tpa[:, c * 128:(c + 1) * 128], AF.Identity,
                            [BK[4 + s2]], [f"hT{jb}"], bias=adaT[:, shcol0 + c, b:b + 1], scale=aX[:, c, b:b + 1])
                    for c in range(4, 8):
                        P.op("dve", I("tensor_scalar", out=hT[:, c, jb * 128:(jb + 1) * 128], in0=tpb[:, (c - 4) * 128:(c - 3) * 128],
                                      scalar1=aX[:, c, b:b + 1], scalar2=adaT[:, shcol0 + c, b:b + 1], op0=ALU.mult, op1=ALU.add),
                             [BK[6 + s2]], [f"hT{jb}"])
            return do_block

        hT_all = [f"hT{j}" for j in range(NB)]

        def hT_keys(t):
            return [f"hT{j}" for j in range(4 * t, 4 * t + 4)]

        for b in range(nseq):
            with contextlib.ExitStack() as st:
                xs = [SB(st, f"xs{i}", [128, D], F32) for i in range(3)]

                def ldx1(j):
                    P.dma("sp", xs[j % 3][:], x[b, j * 128:(j + 1) * 128, :], f"ld_xs{j % 3}", (), [f"xs{j % 3}"])

                def fin1(j):
                    return xs[j % 3][:], [f"xs{j % 3}"]
                blk1 = norm_to_hT(st, b, ldx1, fin1, a1, 0, "a")
                for j in range(NB + 1):
                    blk1(j)
                P.barrier()
            if b == 0:
                ada_part(1)
                st_ada.close()
            if stop_after <= 1:
                continue

            def gates(want_m, want_f, Grow=None, A_tm=None, X_tm=None, NP_=None):
                with contextlib.ExitStack() as st:
                    wi = SB(st, "wi", [128, KC, 4], BF16)
                    wf = SB(st, "wf", [128, KC, 20], BF16)
                    ipre = SB(st, "ipre", [4, S], F32)
                    spl = SB(st, "spl", [20, S], F32)
                    negF = SB(st, "negF", [20, S], F32)
                    P.dma("pool", wi[:], w_in_v[:, :, O_MI:O_MI + 4], "ld_wi", (), ["wi"])
                    P.dma("pool", wf[:, :, 0:4], w_in_v[:, :, O_MF:O_MF + 4], "ld_wf", (), ["wf"])
                    P.dma("pool", wf[:, :, 4:20], w_in_v[:, :, O_FF:O_FF + 16], "ld_wf", (), ["wf"])
                    for t in range(4):
                        ts_ = slice(t * 512, (t + 1) * 512)
                        mm(bk[t % 2][0:4, :], [(wi[:, k, :], hT[:, k, ts_]) for k in range(KC)], ["wi"] + hT_keys(t), [BK[t % 2]])
                        mm(bk[2 + t % 2][0:20, :], [(wf[:, k, :], hT[:, k, ts_]) for k in range(KC)], ["wf"] + hT_keys(t), [BK[2 + t % 2]])
                        act(ipre[:, ts_], bk[t % 2][0:4, :], AF.Identity, [BK[t % 2], "bi"], ["ipre"], bias=bi[:, 0:1])
                        act(spl[:, ts_], bk[2 + t % 2][0:20, :], AF.Exp, [BK[2 + t % 2], "nbf20"], ["spl"], bias=nbf20[:, 0:1], scale=-1.0)
                    act(spl[:], spl[:], AF.Ln, ["spl"], ["spl"], bias=1.0)
                    P.op("dve", I("tensor_tensor_scan", out=negF[:], data0=spl[:], data1=spl[:], initial=0.0, op0=ALU.add, op1=ALU.max),
                         ["spl"], ["negF"])
                    if want_m:
                        Am = SB(st, "Am", [4, S], F32)
                        Xm = SB(st, "Xm", [4, S], F32)
                        P.op("dve", I("tensor_tensor", out=Am[:], in0=ipre[:], in1=negF[0:4, :], op=ALU.add), ["ipre", "negF"], ["Am"])
                        P.op("dve", I("memset", Grow[:, 0:1], 0.0), (), ["Grow"])
                        P.op("dve", I("tensor_tensor_scan", out=Grow[:, 1:S + 1], data0=Am[:], data1=Am[:], initial=0.0, op0=ALU.max, op1=ALU.max),
                             ["Am"], ["Grow"])
                        P.op("dve", I("tensor_tensor", out=Xm[:], in0=negF[0:4, :], in1=Grow[:, 1:S + 1], op=ALU.subtract), ["negF", "Grow"], ["Xm"])
                        P.op("pe", [I("transpose", out=bk[4][:, j * 4:(j + 1) * 4], in_=Am[0:4, j * 128:(j + 1) * 128],
                                                               identity=ident_f[0:4, 0:4]) for j in range(NB)], ["Am", "ident_f"], [BK[4]])
                        P.op("pe", [I("transpose", out=bk[5][:, j * 4:(j + 1) * 4], in_=Xm[0:4, j * 128:(j + 1) * 128],
                                                               identity=ident_f[0:4, 0:4]) for j in range(NB)], ["Xm", "ident_f"], [BK[5]])
                        P.op("dve", I("tensor_copy", out=A_tm[:], in_=bk[4][:, 0:64]), [BK[4]], ["A_tm"])
                        P.op("dve", I("tensor_copy", out=X_tm[:], in_=bk[5][:, 0:64]), [BK[5]], ["X_tm"])
                    if want_f:
                        r1 = SB(st, "r1", [20, S], F32)
                        NP_ = SB(st, "NP", [20, 3, S], BF16)
                        P.op("dve", I("tensor_copy", out=NP_[:, 0, :], in_=negF[:]), ["negF"], ["NP"])
                        P.op("dve", I("tensor_tensor", out=r1[:], in0=negF[:], in1=NP_[:, 0, :], op=ALU.subtract), ["negF", "NP"], ["r1"])
                        P.op("dve", I("tensor_copy", out=NP_[:, 1, :], in_=r1[:]), ["r1"], ["NP"])
                        P.op("dve", I("tensor_tensor", out=r1[:], in0=r1[:], in1=NP_[:, 1, :], op=ALU.subtract), ["r1", "NP"], ["r1"])
                        P.op("dve", I("tensor_copy", out=NP_[:, 2, :], in_=r1[:]), ["r1"], ["NP"])
                        P.dma("sp", npscr.rearrange("p (r t) -> p r t", r=3), NP_[:], "st_np", ["NP"], ["npscr"])
                    P.barrier()

            with contextlib.ExitStack() as st2:
                hmT = SB(st2, "hmT", [128, KC, S], BF16)
                st2b = contextlib.ExitStack()
                Grow = SB(st2b, "Grow", [4, S + 1], F32)
                A_tm = SB(st2b, "A_tm", [128, 64], F32)
                X_tm = SB(st2b, "X_tm", [128, 64], F32)
                gates(True, True, Grow=Grow, A_tm=A_tm, X_tm=X_tm)

                with contextlib.ExitStack() as st:
                    qT = SB(st, "qT", [128, 2, S], BF16)
                    kT = SB(st, "kT", [128, 2, S], BF16)
                    ktm = SB(st, "ktm", [128, NB, 256], BF16)
                    vA = SB(st, "vA", [128, NB, 258], BF16)
                    vwa = SB(st, "vwa", [128, NB, 258], BF16)
                    STb = SB(st, "STb", [128, S], BF16)
                    oS = SB(st, "oS", [128, NB, 256], BF16)
                    Cf = SB(st, "Cf", [128, 2, 257], F32)
                    Cb = [SB(st, f"Cb{i}", [128, 2, 258], BF16) for i in range(2)]
                    sta = SB(st, "sta", [128, 12, 16], F32)
                    hmtm = [SB(st, f"hmtm{i}", [128, 256], BF16) for i in range(2)]
                    junk4 = SB(st, "junk4", [128, 256], BF16)
                    wq_ = [SB(st, f"wq{i}", [128, KC, 256], BF16) for i in range(2)]
                    wk_ = [SB(st, f"wk{i}", [128, KC, 256], BF16) for i in range(2)]
                    wv_ = [SB(st, f"wv{i}", [128, KC, 256], BF16) for i in range(2)]
                    Gbc = SB(st, "Gbc", [128, S + 1], F32)
                    DTa = SB(st, "DTa", [128, S], F32)
                    Eba = SB(st, "Eba", [128, S], F32)
                    tmpS = [SB(st, f"tmpS{i}", [128, 512], F32) for i in range(2)]

                    def ld_m(h):
                        i = h % 2
                        P.dma("pool", wq_[i][:], w_in_v[:, :, O_MQ + 256 * h:O_MQ + 256 * h + 256], f"ld_wq{i}", (), [f"wq{i}"])
                        P.dma("pool", wk_[i][:], w_in_v[:, :, O_MK + 256 * h:O_MK + 256 * h + 256], f"ld_wk{i}", (), [f"wk{i}"])
                        P.dma("pool", wv_[i][:], w_in_v[:, :, O_MV + 256 * h:O_MV + 256 * h + 256], f"ld_wv{i}", (), [f"wv{i}"])
                    ld_m(0)
                    P.op("dve", I("memset", vA[:, :, 256:257], 1.0), (), [f"vA{j}" for j in range(NB)])
                    WALL, DCY, EXA, SS, ND, MDEN, RR, VAR, FAC, TMP, DEN, FACP = range(12)

                    def post_block(h, j):
                        s2 = j % 2
                        js = slice(j * 128, (j + 1) * 128)
                        tb = 2 + s2
                        tp = bkb(tb)
                        act(hmtm[s2][:], oS[:, j, :], AF.Identity, [f"oS{j}", "staFP"], [f"hmtm{s2}"], scale=sta[:, FACP, j:j + 1])
                        P.op("pe", [I("transpose", out=tp[:, c * 128:(c + 1) * 128], in_=hmtm[s2][:, c * 128:(c + 1) * 128],
                                      identity=ident_bf[:]) for c in range(2)], [f"hmtm{s2}", "ident_bf"], [BK[tb]])
                        P.op("dve", I("tensor_copy", out=hmT[:, 2 * h:2 * h + 2, js], in_=tp[:, 0:256].rearrange("p (c t) -> p c t", t=128)),
                             [BK[tb]], ["hmT"])

                    pending_post = []
                    for h in range(4):
                        wq, wk, wv = wq_[h % 2], wk_[h % 2], wv_[h % 2]
                        wqk, wkk, wvk = f"wq{h % 2}", f"wk{h % 2}", f"wv{h % 2}"
                        if h + 1 < 4:
                            ld_m(h + 1)
                        for t5 in range(5):
                            c0, c1 = t5 * 512, min(S + 1, (t5 + 1) * 512)
                            gb_ = 6 + t5 % 2
                            mm(bk[gb_][:, 0:c1 - c0], [(sel[0:4, h * 128:(h + 1) * 128], Grow[0:4, c0:c1])], ["sel", "Grow"], [BK[gb_]])
                            P.op("dve", I("tensor_copy", out=Gbc[:, c0:c1], in_=bk[gb_][:, 0:c1 - c0]), [BK[gb_]], ["Gbc"])
                        g0v = Gbc[:, 0:S].rearrange("p (j t) -> p j t", t=128)[:, :, 0]
                        gEv = Gbc[:, 1:S + 1].rearrange("p (j t) -> p j t", t=128)[:, :, 127]
                        Av = A_tm[:].rearrange("p (j h) -> p j h", h=4)[:, :, h]
                        Xv = X_tm[:].rearrange("p (j h) -> p j h", h=4)[:, :, h]
                        P.op("dve", I("tensor_tensor", out=sta[:, TMP, :], in0=Av, in1=gEv, op=ALU.subtract), ["A_tm", "Gbc"], ["staT"])
                        act(sta[:, WALL, :], sta[:, TMP, :], AF.Exp, ["staT"], ["staW"])
                        P.op("dve", I("tensor_tensor", out=sta[:, ND, :], in0=g0v, in1=gEv, op=ALU.subtract), ["Gbc"], ["staN"])
                        act(sta[:, DCY, :], sta[:, ND, :], AF.Exp, ["staN"], ["staD"])
                        act(sta[:, EXA, :], Xv, AF.Exp, ["X_tm"], ["staE"])
                        exq = []
                        for j in range(NB):
                            js = slice(j * 128, (j + 1) * 128)
                            jg = slice(1 + j * 128, 1 + (j + 1) * 128)
                            exq.append((DTa[:, js], Gbc[:, jg], ["Gbc", "A_tm"], [f"DTa{j // 4}"], A_tm[:, j * 4 + h:j * 4 + h + 1]))
                            exq.append((Eba[:, js], Gbc[:, jg], ["Gbc"], ["Eba"], Gbc[:, j * 128:j * 128 + 1]))
                        pj = 0
                        for (wt, wkey, dst, dkey, scl) in ((wq, wqk, qT, "qT", 1.0 / 16), (wk, wkk, kT, "kT", 1.0)):
                            for c in range(2):
                                for t in range(4):
                                    ts_ = slice(t * 512, (t + 1) * 512)
                                    pb = 4 + pj % 2
                                    mm(bk[pb][:], [(wt[:, k, c * 128:(c + 1) * 128], hT[:, k, ts_]) for k in range(KC)], [wkey] + hT_keys(t), [BK[pb]])
                                    act(dst[:, c, ts_], bk[pb][:], AF.Identity, [BK[pb]], [dkey], scale=scl)
                                    for _ in range(2):
                                        o_, i_, r_, w_, b_ = exq.pop(0)
                                        act(o_, i_, AF.Exp, r_, w_, bias=b_, scale=-1.0)
                                    if pending_post:
                                        post_block(*pending_post.pop(0))
                                    pj += 1
                        for g4 in range(4):
                            gs = slice(g4 * 512, (g4 + 1) * 512)
                            P.op("pool", I("tensor_tensor", out=DTa[:, gs], in0=DTa[:, gs], in1=mask4[:], op=ALU.mult),
                                 [f"DTa{g4}", "mask4"], [f"DTm{g4}", f"DTa{g4}"])
                        for j2 in range(NB // 2):
                            pb = 4 + pj % 2
                            pj += 1
                            for jj in range(2):
                                j = j2 * 2 + jj
                                mm(bk[pb][:, jj * 256:(jj + 1) * 256], [(hT[:, k, j * 128:(j + 1) * 128], wv[:, k, :]) for k in range(KC)],
                                   [wvk, f"hT{j}"], [BK[pb]])
                            act(vA[:, j2 * 2:j2 * 2 + 2, 0:256], bk[pb][:].rearrange("p (j c) -> p j c", c=256), AF.Identity, [BK[pb]],
                                [f"vA{j2 * 2}", f"vA{j2 * 2 + 1}"])
                            for jj in range(2):
                                j = j2 * 2 + jj
                                P.op("dve", I("tensor_scalar", out=vwa[:, j, 0:257], in0=vA[:, j, 0:257], scalar1=sta[:, WALL, j:j + 1], scalar2=None, op0=ALU.mult),
                                     [f"vA{j}", "staW"], [f"vwa{j}"])
                        for g4 in range(4):
                            tb = 6 + g4 % 2
                            tp = bkb(tb)
                            fns = []
                            for jj in range(4):
                                j = g4 * 4 + jj
                                for c in range(2):
                                    fns.append(I("transpose", out=tp[:, jj * 256 + c * 128:jj * 256 + (c + 1) * 128],
                                                 in_=kT[:, c, j * 128:(j + 1) * 128], identity=ident_bf[:]))
                            P.op("pe", fns, ["kT", "ident_bf"], [BK[tb]])
                            P.op("dve", I("tensor_copy", out=ktm[:, g4 * 4:g4 * 4 + 4, :], in_=tp.rearrange("p (j c) -> p j c", c=256)),
                                 [BK[tb]], ["ktm"])
                        while pending_post:
                            post_block(*pending_post.pop(0))
                        for g4 in range(4):
                            gs = slice(g4 * 512, (g4 + 1) * 512)
                            fns = []
                            for jj in range(4):
                                j = g4 * 4 + jj
                                js = slice(j * 128, (j + 1) * 128)
                                for c in range(2):
                                    fns.append(I("matmul", bk[g4][:, jj * 128:(jj + 1) * 128], lhsT=kT[:, c, js], rhs=qT[:, c, js],
                                                 start=(c == 0), stop=(c == 1)))
                            P.op("pe", fns, ["kT", "qT"], [BK[g4]])
                            P.op("dve", I("tensor_tensor", out=STb[:, gs], in0=bk[g4][:], in1=DTa[:, gs], op=ALU.mult),
                                 [BK[g4], f"DTm{g4}"], [f"STb{g4}"])
                        for c in range(2):
                            P.op("dve", I("tensor_tensor", out=qT[:, c, :], in0=qT[:, c, :], in1=Eba[:], op=ALU.mult), ["qT", "Eba"], ["qT"])
                        P.op("dve", I("memset", Cf[:], 0.0), (), ["Cf0", "Cf1"])
                        P.op("dve", I("memset", Cb[0][:], 0.0), (), ["Cb0a", "Cb0b"])

                        def upd(j):
                            for c in range(2):
                                ub = 4 + (j % 2) * 2 + c
                                mm(bk[ub][:, 0:257], [(ktm[:, j, c * 128:(c + 1) * 128], vwa[:, j, 0:257])], ["ktm", f"vwa{j}"], [BK[ub]])
                        upd(0)
                        for j in range(NB):
                            s2 = j % 2
                            js = slice(j * 128, (j + 1) * 128)
                            if j + 1 < NB:
                                upd(j + 1)
                            ob = s2
                            mm(bk[ob][:, 0:257], [(STb[:, js], vA[:, j, 0:257])] + [(qT[:, c, js], Cb[s2][:, c, 0:257]) for c in range(2)],
                               [f"STb{j // 4}", f"vA{j}", "qT", f"Cb{s2}a", f"Cb{s2}b"], [BK[ob]])
                            for c in range(2):
                                ub = 4 + s2 * 2 + c
                                P.op("dve", I("scalar_tensor_tensor", out=Cf[:, c, :], in0=Cf[:, c, :], scalar=sta[:, DCY, j:j + 1], in1=bk[ub][:, 0:257],
                                              op0=ALU.mult, op1=ALU.add), [f"Cf{c}", "staD", BK[ub]], [f"Cf{c}"])
                            if j + 1 < NB:
                                act(Cb[1 - s2][:, 0, 0:257], Cf[:, 0, :], AF.Identity, ["Cf0"], [f"Cb{1 - s2}a"])
                                P.op("dve", I("tensor_copy", out=Cb[1 - s2][:, 1, 0:257], in_=Cf[:, 1, :]), ["Cf1"], [f"Cb{1 - s2}b"])
                            act(junk4[:], bk[ob][:, 0:256], AF.Square, [BK[ob]], ["junk4", "staS"], accum_out=sta[:, SS, j:j + 1])
                            act(oS[:, j, :], bk[ob][:, 0:256], AF.Identity, [BK[ob]], [f"oS{j}"])
                            act(sta[:, DEN, j:j + 1], bk[ob][:, 256:257], AF.Identity, [BK[ob]], ["staDen"])
                        denv = sta[:, DEN, :]
                        P.op("dve", I("tensor_scalar", out=sta[:, ND, :], in0=denv, scalar1=-1.0, scalar2=None, op0=ALU.mult), ["staDen"], ["staN"])
                        P.op("dve", I("tensor_tensor", out=sta[:, MDEN, :], in0=sta[:, ND, :], in1=sta[:, EXA, :], op=ALU.max), ["staN", "staE"], ["staM"])
                        P.op("dve", I("tensor_tensor", out=sta[:, MDEN, :], in0=sta[:, MDEN, :], in1=denv, op=ALU.max), ["staM", "staDen"], ["staM"])
                        P.op("dve", I("reciprocal", out=sta[:, RR, :], in_=sta[:, MDEN, :]), ["staM"], ["staR"])
                        P.op("dve", I("tensor_tensor", out=sta[:, VAR, :], in0=sta[:, SS, :], in1=sta[:, RR, :], op=ALU.mult), ["staS", "staR"], ["staV"])
                        P.op("dve", I("tensor_tensor", out=sta[:, VAR, :], in0=sta[:, VAR, :], in1=sta[:, RR, :], op=ALU.mult), ["staV", "staR"], ["staV"])
                        act(sta[:, VAR, :], sta[:, VAR, :], AF.Ln, ["staV"], ["staV"], bias=EPS, scale=1.0 / 256)
                        act(sta[:, VAR, :], sta[:, VAR, :], AF.Exp, ["staV"], ["staV"], scale=-0.5)
                        P.op("dve", I("tensor_tensor", out=sta[:, FACP, :], in0=sta[:, VAR, :], in1=sta[:, RR, :], op=ALU.mult), ["staV", "staR"], ["staFP"])
                        for j in range(NB):
                            post_block(h, j)
                    while pending_post:
                        post_block(*pending_post.pop(0))
                    P.barrier()
                st2b.close()

                hfT = SB(st2, "hfT", [128, KC, S], BF16)
                st3b = contextlib.ExitStack()

                with contextlib.ExitStack() as st:
                    qaug = [[SB(st, f"qaug{s}{i}", [128, S], BF16) for i in range(2)] for s in range(2)]
                    kaug = [[SB(st, f"kaug{s}{i}", [128, S], BF16) for i in range(2)] for s in range(2)]
                    vaug = [SB(st, f"vaug{s}", [128, NB, 192], BF16) for s in range(2)]
                    wq2_ = [SB(st, f"wq2{i}", [128, KC, 128], BF16) for i in range(2)]
                    wk2_ = [SB(st, f"wk2{i}", [128, KC, 128], BF16) for i in range(2)]
                    wv2_ = [SB(st, f"wv2{i}", [128, KC, 128], BF16) for i in range(2)]

                    def ld_f(p):
                        i = p % 2
                        P.dma("pool", wq2_[i][:], w_in_v[:, :, O_FQ + 128 * p:O_FQ + 128 * p + 128], f"ld_wq2{i}", (), [f"wq2{i}"])
                        P.dma("pool", wk2_[i][:], w_in_v[:, :, O_FK + 128 * p:O_FK + 128 * p + 128], f"ld_wk2{i}", (), [f"wk2{i}"])
                        P.dma("pool", wv2_[i][:], w_in_v[:, :, O_FV + 128 * p:O_FV + 128 * p + 128], f"ld_wv2{i}", (), [f"wv2{i}"])
                    ld_f(0)
                    sq = [SB(st, f"sq{i}", [128, 512], BF16) for i in range(2)]
                    rs = [SB(st, f"rs{i}", [128, 512], F32) for i in range(2)]
                    pT = [SB(st, f"pT{i}", [128, 512], BF16) for i in range(4)]
                    rden = [SB(st, f"rden{i}", [128, 512], F32) for i in range(2)]
                    for s_ in range(2):
                        for i in range(2):
                            P.op("dve", I("memset", qaug[s_][i][64:128, :], 0.0), (), [f"qaug{s_}{i}"])
                            P.op("dve", I("memset", kaug[s_][i][64:128, :], 0.0), (), [f"kaug{s_}{i}"])
                            P.op("dve", I("memset", qaug[s_][i][64:70, :], 1.0), (), [f"qaug{s_}{i}"])
                            P.op("dve", I("memset", kaug[s_][i][64:70, :], -1.0), (), [f"kaug{s_}{i}"])
                        P.op("dve", I("memset", vaug[s_][:, :, 64:128], 1.0), (), [f"vaug{s_}"])
                    cnt3 = {"sq": 0, "pj": 0}

                    def fox_proj(p):
                        for _ in fox_proj_gen(p):
                            pass

                    def fox_proj_gen(p):
                        s_ = p % 2
                        wq2, wk2, wv2 = wq2_[p % 2], wk2_[p % 2], wv2_[p % 2]
                        if p + 1 < 8:
                            ld_f(p + 1)
                        sched = {}

                        def at(tick, fn):
                            sched.setdefault(tick, []).append(fn)
                        u = 0
                        for (wt, wkey, aug, akey, wcol) in ((wq2, f"wq2{p % 2}", qaug, "qaug", fqw_s), (wk2, f"wk2{p % 2}", kaug, "kaug", fkw_s)):
                            for t in range(4):
                                ts_ = slice(t * 512, (t + 1) * 512)
                                pb = 5 + u % 2
                                i2 = u % 2
                                t0 = 8 * u

                                def s0c(ch, wt=wt, wkey=wkey, ts_=ts_, pb=pb, t=t):
                                    P.op("pe", [I("matmul", bk[pb][:], lhsT=wt[:, k, 0:128], rhs=hT[:, k, ts_], start=(k == 0), stop=(k == KC - 1))
                                                for k in (2 * ch, 2 * ch + 1)], [wkey] + hT_keys(t), [BK[pb]])

                                def s1(pb=pb, i2=i2):
                                    act(sq[i2][:], bk[pb][:], AF.Square, [BK[pb]], [f"sq{i2}"])

                                def s2(i2=i2):
                                    mm(bk[7][:], [(bdones[:], sq[i2][:])], ["bdones", f"sq{i2}"], [BK[7]])

                                def s3(i2=i2):
                                    act(rs[i2][:], bk[7][:], AF.Ln, [BK[7]], [f"rs{i2}"], bias=EPS, scale=1.0 / 64)
                                    act(rs[i2][:], rs[i2][:], AF.Exp, [f"rs{i2}"], [f"rs{i2}"], scale=-0.5)

                                def s4(aug=aug, akey=akey, wcol=wcol, ts_=ts_, pb=pb, i2=i2):
                                    for hh in range(2):
                                        pr = slice(hh * 64, hh * 64 + 64)
                                        P.op("dve", I("scalar_tensor_tensor", out=aug[s_][hh][0:64, ts_], in0=bk[pb][pr, :], scalar=wcol[pr, 0:1],
                                                      in1=rs[i2][pr, :], op0=ALU.mult, op1=ALU.mult),
                                             [BK[pb], f"rs{i2}", "fqw_s", "fkw_s"], [f"{akey}{s_}{hh}"])
                                for ch in range(4):
                                    at(t0 + ch, (lambda ch=ch, f=s0c: f(ch)))
                                for si, fn in enumerate((s1, s2, s3, s4)):
                                    at(t0 + 6 + 3 * si, fn)
                                u += 1
                        for g4, t0 in enumerate((64, 72, 76, 80)):
                            pb = 5 + g4 % 2

                            def v0(jj, g4=g4, pb=pb):
                                j = g4 * 4 + jj
                                mm(bk[pb][:, jj * 128:(jj + 1) * 128], [(hT[:, k, j * 128:(j + 1) * 128], wv2[:, k, 0:128]) for k in range(KC)],
                                   [f"wv2{p % 2}", f"hT{j}"], [BK[pb]])

                            def v1(g4=g4, pb=pb):
                                pv = bk[pb][:].rearrange("p (j c) -> p j c", c=128)
                                act(vaug[s_][:, g4 * 4:g4 * 4 + 4, 0:64], pv[:, :, 0:64], AF.Identity, [BK[pb]], [f"vaug{s_}"])
                                act(vaug[s_][:, g4 * 4:g4 * 4 + 4, 128:192], pv[:, :, 64:128], AF.Identity, [BK[pb]], [f"vaug{s_}"])
                            for jj in range(4):
                                at(t0 + jj, (lambda jj=jj, f=v0: f(jj)))
                            at(t0 + 6, v1)

                        def dmas():
                            for hh in range(2):
                                h = 2 * p + hh
                                for r3 in range(3):
                                    P.dma("sp", kaug[s_][hh][64 + r3:65 + r3, :], npscr[4 + h:5 + h, r3 * S:(r3 + 1) * S], f"ld_ka{s_}{hh}", ["npscr"], [f"kaug{s_}{hh}"])
                                    P.dma("sp", qaug[s_][hh][67 + r3:68 + r3, :], npscr[4 + h:5 + h, r3 * S:(r3 + 1) * S], f"ld_qa{s_}{hh}", ["npscr"], [f"qaug{s_}{hh}"])
                        at(1, dmas)
                        for tick in range(max(sched) + 1):
                            for fn in sched.get(tick, ()):
                                fn()
                            yield

                    cnt_a = {"s": 0, "p": 0, "o": 0}

                    def fox_att(p, hh, gen=None):
                        s_ = p % 2
                        qa, ka, va = qaug[s_][hh], kaug[s_][hh], vaug[s_]
                        qk, kk, vk = f"qaug{s_}{hh}", f"kaug{s_}{hh}", f"vaug{s_}"
                        vlo = 0 if hh == 0 else 64
                        ol = 0 if hh == 0 else 64
                        dl = 64 if hh == 0 else 0
                        for g in range(4):
                            ob = 3 + cnt_a["o"] % 2
                            cnt_a["o"] += 1
                            nkb = 4 * g + 4
                            pend = []

                            def score(i):
                                r = i - 4 * g
                                qo = 128 * r if r >= 0 else 0
                                sb_ = cnt_a["s"] % 3
                                cnt_a["s"] += 1
                                pairs = [(ka[:, i * 128:(i + 1) * 128], qa[:, 512 * g + qo:512 * g + 512])]
                                fns = [I("matmul", bk[sb_][:, qo:512], lhsT=pairs[0][0], rhs=pairs[0][1], start=True, stop=(r < 0))]
                                if r >= 0:
                                    fns.append(I("matmul", bk[sb_][:, qo:qo + 128], lhsT=ident_bf[:], rhs=maskb[:], start=False, stop=True))
                                P.op("pe", fns, [kk, qk, "ident_bf", "maskb"], [BK[sb_]])
                                return (i, qo, sb_)

                            def expo(i, qo, sb_):
                                pi = cnt_a["p"] % 4
                                cnt_a["p"] += 1
                                act(pT[pi][:, qo:512], bk[sb_][:, qo:512], AF.Exp, [BK[sb_]], [f"pT{pi}"])
                                return pi

                            def pv(i, qo, pi):
                                P.op("pe", I("matmul", bk[ob][:, qo:512], lhsT=va[:, i, vlo:vlo + 128], rhs=pT[pi][:, qo:512],
                                                                             start=(i == 0), stop=(i == nkb - 1)), [vk, f"pT{pi}"], [BK[ob]])
                            scq = [score(i) for i in range(min(2, nkb))]
                            for i in range(nkb):
                                if i + 2 < nkb:
                                    scq.append(score(i + 2))
                                sc0 = scq.pop(0)
                                pi = expo(*sc0)
                                pv(sc0[0], sc0[1], pi)
                                if gen is not None:
                                    next(gen, None)
                            rd = rden[g % 2]
                            P.op("dve", I("reciprocal", out=rd[dl:dl + 64, :], in_=bk[ob][dl:dl + 64, :]), [BK[ob]], [f"rden{g % 2}"])
                            P.op("dve", I("tensor_tensor", out=hfT[ol:ol + 64, p, g * 512:(g + 1) * 512], in0=bk[ob][ol:ol + 64, :],
                                                                         in1=rd[dl:dl + 64, :], op=ALU.mult), [BK[ob], f"rden{g % 2}"], ["hfT"])

                    fox_proj(0)
                    for p in range(8):
                        if fox_limit <= 2 * p:
                            break
                        gen = fox_proj_gen(p + 1) if p + 1 < 8 else None
                        fox_att(p, 0, gen)
                        if fox_limit <= 2 * p + 1:
                            break
                        fox_att(p, 1, gen)
                        if gen is not None:
                            for _ in gen:
                                pass
                    P.barrier()
                st3b.close()

                with contextlib.ExitStack() as st:
                    wo = [SB(st, f"wo{i}", [128, KC, 512], BF16) for i in range(2)]
                    sg = [SB(st, f"sg{i}", [128, 512], F32) for i in range(2)]
                    n4 = 0
                    for half in range(2):
                        P.dma("pool", wo[half][:], w_in_v[:, :, O_MO + 512 * half:O_MO + 512 * half + 512], f"ld_wo{half}", (), [f"wo{half}"])
                    for c in range(KC):
                        half, cl = c // 4, (c % 4) * 128
                        for t in range(4):
                            ts_ = slice(t * 512, (t + 1) * 512)
                            pb = n4 % 4
                            i2 = n4 % 2
                            n4 += 1
                            mm(bk[pb][:], [(wo[half][:, k, cl:cl + 128], hT[:, k, ts_]) for k in range(KC)], [f"wo{half}"] + hT_keys(t), [BK[pb]])
                            act(sg[i2][:], bk[pb][:], AF.Sigmoid, [BK[pb]], [f"sg{i2}"])
                            P.op("dve", I("scalar_tensor_tensor", out=hmT[:, c, ts_], in0=hmT[:, c, ts_], scalar=mnw[:, c:c + 1],
                                                                                             in1=sg[i2][:], op0=ALU.mult, op1=ALU.mult),
                                 [f"hmTc{c}", "mnw", f"sg{i2}"], [f"hmTc{c}"])
                    P.barrier()

                st5 = contextlib.ExitStack()
                mgT = SB(st5, "mgT", [128, KC, S], BF16)
                wot = SB(st5, "wot", [128, KC, D], BF16)
                with contextlib.ExitStack() as st:
                    wsl = {nm: [SB(st, f"w5{nm}{i}", [128, KC, 256], BF16) for i in range(2)] for nm in ("bm", "bf", "ga", "gb")}
                    sga = [SB(st, "sga0", [128, 512], F32)] * 2
                    sgb = [SB(st, "sgb0", [128, 512], F32)] * 2
                    t1 = [SB(st, f"t1{i}", [128, 512], F32) for i in range(2)]
                    t2 = [SB(st, f"t2{i}", [128, 512], F32) for i in range(2)]
                    srcs = {"bm": (w_bm_v, 0), "bf": (w_bf_v, 0), "ga": (w_in_v, O_GA), "gb": (w_in_v, O_GB)}
                    n5 = 0
                    def ld_5(c2):
                        sl = c2 % 2
                        for nm in ("bm", "bf", "ga", "gb"):
                            src, off = srcs[nm]
                            P.dma("pool", wsl[nm][sl][:], src[:, :, off + 256 * c2:off + 256 * c2 + 256], f"ld_w5{nm}{sl}", (), [f"w5{nm}{sl}"])
                    ld_5(0)
                    for c2 in range(4):
                        sl = c2 % 2
                        if c2 + 1 < 4:
                            ld_5(c2 + 1)
                        else:
                            for half in range(2):
                                P.dma("pool", wot[:, :, half * 512:(half + 1) * 512], w_out_v[:, :, half * 512:(half + 1) * 512], "ld_wot", (), ["wot"])
                        for cc in range(2):
                            c = c2 * 2 + cc
                            cl = cc * 128
                            for t in range(4):
                                ts_ = slice(t * 512, (t + 1) * 512)
                                i2 = n5 % 2
                                pb0 = (n5 % 2) * 4
                                n5 += 1
                                mm(bk[pb0][:], [(wsl["bm"][sl][:, k, cl:cl + 128], hmT[:, k, ts_]) for k in range(KC)], [f"w5bm{sl}", "hmT"], [BK[pb0]])
                                mm(bk[pb0 + 1][:], [(wsl["bf"][sl][:, k, cl:cl + 128], hfT[:, k, ts_]) for k in range(KC)], [f"w5bf{sl}", "hfT"], [BK[pb0 + 1]])
                                mm(bk[pb0 + 2][:], [(wsl["ga"][sl][:, k, cl:cl + 128], hT[:, k, ts_]) for k in range(KC)], [f"w5ga{sl}"] + hT_keys(t), [BK[pb0 + 2]])
                                mm(bk[pb0 + 3][:], [(wsl["gb"][sl][:, k, cl:cl + 128], hT[:, k, ts_]) for k in range(KC)], [f"w5gb{sl}"] + hT_keys(t), [BK[pb0 + 3]])
                                act(sga[i2][:], bk[pb0 + 2][:], AF.Sigmoid, [BK[pb0 + 2]], ["sga"])
                                act(sgb[i2][:], bk[pb0 + 3][:], AF.Sigmoid, [BK[pb0 + 3]], ["sgb"])
                                P.op("dve", I("tensor_tensor", out=t1[i2][:], in0=bk[pb0][:], in1=sga[i2][:], op=ALU.mult),
                                     [BK[pb0], "sga"], [f"t1{i2}"])
                                P.op("dve", I("tensor_tensor", out=t2[i2][:], in0=bk[pb0 + 1][:], in1=sgb[i2][:], op=ALU.mult),
                                     [BK[pb0 + 1], "sgb"], [f"t2{i2}"])
                                P.op("dve", I("tensor_tensor", out=mgT[:, c, ts_], in0=t1[i2][:], in1=t2[i2][:], op=ALU.add),
                                     [f"t1{i2}", f"t2{i2}"], ["mgT"])
                    P.barrier()

                with contextlib.ExitStack() as st:
                    g1bc = SB(st, "g1bc", [128, D], F32)
                    xs = [SB(st, f"xr{i}", [128, D], F32) for i in range(2)]
                    x1 = [SB(st, f"x1{i}", [128, D], F32) for i in range(3)]
                    build_gate_bc(g1bc, "g1bc", b, 0)

                    def pre2(j):
                        sl = j % 2
                        P.dma("sp", xs[sl][:], x[b, j * 128:(j + 1) * 128, :], f"ld_xr{sl}", (), [f"xr{sl}"])
                        for hf in range(2):
                            pb = 2 * sl + hf
                            cs = slice(hf * 512, (hf + 1) * 512)
                            mm(bk[pb][:], [(mgT[:, k, j * 128:(j + 1) * 128], wot[:, k, cs]) for k in range(KC)], ["mgT", "wot"], [BK[pb]])

                    def fin2(j):
                        sl = j % 2
                        for hf in range(2):
                            pb = 2 * sl + hf
                            cs = slice(hf * 512, (hf + 1) * 512)
                            P.op("dve", I("tensor_tensor", out=x1[j % 3][:, cs], in0=bk[pb][:], in1=g1bc[:, cs], op=ALU.mult),
                                 [BK[pb], "g1bc"], [f"x1{j % 3}"])
                        P.op("dve", I("tensor_tensor", out=x1[j % 3][:], in0=x1[j % 3][:], in1=xs[sl][:], op=ALU.add), [f"x1{j % 3}", f"xr{sl}"], [f"x1{j % 3}"])
                        P.dma("sp", out[b, j * 128:(j + 1) * 128, :], x1[j % 3][:], f"st_x1{j % 3}", [f"x1{j % 3}"], [f"out{j}"])
                        return x1[j % 3][:], [f"x1{j % 3}"]
                    blk2 = norm_to_hT(st, b, pre2, fin2, a2, 24, "b")
                    for j in range(NB + 1):
                        blk2(j)
                    P.barrier()
                st5.close()
            if stop_after <= 5:
                continue

            with contextlib.ExitStack() as st:
                wdn = SB(st, "wdn", [128, NF, D], BF16)
                g2bc = SB(st, "g2bc", [128, D], F32)
                gT = SB(st, "gT", [128, NF, 1024], BF16)
                wug = [SB(st, f"wug{i}", [128, KC, 256], BF16) for i in range(2)]
                wuv = [SB(st, f"wuv{i}", [128, KC, 256], BF16) for i in range(2)]
                ug = [SB(st, f"ug{i}", [128, 1026], F32) for i in range(2)]
                uv = [SB(st, f"uv{i}", [128, 1026], F32) for i in range(2)]
                cg = [SB(st, f"cg{i}", [128, 512], F32) for i in range(3)]
                cv = [SB(st, f"cv{i}", [128, 512], F32) for i in range(3)]
                hal = SB(st, "hal", [128, 44, 2], F32)
                xr = [SB(st, f"xq{i}", [128, D], F32) for i in range(2)]
                ot = [SB(st, f"ot{i}", [128, D], F32) for i in range(2)]
                def ld_u(u):
                    f2_, sl_ = u % (NF // 2), u % 2
                    P.dma("pool", wug[sl_][:], w_up_v[:, :, 256 * f2_:256 * f2_ + 256], f"ld_wug{sl_}", (), [f"wug{sl_}"])
                    P.dma("pool", wuv[sl_][:], w_up_v[:, :, FF + 256 * f2_:FF + 256 * f2_ + 256], f"ld_wuv{sl_}", (), [f"wuv{sl_}"])
                ld_u(0)
                for f4 in range(0, NF, 4):
                    n_ = min(4, NF - f4)
                    P.dma("pool", wdn[:, f4:f4 + n_, :], w_down_v[:, f4:f4 + n_, :], "ld_wdn", (), ["wdn"])
                build_gate_bc(g2bc, "g2bc", b, 1)
                P.op("dve", I("memset", hal[:], 0.0), (), ["hal"])
                n6 = 0
                pend6 = []
                nu6 = [0]
                for half in range(2):
                    for f2 in range(NF // 2):
                        u_ = half * (NF // 2) + f2
                        sl = u_ % 2
                        if u_ + 1 < NF:
                            ld_u(u_ + 1)
                        for ff_ in range(2):
                            f = f2 * 2 + ff_
                            fl = ff_ * 128
                            us = n6 % 2
                            n6 += 1
                            for (ubuf, ukey, hidx) in ((ug[us], f"ug{us}", f), (uv[us], f"uv{us}", 22 + f)):
                                act(ubuf[:, 0:2], hal[:, hidx, :], AF.Identity, ["hal"], [ukey])
                            for tt in range(2):
                                t = half * 2 + tt
                                ts_ = slice(t * 512, (t + 1) * 512)
                                i2 = nu6[0] % 3
                                pbg, pbv = 2 * i2, 2 * i2 + 1
                                nu6[0] += 1
                                mm(bk[pbg][:], [(wug[sl][:, k, fl:fl + 128], hT[:, k, ts_]) for k in range(KC)], [f"wug{sl}"] + hT_keys(t), [BK[pbg]])
                                mm(bk[pbv][:], [(wuv[sl][:, k, fl:fl + 128], hT[:, k, ts_]) for k in range(KC)], [f"wuv{sl}"] + hT_keys(t), [BK[pbv]])
                                for (pb, ubuf, ukey, cbuf, ckey, widx) in ((pbg, ug[us], f"ug{us}", cg[i2], f"cg{i2}", f), (pbv, uv[us], f"uv{us}", cv[i2], f"cv{i2}", 22 + f)):
                                    u0 = 2 + tt * 512
                                    act(ubuf[:, u0:u0 + 512], bk[pb][:], AF.Identity, [BK[pb]], [ukey])
                                    act(cbuf[:], bk[pb][:], AF.Identity, [BK[pb], "convw", "convb"], [ckey], bias=convb[:, widx:widx + 1], scale=convw[:, widx, 2:3])
                                    P.op("dve", I("scalar_tensor_tensor",
                                        out=cbuf[:], in0=ubuf[:, u0 - 1:u0 + 511], scalar=convw[:, widx, 1:2], in1=cbuf[:], op0=ALU.mult, op1=ALU.add),
                                        [ukey, ckey, "convw"], [ckey])
                                    P.op("dve", I("scalar_tensor_tensor",
                                        out=cbuf[:], in0=ubuf[:, u0 - 2:u0 + 510], scalar=convw[:, widx, 0:1], in1=cbuf[:], op0=ALU.mult, op1=ALU.add),
                                        [ukey, ckey, "convw"], [ckey])
                                if pend6:
                                    pend6.pop()()

                                def fin(i2=i2, f=f, tt=tt):
                                    act(cg[i2][:], cg[i2][:], AF.Silu, [f"cg{i2}"], [f"cg{i2}"])
                                    P.op("dve", I("tensor_tensor", out=gT[:, f, tt * 512:(tt + 1) * 512], in0=cg[i2][:], in1=cv[i2][:], op=ALU.mult),
                                         [f"cg{i2}", f"cv{i2}"], ["gT"])
                                pend6.append(fin)
                            for (ubuf, ukey, hidx) in ((ug[us], f"ug{us}", f), (uv[us], f"uv{us}", 22 + f)):
                                act(hal[:, hidx, :], ubuf[:, 1024:1026], AF.Identity, [ukey], ["hal"])
                    if pend6:
                        pend6.pop()()
                    for jj in range(8):
                        j = half * 8 + jj
                        sl = jj % 2
                        P.dma("sp", xr[sl][:], out[b, j * 128:(j + 1) * 128, :], f"ld_xq{sl}", [f"out{j}"], [f"xq{sl}"])
                        for hf in range(2):
                            pb = 6 + hf
                            cs = slice(hf * 512, (hf + 1) * 512)
                            mm(bk[pb][:], [(gT[:, f, jj * 128:(jj + 1) * 128], wdn[:, f, cs]) for f in range(NF)], ["gT", "wdn"], [BK[pb]])
                            P.op("dve", I("tensor_tensor", out=ot[sl][:, cs], in0=bk[pb][:], in1=g2bc[:, cs], op=ALU.mult),
                                 [BK[pb], "g2bc"], [f"ot{sl}"])
                        P.op("dve", I("tensor_tensor", out=ot[sl][:], in0=ot[sl][:], in1=xr[sl][:], op=ALU.add), [f"ot{sl}", f"xq{sl}"], [f"ot{sl}"])
                        P.dma("sp", out[b, j * 128:(j + 1) * 128, :], ot[sl][:], f"st_ot{sl}", [f"ot{sl}"], [f"out{j}"])
                P.barrier()
        P.barrier()
        P.run()
    return nc


def _prep_inputs(inp, core):
    f = lambda a: np.ascontiguousarray(np.asarray(a, dtype=np.float32))
    b0 = 2 * core
    c = np.asarray(inp["c"], dtype=np.float32)[b0:b0 + 2]

    def pk(v, n):
        return f(np.asarray(v, dtype=np.float32).reshape(n, 128).T)
    conv_w = np.asarray(inp["conv_w"], dtype=np.float32)[0]
    m = {
        "x": f(np.asarray(inp["x"])[b0:b0 + 2]),
        "cT": f(c.T.reshape(KC, 128, 2).transpose(1, 0, 2)),
        "w_ada": f(inp["w_ada"][0]),
        "b_ada": f(inp["b_ada"][0]),
        "b_adaT": pk(inp["b_ada"][0], 48),
        "n1wT": pk(inp["norm1_w"][0], KC),
        "n2wT": pk(inp["norm2_w"][0], KC),
        "mnwT": pk(inp["mlstm_norm_w"][0], KC),
        "w_in": f(inp["w_in"][0]),
        "b_i": f(np.asarray(inp["b_mlstm_i"][0]).reshape(4, 1)),
        "b_f20": f(np.concatenate([np.asarray(inp["b_mlstm_f"][0]), np.asarray(inp["b_fox_f"][0])]).reshape(20, 1)),
        "fqw": f(np.tile(np.asarray(inp["fox_q_norm_w"][0]), 2).reshape(128, 1)),
        "fkw": f(np.tile(np.asarray(inp["fox_k_norm_w"][0]), 2).reshape(128, 1)),
        "w_bm": f(inp["w_branch_mlstm"][0]),
        "w_bf": f(inp["w_branch_fox"][0]),
        "w_out": f(inp["w_out"][0]),
        "w_up": f(inp["w_up"][0]),
        "convT": f(conv_w.reshape(3, 44, 128).transpose(2, 1, 0)),
        "convbT": pk(inp["conv_b"][0], 44),
        "w_down": f(inp["w_down"][0]),
    }
    return m


def kernel(**inputs):
    nc = build_program()
    in_maps = [_prep_inputs(inputs, i) for i in range(NCORES)]
    res = run_bass_kernel_spmd(nc, in_maps, core_ids=list(range(NCORES)))
    return np.concatenate([np.asarray(r["out"]) for r in res.results], axis=0).astype(np.float32)
```

```python
import contextlib
import os
import numpy as np
import concourse.bass as bass
import concourse.mybir as mybir
from concourse.bass_utils import run_bass_kernel_spmd

F32 = mybir.dt.float32
BF16 = mybir.dt.bfloat16
AF = mybir.ActivationFunctionType
ALU = mybir.AluOpType

ENGS = ("pe", "act", "dve", "pool", "sp")
NCORES = 8
S = 2048
NB = 16
D = 1024
KC = 8
FF = 2816
NF = 22
EPS = 1e-6
O_MQ, O_MK, O_MV, O_MO, O_MI, O_MF = 0, 1024, 2048, 3072, 4096, 4100
O_FQ, O_FK, O_FV, O_FF, O_GA, O_GB = 4104, 5128, 6152, 7176, 7192, 8216


class Prog:
    def __init__(self, nc, stack):
        self.nc = nc
        self.q = {e: [] for e in ENGS}
        self.cnt = {}
        self.sem = {}
        self.stack = stack
        for e in ENGS:
            self.sem[e] = stack.enter_context(nc.semaphore("s_" + e))
            self.cnt[e] = 0
        self.waited = {e: {} for e in ENGS}
        self.last_w = {}
        self.readers = {}

    def dma_sem(self, key):
        if key not in self.sem:
            self.sem[key] = self.stack.enter_context(self.nc.semaphore("d_" + key))
            self.cnt[key] = 0
        return key

    def _deps(self, eng, reads, writes):
        deps = {}

        def add(d):
            if d is None:
                return
            k, v = d
            if deps.get(k, 0) < v:
                deps[k] = v

        for r in reads:
            add(self.last_w.get(r))
            if r.startswith("bk"):
                for k, v in self.readers.get(r, {}).items():
                    if k != eng:
                        add((k, v))
        for w in writes:
            add(self.last_w.get(w))
            for k, v in self.readers.get(w, {}).items():
                add((k, v))
        return deps

    def _emit_waits(self, eng, deps):
        for k, v in deps.items():
            if self.waited[eng].get(k, 0) >= v:
                continue
            self.waited[eng][k] = v
            sem = self.sem[k]
            self.q[eng].append(lambda e, sem=sem, v=v: e.wait_ge(sem, v))

    def _mark(self, key, val, reads, writes):
        for r in reads:
            self.readers.setdefault(r, {})[key] = val
        for w in writes:
            self.last_w[w] = (key, val)
            self.readers[w] = {}

    def op(self, eng, fns, reads=(), writes=()):
        if not isinstance(fns, (list, tuple)):
            fns = [fns]
        self._emit_waits(eng, self._deps(eng, reads, writes))
        sem = self.sem[eng]
        self.cnt[eng] += 1
        n = self.cnt[eng]
        for f in fns[:-1]:
            self.q[eng].append(f)
        last = fns[-1]
        self.q[eng].append(lambda e, last=last, sem=sem: last(e).then_inc(sem, 1))
        self._mark(eng, n, reads, writes)
        return n

    def dma(self, qeng, out, in_, semkey, reads=(), writes=()):
        self.dma_sem(semkey)
        self._emit_waits(qeng, self._deps(None, reads, writes))
        sem = self.sem[semkey]
        self.cnt[semkey] += 16
        v = self.cnt[semkey]
        self.q[qeng].append(lambda e, out=out, in_=in_, sem=sem: e.dma_start(out=out, in_=in_).then_inc(sem, 16))
        self._mark(semkey, v, reads, writes)
        return v

    def barrier(self):
        allk = {k: v for k, v in self.cnt.items() if v > 0}
        for e in ENGS:
            self._emit_waits(e, dict(allk))
        self.last_w = {}
        self.readers = {}

    def run(self):
        nc = self.nc
        with nc.Block() as block:
            @block.tensor
            def _(e):
                for f in self.q["pe"]:
                    f(e)

            @block.scalar
            def _(e):
                for f in self.q["act"]:
                    f(e)

            @block.vector
            def _(e):
                for f in self.q["dve"]:
                    f(e)

            @block.gpsimd
            def _(e):
                for f in self.q["pool"]:
                    f(e)

            @block.sync
            def _(e):
                for f in self.q["sp"]:
                    f(e)


def build_program(nseq=2, stop_after=99, dbg=None, fox_limit=99):
    nc = bass.Bass("TRN2", target_bir_lowering=False)

    def din(name, shape):
        return nc.dram_tensor(name, list(shape), F32, kind="ExternalInput").ap()

    x = din("x", [2, S, D])
    cT = din("cT", [128, KC, 2])
    w_ada = din("w_ada", [D, 6 * D])
    b_ada = din("b_ada", [6 * D])
    b_adaT = din("b_adaT", [128, 48])
    n1wT = din("n1wT", [128, KC])
    n2wT = din("n2wT", [128, KC])
    mnwT = din("mnwT", [128, KC])
    w_in = din("w_in", [D, 9240])
    b_i = din("b_i", [4, 1])
    b_f20 = din("b_f20", [20, 1])
    fqw = din("fqw", [128, 1])
    fkw = din("fkw", [128, 1])
    w_bm = din("w_bm", [D, D])
    w_bf = din("w_bf", [D, D])
    w_out = din("w_out", [D, D])
    w_up = din("w_up", [D, 2 * FF])
    convT = din("convT", [128, 44, 3])
    convbT = din("convbT", [128, 44])
    w_down = din("w_down", [FF, D])
    out = nc.dram_tensor("out", [2, S, D], F32, kind="ExternalOutput").ap()
    npscr = nc.dram_tensor("npscr", [20, 3 * S], BF16).ap()
    dbg_t = None
    if dbg is not None:
        dbg_t = nc.dram_tensor("dbg", list(dbg), F32, kind="ExternalOutput").ap()

    w_in_v = w_in.rearrange("(k p) n -> p k n", p=128)
    w_ada_v = w_ada.rearrange("(k p) n -> p k n", p=128)
    w_bm_v = w_bm.rearrange("(k p) n -> p k n", p=128)
    w_bf_v = w_bf.rearrange("(k p) n -> p k n", p=128)
    w_out_v = w_out.rearrange("(k p) n -> p k n", p=128)
    w_up_v = w_up.rearrange("(k p) n -> p k n", p=128)
    w_down_v = w_down.rearrange("(k p) n -> p k n", p=128)

    with contextlib.ExitStack() as st0:
        P = Prog(nc, st0)

        uid = [0]

        def SB(st, name, shape, dt):
            uid[0] += 1
            return st.enter_context(nc.sbuf_tensor(f"{name}_{uid[0]}", list(shape), dt))

        bk = [st0.enter_context(nc.psum_tensor(f"bk{i}", [128, 512], F32)) for i in range(8)]
        BK = [f"bk{i}" for i in range(8)]

        def bkb(i):
            return bk[i][:].bitcast(BF16)

        def I(name, *a, **kw):
            return lambda e: getattr(e, name)(*a, **kw)

        def mm(out_ap, pairs, reads, writes):
            n = len(pairs)
            fns = []
            for i, (l, r) in enumerate(pairs):
                fns.append(I("matmul", out_ap, lhsT=l, rhs=r, start=(i == 0), stop=(i == n - 1)))
            P.op("pe", fns, reads, writes)

        def act(out_ap, in_ap, func, reads, writes, bias=None, scale=None, accum_out=None):
            kw = {}
            if bias is not None:
                kw["bias"] = bias
            if scale is not None:
                kw["scale"] = scale
            if accum_out is not None:
                kw["accum_out"] = accum_out
            P.op("act", I("activation", out=out_ap, in_=in_ap, func=func, **kw), reads, writes)

        ident_bf = SB(st0, "ident_bf", [128, 128], BF16)
        ident_f = SB(st0, "ident_f", [128, 128], F32)
        maskb = SB(st0, "maskb", [128, 128], BF16)
        mask01 = SB(st0, "mask01", [128, 128], F32)
        bdones = SB(st0, "bdones", [128, 128], BF16)
        sel = SB(st0, "sel", [4, 512], F32)
        tmpc = SB(st0, "tmpc", [128, 512], F32)
        mask4 = SB(st0, "mask4", [128, 512], F32)

        P.op("pool", I("memset", tmpc[:], 1.0), (), ["tmpc"])
        P.op("pool", I("affine_select", out=ident_f[:], in_=tmpc[:, 0:128], pattern=[[-1, 128]], compare_op=ALU.is_equal,
                                               fill=0.0, base=0, channel_multiplier=1), ["tmpc"], ["ident_f"])
        P.op("pool", I("tensor_copy", out=ident_bf[:], in_=ident_f[:]), ["ident_f"], ["ident_bf"])
        P.op("pool", I("affine_select", out=mask01[:], in_=tmpc[:, 0:128], pattern=[[1, 128]], compare_op=ALU.is_ge,
                                               fill=0.0, base=0, channel_multiplier=-1), ["tmpc"], ["mask01"])
        for i4 in range(4):
            P.op("pool", I("tensor_copy", out=mask4[:, i4 * 128:(i4 + 1) * 128], in_=mask01[:]), ["mask01"], ["mask4"])
        P.op("pool", I("tensor_scalar", out=maskb[:], in0=mask01[:], scalar1=-1.0, scalar2=30000.0,
                                               op0=ALU.add, op1=ALU.mult), ["mask01"], ["maskb"])
        P.op("pool", I("memset", bdones[:], 0.0), (), ["bdones"])
        P.op("pool", I("memset", bdones[0:64, 0:64], 1.0), (), ["bdones"])
        P.op("pool", I("memset", bdones[64:128, 64:128], 1.0), (), ["bdones"])
        P.op("pool", I("affine_select", out=sel[:].rearrange("p (h j) -> p h j", j=128),
                                               in_=tmpc[0:4, :].rearrange("p (h j) -> p h j", j=128),
                                               pattern=[[-1, 4], [0, 128]], compare_op=ALU.is_equal,
                                               fill=0.0, base=0, channel_multiplier=1), ["tmpc"], ["sel"])

        cT_sb = SB(st0, "cT_sb", [128, KC, 2], F32)
        sc = SB(st0, "sc", [128, KC, 2], BF16)
        badaT = SB(st0, "badaT", [128, 48], F32)
        n1w = SB(st0, "n1w", [128, KC], F32)
        n2w = SB(st0, "n2w", [128, KC], F32)
        mnw = SB(st0, "mnw", [128, KC], F32)
        bi = SB(st0, "bi", [4, 1], F32)
        nbf20 = SB(st0, "nbf20", [20, 1], F32)
        fqw_s = SB(st0, "fqw_s", [128, 1], F32)
        fkw_s = SB(st0, "fkw_s", [128, 1], F32)
        convw = SB(st0, "convw", [128, 44, 3], F32)
        convb = SB(st0, "convb", [128, 44], F32)
        adaT = SB(st0, "adaT", [128, 48, 2], F32)
        a1 = SB(st0, "a1", [128, KC, 2], F32)
        a2 = SB(st0, "a2", [128, KC, 2], F32)
        grow = SB(st0, "grow", [2, 2048], F32)
        hT = SB(st0, "hT", [128, KC, S], BF16)

        for (t_, src, key) in [(cT_sb, cT, "cT_sb"), (badaT, b_adaT, "badaT"), (n1w, n1wT, "n1w"), (n2w, n2wT, "n2w"),
                               (mnw, mnwT, "mnw"), (bi, b_i, "bi"), (nbf20, b_f20, "nbf20"), (fqw_s, fqw, "fqw_s"),
                               (fkw_s, fkw, "fkw_s"), (convw, convT, "convw"), (convb, convbT, "convb")]:
            P.dma("sp", t_[:], src, "ld_" + key, (), [key])
        P.op("dve", I("tensor_scalar", out=nbf20[:], in0=nbf20[:], scalar1=-1.0, scalar2=None, op0=ALU.mult), ["nbf20"], ["nbf20"])
        P.op("dve", I("tensor_scalar", out=fqw_s[:], in0=fqw_s[:], scalar1=0.125, scalar2=None, op0=ALU.mult), ["fqw_s"], ["fqw_s"])

        with contextlib.ExitStack() as st:
            bgate = SB(st, "bgate", [2, 2048], F32)
            wa = [SB(st, f"wa{i}", [128, KC, 1536], BF16) for i in range(2)]
            for r in range(2):
                P.dma("sp", bgate[r:r + 1, 0:1024], b_ada[2048:3072].rearrange("(o n) -> o n", o=1), "ld_bg", (), ["bgate"])
                P.dma("sp", bgate[r:r + 1, 1024:2048], b_ada[5120:6144].rearrange("(o n) -> o n", o=1), "ld_bg", (), ["bgate"])
            act(sc[:], cT_sb[:], AF.Silu, ["cT_sb"], ["sc"])
            for pc in range(4):
                sl = pc % 2
                P.dma("pool", wa[sl][:], w_ada_v[:, :, pc * 1536:(pc + 1) * 1536], f"ld_wa{sl}", (), [f"wa{sl}"])
                for ch in range(12):
                    gch = pc * 12 + ch
                    mm(bk[0][:, gch * 2:gch * 2 + 2],
                       [(wa[sl][:, k, ch * 128:(ch + 1) * 128], sc[:, k, :]) for k in range(KC)],
                       [f"wa{sl}", "sc"], [BK[0]])
                if pc in (1, 3):
                    for hf in range(2):
                        bi_ = 1 + (pc // 2) * 2 + hf
                        mm(bk[bi_][0:2, :],
                           [(sc[:, k, :], wa[sl][:, k, 512 + hf * 512:1024 + hf * 512]) for k in range(KC)],
                           [f"wa{sl}", "sc"], [BK[bi_]])
            psv = bk[0][:, 0:96].rearrange("p (c b) -> p c b", b=2)
            for b in range(2):
                P.op("dve", I("tensor_tensor", out=adaT[:, :, b], in0=psv[:, :, b], in1=badaT[:], op=ALU.add),
                     [BK[0], "badaT"], ["adaT"])
            for b in range(2):
                P.op("dve", I("scalar_tensor_tensor", out=a1[:, :, b], in0=adaT[:, 8:16, b], scalar=1.0, in1=n1w[:],
                                                                  op0=ALU.add, op1=ALU.mult), ["adaT", "n1w"], ["a1"])
                P.op("dve", I("scalar_tensor_tensor", out=a2[:, :, b], in0=adaT[:, 32:40, b], scalar=1.0, in1=n2w[:],
                                                                  op0=ALU.add, op1=ALU.mult), ["adaT", "n2w"], ["a2"])
            for i in range(4):
                P.op("dve", I("tensor_tensor", out=grow[:, i * 512:(i + 1) * 512], in0=bk[1 + i][0:2, :],
                                                           in1=bgate[:, i * 512:(i + 1) * 512], op=ALU.add),
                     [BK[1 + i], "bgate"], ["grow"])
            P.barrier()

        def build_gate_bc(gbc, key, b, g):
            for hf in range(2):
                mm(bk[6 + hf][:], [(sel[0:2, b * 128:(b + 1) * 128], grow[0:2, g * 1024 + hf * 512:g * 1024 + (hf + 1) * 512])],
                   ["sel", "grow"], [BK[6 + hf]])
                P.op("dve", I("tensor_copy", out=gbc[:, hf * 512:(hf + 1) * 512], in_=bk[6 + hf][:]),
                     [BK[6 + hf]], [key])

        def norm_to_hT(st, b, pre_fn, fin_fn, aX, shcol0, tag):
            xn = [SB(st, f"xn{tag}{i}", [128, D], BF16) for i in range(2)]
            junk = SB(st, f"junk{tag}", [128, D], BF16)
            stt = [SB(st, f"stt{tag}{i}", [128, 4], F32) for i in range(2)]
            held = {}

            def do_block(j):
                if j == 0:
                    pre_fn(0)
                if j + 1 < NB:
                    pre_fn(j + 1)
                if j < NB:
                    xs_ap, xs_keys = fin_fn(j)
                    s2 = j % 2
                    act(junk[:], xs_ap, AF.Square, xs_keys, [f"junk{tag}", f"stt{tag}{s2}"], accum_out=stt[s2][:, 0:1])
                    act(stt[s2][:, 1:2], stt[s2][:, 0:1], AF.Ln, [f"stt{tag}{s2}"], [f"stt{tag}{s2}"], bias=EPS, scale=1.0 / D)
                    act(stt[s2][:, 2:3], stt[s2][:, 1:2], AF.Exp, [f"stt{tag}{s2}"], [f"stt{tag}{s2}"], scale=-0.5)
                    P.op("dve", I("tensor_scalar", out=xn[s2][:], in0=xs_ap, scalar1=stt[s2][:, 2:3], scalar2=None, op0=ALU.mult),
                         xs_keys + [f"stt{tag}{s2}"], [f"xn{tag}{s2}"])
                    for hf in range(2):
                        tb = 4 + 2 * hf + s2
                        tp = bkb(tb)
                        P.op("pe", [I("transpose", out=tp[:, c * 128:(c + 1) * 128], in_=xn[s2][:, (4 * hf + c) * 128:(4 * hf + c + 1) * 128],
                                      identity=ident_bf[:]) for c in range(4)], [f"xn{tag}{s2}", "ident_bf"], [BK[tb]])
                if j >= 1:
                    jb = j - 1
                    s2 = jb % 2
                    tpa, tpb = bkb(4 + s2), bkb(6 + s2)
                    for c in range(4):
                        act(hT[:, c, jb * 128:(jb + 1) * 128], tpa[:, c * 128:(c + 1) * 128], AF.Identity,
                            [BK[4 + s2]], [f"hT{jb}"], bias=adaT[:, shcol0 + c, b:b + 1], scale=aX[:, c, b:b + 1])
                    for c in range(4, 8):
                        P.op("dve", I("tensor_scalar", out=hT[:, c, jb * 128:(jb + 1) * 128], in0=tpb[:, (c - 4) * 128:(c - 3) * 128],
                                      scalar1=aX[:, c, b:b + 1], scalar2=adaT[:, shcol0 + c, b:b + 1], op0=ALU.mult, op1=ALU.add),
                             [BK[6 + s2]], [f"hT{jb}"])
            return do_block

        hT_all = [f"hT{j}" for j in range(NB)]

        def hT_keys(t):
            return [f"hT{j}" for j in range(4 * t, 4 * t + 4)]

        for b in range(nseq):
            with contextlib.ExitStack() as st:
                xs = [SB(st, f"xs{i}", [128, D], F32) for i in range(3)]

                def ldx1(j):
                    P.dma("sp", xs[j % 3][:], x[b, j * 128:(j + 1) * 128, :], f"ld_xs{j % 3}", (), [f"xs{j % 3}"])

                def fin1(j):
                    return xs[j % 3][:], [f"xs{j % 3}"]
                blk1 = norm_to_hT(st, b, ldx1, fin1, a1, 0, "a")
                for j in range(NB + 1):
                    blk1(j)
                P.barrier()
            if stop_after <= 1:
                continue

            def gates(want_m, want_f, Grow=None, A_tm=None, X_tm=None, NP_=None):
                with contextlib.ExitStack() as st:
                    wi = SB(st, "wi", [128, KC, 4], BF16)
                    wf = SB(st, "wf", [128, KC, 20], BF16)
                    ipre = SB(st, "ipre", [4, S], F32)
                    spl = SB(st, "spl", [20, S], F32)
                    negF = SB(st, "negF", [20, S], F32)
                    P.dma("pool", wi[:], w_in_v[:, :, O_MI:O_MI + 4], "ld_wi", (), ["wi"])
                    P.dma("pool", wf[:, :, 0:4], w_in_v[:, :, O_MF:O_MF + 4], "ld_wf", (), ["wf"])
                    P.dma("pool", wf[:, :, 4:20], w_in_v[:, :, O_FF:O_FF + 16], "ld_wf", (), ["wf"])
                    for t in range(4):
                        ts_ = slice(t * 512, (t + 1) * 512)
                        mm(bk[t % 2][0:4, :], [(wi[:, k, :], hT[:, k, ts_]) for k in range(KC)], ["wi"] + hT_keys(t), [BK[t % 2]])
                        mm(bk[2 + t % 2][0:20, :], [(wf[:, k, :], hT[:, k, ts_]) for k in range(KC)], ["wf"] + hT_keys(t), [BK[2 + t % 2]])
                        act(ipre[:, ts_], bk[t % 2][0:4, :], AF.Identity, [BK[t % 2], "bi"], ["ipre"], bias=bi[:, 0:1])
                        act(spl[:, ts_], bk[2 + t % 2][0:20, :], AF.Exp, [BK[2 + t % 2], "nbf20"], ["spl"], bias=nbf20[:, 0:1], scale=-1.0)
                    act(spl[:], spl[:], AF.Ln, ["spl"], ["spl"], bias=1.0)
                    P.op("dve", I("tensor_tensor_scan", out=negF[:], data0=spl[:], data1=spl[:], initial=0.0, op0=ALU.add, op1=ALU.max),
                         ["spl"], ["negF"])
                    if want_m:
                        Am = SB(st, "Am", [4, S], F32)
                        Xm = SB(st, "Xm", [4, S], F32)
                        P.op("dve", I("tensor_tensor", out=Am[:], in0=ipre[:], in1=negF[0:4, :], op=ALU.add), ["ipre", "negF"], ["Am"])
                        P.op("dve", I("memset", Grow[:, 0:1], 0.0), (), ["Grow"])
                        P.op("dve", I("tensor_tensor_scan", out=Grow[:, 1:S + 1], data0=Am[:], data1=Am[:], initial=0.0, op0=ALU.max, op1=ALU.max),
                             ["Am"], ["Grow"])
                        P.op("dve", I("tensor_tensor", out=Xm[:], in0=negF[0:4, :], in1=Grow[:, 1:S + 1], op=ALU.subtract), ["negF", "Grow"], ["Xm"])
                        P.op("pe", [I("transpose", out=bk[4][:, j * 4:(j + 1) * 4], in_=Am[0:4, j * 128:(j + 1) * 128],
                                                               identity=ident_f[0:4, 0:4]) for j in range(NB)], ["Am", "ident_f"], [BK[4]])
                        P.op("pe", [I("transpose", out=bk[5][:, j * 4:(j + 1) * 4], in_=Xm[0:4, j * 128:(j + 1) * 128],
                                                               identity=ident_f[0:4, 0:4]) for j in range(NB)], ["Xm", "ident_f"], [BK[5]])
                        P.op("dve", I("tensor_copy", out=A_tm[:], in_=bk[4][:, 0:64]), [BK[4]], ["A_tm"])
                        P.op("dve", I("tensor_copy", out=X_tm[:], in_=bk[5][:, 0:64]), [BK[5]], ["X_tm"])
                    if want_f:
                        r1 = SB(st, "r1", [20, S], F32)
                        NP_ = SB(st, "NP", [20, 3, S], BF16)
                        P.op("dve", I("tensor_copy", out=NP_[:, 0, :], in_=negF[:]), ["negF"], ["NP"])
                        P.op("dve", I("tensor_tensor", out=r1[:], in0=negF[:], in1=NP_[:, 0, :], op=ALU.subtract), ["negF", "NP"], ["r1"])
                        P.op("dve", I("tensor_copy", out=NP_[:, 1, :], in_=r1[:]), ["r1"], ["NP"])
                        P.op("dve", I("tensor_tensor", out=r1[:], in0=r1[:], in1=NP_[:, 1, :], op=ALU.subtract), ["r1", "NP"], ["r1"])
                        P.op("dve", I("tensor_copy", out=NP_[:, 2, :], in_=r1[:]), ["r1"], ["NP"])
                        P.dma("sp", npscr.rearrange("p (r t) -> p r t", r=3), NP_[:], "st_np", ["NP"], ["npscr"])
                    P.barrier()

            with contextlib.ExitStack() as st2:
                hmT = SB(st2, "hmT", [128, KC, S], BF16)
                st2b = contextlib.ExitStack()
                Grow = SB(st2b, "Grow", [4, S + 1], F32)
                A_tm = SB(st2b, "A_tm", [128, 64], F32)
                X_tm = SB(st2b, "X_tm", [128, 64], F32)
                gates(True, True, Grow=Grow, A_tm=A_tm, X_tm=X_tm)

                with contextlib.ExitStack() as st:
                    qT = SB(st, "qT", [128, 2, S], BF16)
                    kT = SB(st, "kT", [128, 2, S], BF16)
                    ktm = SB(st, "ktm", [128, NB, 256], BF16)
                    vA = SB(st, "vA", [128, NB, 258], BF16)
                    vwa = SB(st, "vwa", [128, NB, 258], BF16)
                    STb = SB(st, "STb", [128, S], BF16)
                    oS = SB(st, "oS", [128, NB, 256], BF16)
                    Cf = SB(st, "Cf", [128, 2, 257], F32)
                    Cb = [SB(st, f"Cb{i}", [128, 2, 258], BF16) for i in range(2)]
                    sta = SB(st, "sta", [128, 12, 16], F32)
                    hmtm = [SB(st, f"hmtm{i}", [128, 256], BF16) for i in range(2)]
                    junk4 = SB(st, "junk4", [128, 256], BF16)
                    wq_ = [SB(st, f"wq{i}", [128, KC, 256], BF16) for i in range(2)]
                    wk_ = [SB(st, f"wk{i}", [128, KC, 256], BF16) for i in range(2)]
                    wv_ = [SB(st, f"wv{i}", [128, KC, 256], BF16) for i in range(2)]
                    Gbc = SB(st, "Gbc", [128, S + 1], F32)
                    DTa = SB(st, "DTa", [128, S], F32)
                    Eba = SB(st, "Eba", [128, S], F32)
                    tmpS = [SB(st, f"tmpS{i}", [128, 512], F32) for i in range(2)]

                    def ld_m(h):
                        i = h % 2
                        P.dma("pool", wq_[i][:], w_in_v[:, :, O_MQ + 256 * h:O_MQ + 256 * h + 256], f"ld_wq{i}", (), [f"wq{i}"])
                        P.dma("pool", wk_[i][:], w_in_v[:, :, O_MK + 256 * h:O_MK + 256 * h + 256], f"ld_wk{i}", (), [f"wk{i}"])
                        P.dma("pool", wv_[i][:], w_in_v[:, :, O_MV + 256 * h:O_MV + 256 * h + 256], f"ld_wv{i}", (), [f"wv{i}"])
                    ld_m(0)
                    P.op("dve", I("memset", vA[:, :, 256:257], 1.0), (), [f"vA{j}" for j in range(NB)])
                    WALL, DCY, EXA, SS, ND, MDEN, RR, VAR, FAC, TMP, DEN, FACP = range(12)

                    def post_block(h, j):
                        s2 = j % 2
                        js = slice(j * 128, (j + 1) * 128)
                        tb = 2 + s2
                        tp = bkb(tb)
                        act(hmtm[s2][:], oS[:, j, :], AF.Identity, [f"oS{j}", "staFP"], [f"hmtm{s2}"], scale=sta[:, FACP, j:j + 1])
                        P.op("pe", [I("transpose", out=tp[:, c * 128:(c + 1) * 128], in_=hmtm[s2][:, c * 128:(c + 1) * 128],
                                      identity=ident_bf[:]) for c in range(2)], [f"hmtm{s2}", "ident_bf"], [BK[tb]])
                        P.op("dve", I("tensor_copy", out=hmT[:, 2 * h:2 * h + 2, js], in_=tp[:, 0:256].rearrange("p (c t) -> p c t", t=128)),
                             [BK[tb]], ["hmT"])

                    pending_post = []
                    for h in range(4):
                        wq, wk, wv = wq_[h % 2], wk_[h % 2], wv_[h % 2]
                        wqk, wkk, wvk = f"wq{h % 2}", f"wk{h % 2}", f"wv{h % 2}"
                        if h + 1 < 4:
                            ld_m(h + 1)
                        for t5 in range(5):
                            c0, c1 = t5 * 512, min(S + 1, (t5 + 1) * 512)
                            gb_ = 6 + t5 % 2
                            mm(bk[gb_][:, 0:c1 - c0], [(sel[0:4, h * 128:(h + 1) * 128], Grow[0:4, c0:c1])], ["sel", "Grow"], [BK[gb_]])
                            P.op("dve", I("tensor_copy", out=Gbc[:, c0:c1], in_=bk[gb_][:, 0:c1 - c0]), [BK[gb_]], ["Gbc"])
                        g0v = Gbc[:, 0:S].rearrange("p (j t) -> p j t", t=128)[:, :, 0]
                        gEv = Gbc[:, 1:S + 1].rearrange("p (j t) -> p j t", t=128)[:, :, 127]
                        Av = A_tm[:].rearrange("p (j h) -> p j h", h=4)[:, :, h]
                        Xv = X_tm[:].rearrange("p (j h) -> p j h", h=4)[:, :, h]
                        P.op("dve", I("tensor_tensor", out=sta[:, TMP, :], in0=Av, in1=gEv, op=ALU.subtract), ["A_tm", "Gbc"], ["staT"])
                        act(sta[:, WALL, :], sta[:, TMP, :], AF.Exp, ["staT"], ["staW"])
                        P.op("dve", I("tensor_tensor", out=sta[:, ND, :], in0=g0v, in1=gEv, op=ALU.subtract), ["Gbc"], ["staN"])
                        act(sta[:, DCY, :], sta[:, ND, :], AF.Exp, ["staN"], ["staD"])
                        act(sta[:, EXA, :], Xv, AF.Exp, ["X_tm"], ["staE"])
                        exq = []
                        for j in range(NB):
                            js = slice(j * 128, (j + 1) * 128)
                            jg = slice(1 + j * 128, 1 + (j + 1) * 128)
                            exq.append((DTa[:, js], Gbc[:, jg], ["Gbc", "A_tm"], [f"DTa{j // 4}"], A_tm[:, j * 4 + h:j * 4 + h + 1]))
                            exq.append((Eba[:, js], Gbc[:, jg], ["Gbc"], ["Eba"], Gbc[:, j * 128:j * 128 + 1]))
                        pj = 0
                        for (wt, wkey, dst, dkey, scl) in ((wq, wqk, qT, "qT", 1.0 / 16), (wk, wkk, kT, "kT", 1.0)):
                            for c in range(2):
                                for t in range(4):
                                    ts_ = slice(t * 512, (t + 1) * 512)
                                    pb = 4 + pj % 2
                                    mm(bk[pb][:], [(wt[:, k, c * 128:(c + 1) * 128], hT[:, k, ts_]) for k in range(KC)], [wkey] + hT_keys(t), [BK[pb]])
                                    act(dst[:, c, ts_], bk[pb][:], AF.Identity, [BK[pb]], [dkey], scale=scl)
                                    for _ in range(2):
                                        o_, i_, r_, w_, b_ = exq.pop(0)
                                        act(o_, i_, AF.Exp, r_, w_, bias=b_, scale=-1.0)
                                    if pending_post:
                                        post_block(*pending_post.pop(0))
                                    pj += 1
                        for g4 in range(4):
                            gs = slice(g4 * 512, (g4 + 1) * 512)
                            P.op("pool", I("tensor_tensor", out=DTa[:, gs], in0=DTa[:, gs], in1=mask4[:], op=ALU.mult),
                                 [f"DTa{g4}", "mask4"], [f"DTm{g4}", f"DTa{g4}"])
                        for j2 in range(NB // 2):
                            pb = 4 + pj % 2
                            pj += 1
                            for jj in range(2):
                                j = j2 * 2 + jj
                                mm(bk[pb][:, jj * 256:(jj + 1) * 256], [(hT[:, k, j * 128:(j + 1) * 128], wv[:, k, :]) for k in range(KC)],
                                   [wvk, f"hT{j}"], [BK[pb]])
                            act(vA[:, j2 * 2:j2 * 2 + 2, 0:256], bk[pb][:].rearrange("p (j c) -> p j c", c=256), AF.Identity, [BK[pb]],
                                [f"vA{j2 * 2}", f"vA{j2 * 2 + 1}"])
                            for jj in range(2):
                                j = j2 * 2 + jj
                                P.op("dve", I("tensor_scalar", out=vwa[:, j, 0:257], in0=vA[:, j, 0:257], scalar1=sta[:, WALL, j:j + 1], scalar2=None, op0=ALU.mult),
                                     [f"vA{j}", "staW"], [f"vwa{j}"])
                        for g4 in range(4):
                            tb = 6 + g4 % 2
                            tp = bkb(tb)
                            fns = []
                            for jj in range(4):
                                j = g4 * 4 + jj
                                for c in range(2):
                                    fns.append(I("transpose", out=tp[:, jj * 256 + c * 128:jj * 256 + (c + 1) * 128],
                                                 in_=kT[:, c, j * 128:(j + 1) * 128], identity=ident_bf[:]))
                            P.op("pe", fns, ["kT", "ident_bf"], [BK[tb]])
                            P.op("dve", I("tensor_copy", out=ktm[:, g4 * 4:g4 * 4 + 4, :], in_=tp.rearrange("p (j c) -> p j c", c=256)),
                                 [BK[tb]], ["ktm"])
                        while pending_post:
                            post_block(*pending_post.pop(0))
                        for g4 in range(4):
                            gs = slice(g4 * 512, (g4 + 1) * 512)
                            fns = []
                            for jj in range(4):
                                j = g4 * 4 + jj
                                js = slice(j * 128, (j + 1) * 128)
                                for c in range(2):
                                    fns.append(I("matmul", bk[g4][:, jj * 128:(jj + 1) * 128], lhsT=kT[:, c, js], rhs=qT[:, c, js],
                                                 start=(c == 0), stop=(c == 1)))
                            P.op("pe", fns, ["kT", "qT"], [BK[g4]])
                            P.op("dve", I("tensor_tensor", out=STb[:, gs], in0=bk[g4][:], in1=DTa[:, gs], op=ALU.mult),
                                 [BK[g4], f"DTm{g4}"], [f"STb{g4}"])
                        for c in range(2):
                            P.op("dve", I("tensor_tensor", out=qT[:, c, :], in0=qT[:, c, :], in1=Eba[:], op=ALU.mult), ["qT", "Eba"], ["qT"])
                        P.op("dve", I("memset", Cf[:], 0.0), (), ["Cf0", "Cf1"])
                        P.op("dve", I("memset", Cb[0][:], 0.0), (), ["Cb0a", "Cb0b"])

                        def upd(j):
                            for c in range(2):
                                ub = 4 + (j % 2) * 2 + c
                                mm(bk[ub][:, 0:257], [(ktm[:, j, c * 128:(c + 1) * 128], vwa[:, j, 0:257])], ["ktm", f"vwa{j}"], [BK[ub]])
                        upd(0)
                        for j in range(NB):
                            s2 = j % 2
                            js = slice(j * 128, (j + 1) * 128)
                            if j + 1 < NB:
                                upd(j + 1)
                            ob = s2
                            mm(bk[ob][:, 0:257], [(STb[:, js], vA[:, j, 0:257])] + [(qT[:, c, js], Cb[s2][:, c, 0:257]) for c in range(2)],
                               [f"STb{j // 4}", f"vA{j}", "qT", f"Cb{s2}a", f"Cb{s2}b"], [BK[ob]])
                            for c in range(2):
                                ub = 4 + s2 * 2 + c
                                P.op("dve", I("scalar_tensor_tensor", out=Cf[:, c, :], in0=Cf[:, c, :], scalar=sta[:, DCY, j:j + 1], in1=bk[ub][:, 0:257],
                                              op0=ALU.mult, op1=ALU.add), [f"Cf{c}", "staD", BK[ub]], [f"Cf{c}"])
                            if j + 1 < NB:
                                act(Cb[1 - s2][:, 0, 0:257], Cf[:, 0, :], AF.Identity, ["Cf0"], [f"Cb{1 - s2}a"])
                                P.op("dve", I("tensor_copy", out=Cb[1 - s2][:, 1, 0:257], in_=Cf[:, 1, :]), ["Cf1"], [f"Cb{1 - s2}b"])
                            act(junk4[:], bk[ob][:, 0:256], AF.Square, [BK[ob]], ["junk4", "staS"], accum_out=sta[:, SS, j:j + 1])
                            act(oS[:, j, :], bk[ob][:, 0:256], AF.Identity, [BK[ob]], [f"oS{j}"])
                            act(sta[:, DEN, j:j + 1], bk[ob][:, 256:257], AF.Identity, [BK[ob]], ["staDen"])
                        denv = sta[:, DEN, :]
                        P.op("dve", I("tensor_scalar", out=sta[:, ND, :], in0=denv, scalar1=-1.0, scalar2=None, op0=ALU.mult), ["staDen"], ["staN"])
                        P.op("dve", I("tensor_tensor", out=sta[:, MDEN, :], in0=sta[:, ND, :], in1=sta[:, EXA, :], op=ALU.max), ["staN", "staE"], ["staM"])
                        P.op("dve", I("tensor_tensor", out=sta[:, MDEN, :], in0=sta[:, MDEN, :], in1=denv, op=ALU.max), ["staM", "staDen"], ["staM"])
                        P.op("dve", I("reciprocal", out=sta[:, RR, :], in_=sta[:, MDEN, :]), ["staM"], ["staR"])
                        P.op("dve", I("tensor_tensor", out=sta[:, VAR, :], in0=sta[:, SS, :], in1=sta[:, RR, :], op=ALU.mult), ["staS", "staR"], ["staV"])
                        P.op("dve", I("tensor_tensor", out=sta[:, VAR, :], in0=sta[:, VAR, :], in1=sta[:, RR, :], op=ALU.mult), ["staV", "staR"], ["staV"])
                        act(sta[:, VAR, :], sta[:, VAR, :], AF.Ln, ["staV"], ["staV"], bias=EPS, scale=1.0 / 256)
                        act(sta[:, VAR, :], sta[:, VAR, :], AF.Exp, ["staV"], ["staV"], scale=-0.5)
                        P.op("dve", I("tensor_tensor", out=sta[:, FACP, :], in0=sta[:, VAR, :], in1=sta[:, RR, :], op=ALU.mult), ["staV", "staR"], ["staFP"])
                        for j in range(NB):
                            post_block(h, j)
                    while pending_post:
                        post_block(*pending_post.pop(0))
                    P.barrier()
                st2b.close()

                hfT = SB(st2, "hfT", [128, KC, S], BF16)
                st3b = contextlib.ExitStack()

                with contextlib.ExitStack() as st:
                    qaug = [[SB(st, f"qaug{s}{i}", [128, S], BF16) for i in range(2)] for s in range(2)]
                    kaug = [[SB(st, f"kaug{s}{i}", [128, S], BF16) for i in range(2)] for s in range(2)]
                    vaug = [SB(st, f"vaug{s}", [128, NB, 192], BF16) for s in range(2)]
                    wq2_ = [SB(st, f"wq2{i}", [128, KC, 128], BF16) for i in range(2)]
                    wk2_ = [SB(st, f"wk2{i}", [128, KC, 128], BF16) for i in range(2)]
                    wv2_ = [SB(st, f"wv2{i}", [128, KC, 128], BF16) for i in range(2)]

                    def ld_f(p):
                        i = p % 2
                        P.dma("pool", wq2_[i][:], w_in_v[:, :, O_FQ + 128 * p:O_FQ + 128 * p + 128], f"ld_wq2{i}", (), [f"wq2{i}"])
                        P.dma("pool", wk2_[i][:], w_in_v[:, :, O_FK + 128 * p:O_FK + 128 * p + 128], f"ld_wk2{i}", (), [f"wk2{i}"])
                        P.dma("pool", wv2_[i][:], w_in_v[:, :, O_FV + 128 * p:O_FV + 128 * p + 128], f"ld_wv2{i}", (), [f"wv2{i}"])
                    ld_f(0)
                    sq = [SB(st, f"sq{i}", [128, 512], BF16) for i in range(2)]
                    rs = [SB(st, f"rs{i}", [128, 512], F32) for i in range(2)]
                    pT = [SB(st, f"pT{i}", [128, 512], BF16) for i in range(4)]
                    rden = [SB(st, f"rden{i}", [128, 512], F32) for i in range(2)]
                    for s_ in range(2):
                        for i in range(2):
                            P.op("dve", I("memset", qaug[s_][i][64:128, :], 0.0), (), [f"qaug{s_}{i}"])
                            P.op("dve", I("memset", kaug[s_][i][64:128, :], 0.0), (), [f"kaug{s_}{i}"])
                            P.op("dve", I("memset", qaug[s_][i][64:70, :], 1.0), (), [f"qaug{s_}{i}"])
                            P.op("dve", I("memset", kaug[s_][i][64:70, :], -1.0), (), [f"kaug{s_}{i}"])
                        P.op("dve", I("memset", vaug[s_][:, :, 64:128], 1.0), (), [f"vaug{s_}"])
                    cnt3 = {"sq": 0, "pj": 0}

                    def fox_proj(p):
                        for _ in fox_proj_gen(p):
                            pass

                    def fox_proj_gen(p):
                        s_ = p % 2
                        wq2, wk2, wv2 = wq2_[p % 2], wk2_[p % 2], wv2_[p % 2]
                        if p + 1 < 8:
                            ld_f(p + 1)
                        sched = {}

                        def at(tick, fn):
                            sched.setdefault(tick, []).append(fn)
                        u = 0
                        for (wt, wkey, aug, akey, wcol) in ((wq2, f"wq2{p % 2}", qaug, "qaug", fqw_s), (wk2, f"wk2{p % 2}", kaug, "kaug", fkw_s)):
                            for t in range(4):
                                ts_ = slice(t * 512, (t + 1) * 512)
                                pb = 5 + u % 2
                                i2 = u % 2
                                t0 = 8 * u

                                def s0c(ch, wt=wt, wkey=wkey, ts_=ts_, pb=pb, t=t):
                                    P.op("pe", [I("matmul", bk[pb][:], lhsT=wt[:, k, 0:128], rhs=hT[:, k, ts_], start=(k == 0), stop=(k == KC - 1))
                                                for k in (2 * ch, 2 * ch + 1)], [wkey] + hT_keys(t), [BK[pb]])

                                def s1(pb=pb, i2=i2):
                                    act(sq[i2][:], bk[pb][:], AF.Square, [BK[pb]], [f"sq{i2}"])

                                def s2(i2=i2):
                                    mm(bk[7][:], [(bdones[:], sq[i2][:])], ["bdones", f"sq{i2}"], [BK[7]])

                                def s3(i2=i2):
                                    act(rs[i2][:], bk[7][:], AF.Ln, [BK[7]], [f"rs{i2}"], bias=EPS, scale=1.0 / 64)
                                    act(rs[i2][:], rs[i2][:], AF.Exp, [f"rs{i2}"], [f"rs{i2}"], scale=-0.5)

                                def s4(aug=aug, akey=akey, wcol=wcol, ts_=ts_, pb=pb, i2=i2):
                                    for hh in range(2):
                                        pr = slice(hh * 64, hh * 64 + 64)
                                        P.op("dve", I("scalar_tensor_tensor", out=aug[s_][hh][0:64, ts_], in0=bk[pb][pr, :], scalar=wcol[pr, 0:1],
                                                      in1=rs[i2][pr, :], op0=ALU.mult, op1=ALU.mult),
                                             [BK[pb], f"rs{i2}", "fqw_s", "fkw_s"], [f"{akey}{s_}{hh}"])
                                for ch in range(4):
                                    at(t0 + ch, (lambda ch=ch, f=s0c: f(ch)))
                                for si, fn in enumerate((s1, s2, s3, s4)):
                                    at(t0 + 6 + 3 * si, fn)
                                u += 1
                        for g4, t0 in enumerate((64, 72, 76, 80)):
                            pb = 5 + g4 % 2

                            def v0(jj, g4=g4, pb=pb):
                                j = g4 * 4 + jj
                                mm(bk[pb][:, jj * 128:(jj + 1) * 128], [(hT[:, k, j * 128:(j + 1) * 128], wv2[:, k, 0:128]) for k in range(KC)],
                                   [f"wv2{p % 2}", f"hT{j}"], [BK[pb]])

                            def v1(g4=g4, pb=pb):
                                pv = bk[pb][:].rearrange("p (j c) -> p j c", c=128)
                                act(vaug[s_][:, g4 * 4:g4 * 4 + 4, 0:64], pv[:, :, 0:64], AF.Identity, [BK[pb]], [f"vaug{s_}"])
                                act(vaug[s_][:, g4 * 4:g4 * 4 + 4, 128:192], pv[:, :, 64:128], AF.Identity, [BK[pb]], [f"vaug{s_}"])
                            for jj in range(4):
                                at(t0 + jj, (lambda jj=jj, f=v0: f(jj)))
                            at(t0 + 6, v1)

                        def dmas():
                            for hh in range(2):
                                h = 2 * p + hh
                                for r3 in range(3):
                                    P.dma("sp", kaug[s_][hh][64 + r3:65 + r3, :], npscr[4 + h:5 + h, r3 * S:(r3 + 1) * S], f"ld_ka{s_}{hh}", ["npscr"], [f"kaug{s_}{hh}"])
                                    P.dma("sp", qaug[s_][hh][67 + r3:68 + r3, :], npscr[4 + h:5 + h, r3 * S:(r3 + 1) * S], f"ld_qa{s_}{hh}", ["npscr"], [f"qaug{s_}{hh}"])
                        at(1, dmas)
                        for tick in range(max(sched) + 1):
                            for fn in sched.get(tick, ()):
                                fn()
                            yield

                    cnt_a = {"s": 0, "p": 0, "o": 0}

                    def fox_att(p, hh, gen=None):
                        s_ = p % 2
                        qa, ka, va = qaug[s_][hh], kaug[s_][hh], vaug[s_]
                        qk, kk, vk = f"qaug{s_}{hh}", f"kaug{s_}{hh}", f"vaug{s_}"
                        vlo = 0 if hh == 0 else 64
                        ol = 0 if hh == 0 else 64
                        dl = 64 if hh == 0 else 0
                        obs = []
                        for g in range(4):
                            obs.append(3 + cnt_a["o"] % 2)
                            cnt_a["o"] += 1
                        tiles = [(g, i) for g in range(4) for i in range(4 * g + 4)]

                        def score(g, i):
                            r = i - 4 * g
                            qo = 128 * r if r >= 0 else 0
                            sb_ = cnt_a["s"] % 3
                            cnt_a["s"] += 1
                            fns = [I("matmul", bk[sb_][:, qo:512], lhsT=ka[:, i * 128:(i + 1) * 128], rhs=qa[:, 512 * g + qo:512 * g + 512],
                                     start=True, stop=(r < 0))]
                            if r >= 0:
                                fns.append(I("matmul", bk[sb_][:, qo:qo + 128], lhsT=ident_bf[:], rhs=maskb[:], start=False, stop=True))
                            P.op("pe", fns, [kk, qk, "ident_bf", "maskb"], [BK[sb_]])
                            return (g, i, qo, sb_)

                        scq = [score(*tiles[t]) for t in range(2)]
                        for t in range(len(tiles)):
                            if t + 2 < len(tiles):
                                scq.append(score(*tiles[t + 2]))
                            g, i, qo, sb_ = scq.pop(0)
                            ob, nkb = obs[g], 4 * g + 4
                            pi = cnt_a["p"] % 4
                            cnt_a["p"] += 1
                            act(pT[pi][:, qo:512], bk[sb_][:, qo:512], AF.Exp, [BK[sb_]], [f"pT{pi}"])
                            P.op("pe", I("matmul", bk[ob][:, qo:512], lhsT=va[:, i, vlo:vlo + 128], rhs=pT[pi][:, qo:512],
                                         start=(i == 0), stop=(i == nkb - 1)), [vk, f"pT{pi}"], [BK[ob]])
                            if gen is not None:
                                next(gen, None)
                            if i == nkb - 1:
                                rd = rden[g % 2]
                                P.op("dve", I("reciprocal", out=rd[dl:dl + 64, :], in_=bk[ob][dl:dl + 64, :]), [BK[ob]], [f"rden{g % 2}"])
                                P.op("dve", I("tensor_tensor", out=hfT[ol:ol + 64, p, g * 512:(g + 1) * 512], in0=bk[ob][ol:ol + 64, :],
                                              in1=rd[dl:dl + 64, :], op=ALU.mult), [BK[ob], f"rden{g % 2}"], ["hfT"])

                    fox_proj(0)
                    for p in range(8):
                        if fox_limit <= 2 * p:
                            break
                        gen = fox_proj_gen(p + 1) if p + 1 < 8 else None
                        fox_att(p, 0, gen)
                        if fox_limit <= 2 * p + 1:
                            break
                        fox_att(p, 1, gen)
                        if gen is not None:
                            for _ in gen:
                                pass
                    P.barrier()
                st3b.close()

                with contextlib.ExitStack() as st:
                    wo = [SB(st, f"wo{i}", [128, KC, 512], BF16) for i in range(2)]
                    sg = [SB(st, f"sg{i}", [128, 512], F32) for i in range(2)]
                    n4 = 0
                    for half in range(2):
                        P.dma("pool", wo[half][:], w_in_v[:, :, O_MO + 512 * half:O_MO + 512 * half + 512], f"ld_wo{half}", (), [f"wo{half}"])
                    for c in range(KC):
                        half, cl = c // 4, (c % 4) * 128
                        for t in range(4):
                            ts_ = slice(t * 512, (t + 1) * 512)
                            pb = n4 % 4
                            i2 = n4 % 2
                            n4 += 1
                            mm(bk[pb][:], [(wo[half][:, k, cl:cl + 128], hT[:, k, ts_]) for k in range(KC)], [f"wo{half}"] + hT_keys(t), [BK[pb]])
                            act(sg[i2][:], bk[pb][:], AF.Sigmoid, [BK[pb]], [f"sg{i2}"])
                            P.op("dve", I("scalar_tensor_tensor", out=hmT[:, c, ts_], in0=hmT[:, c, ts_], scalar=mnw[:, c:c + 1],
                                                                                             in1=sg[i2][:], op0=ALU.mult, op1=ALU.mult),
                                 [f"hmTc{c}", "mnw", f"sg{i2}"], [f"hmTc{c}"])
                    P.barrier()

                st5 = contextlib.ExitStack()
                mgT = SB(st5, "mgT", [128, KC, S], BF16)
                wot = SB(st5, "wot", [128, KC, D], BF16)
                with contextlib.ExitStack() as st:
                    wsl = {nm: [SB(st, f"w5{nm}{i}", [128, KC, 256], BF16) for i in range(2)] for nm in ("bm", "bf", "ga", "gb")}
                    sga = [SB(st, "sga0", [128, 512], F32)] * 2
                    sgb = [SB(st, "sgb0", [128, 512], F32)] * 2
                    t1 = [SB(st, f"t1{i}", [128, 512], F32) for i in range(2)]
                    t2 = [SB(st, f"t2{i}", [128, 512], F32) for i in range(2)]
                    srcs = {"bm": (w_bm_v, 0), "bf": (w_bf_v, 0), "ga": (w_in_v, O_GA), "gb": (w_in_v, O_GB)}
                    n5 = 0
                    def ld_5(c2):
                        sl = c2 % 2
                        for nm in ("bm", "bf", "ga", "gb"):
                            src, off = srcs[nm]
                            P.dma("pool", wsl[nm][sl][:], src[:, :, off + 256 * c2:off + 256 * c2 + 256], f"ld_w5{nm}{sl}", (), [f"w5{nm}{sl}"])
                    ld_5(0)
                    for c2 in range(4):
                        sl = c2 % 2
                        if c2 + 1 < 4:
                            ld_5(c2 + 1)
                        else:
                            for half in range(2):
                                P.dma("pool", wot[:, :, half * 512:(half + 1) * 512], w_out_v[:, :, half * 512:(half + 1) * 512], "ld_wot", (), ["wot"])
                        for cc in range(2):
                            c = c2 * 2 + cc
                            cl = cc * 128
                            for t in range(4):
                                ts_ = slice(t * 512, (t + 1) * 512)
                                i2 = n5 % 2
                                pb0 = (n5 % 2) * 4
                                n5 += 1
                                mm(bk[pb0][:], [(wsl["bm"][sl][:, k, cl:cl + 128], hmT[:, k, ts_]) for k in range(KC)], [f"w5bm{sl}", "hmT"], [BK[pb0]])
                                mm(bk[pb0 + 1][:], [(wsl["bf"][sl][:, k, cl:cl + 128], hfT[:, k, ts_]) for k in range(KC)], [f"w5bf{sl}", "hfT"], [BK[pb0 + 1]])
                                mm(bk[pb0 + 2][:], [(wsl["ga"][sl][:, k, cl:cl + 128], hT[:, k, ts_]) for k in range(KC)], [f"w5ga{sl}"] + hT_keys(t), [BK[pb0 + 2]])
                                mm(bk[pb0 + 3][:], [(wsl["gb"][sl][:, k, cl:cl + 128], hT[:, k, ts_]) for k in range(KC)], [f"w5gb{sl}"] + hT_keys(t), [BK[pb0 + 3]])
                                act(sga[i2][:], bk[pb0 + 2][:], AF.Sigmoid, [BK[pb0 + 2]], ["sga"])
                                act(sgb[i2][:], bk[pb0 + 3][:], AF.Sigmoid, [BK[pb0 + 3]], ["sgb"])
                                P.op("dve", I("tensor_tensor", out=t1[i2][:], in0=bk[pb0][:], in1=sga[i2][:], op=ALU.mult),
                                     [BK[pb0], "sga"], [f"t1{i2}"])
                                P.op("dve", I("tensor_tensor", out=t2[i2][:], in0=bk[pb0 + 1][:], in1=sgb[i2][:], op=ALU.mult),
                                     [BK[pb0 + 1], "sgb"], [f"t2{i2}"])
                                P.op("dve", I("tensor_tensor", out=mgT[:, c, ts_], in0=t1[i2][:], in1=t2[i2][:], op=ALU.add),
                                     [f"t1{i2}", f"t2{i2}"], ["mgT"])
                    P.barrier()

                with contextlib.ExitStack() as st:
                    g1bc = SB(st, "g1bc", [128, D], F32)
                    xs = [SB(st, f"xr{i}", [128, D], F32) for i in range(2)]
                    x1 = [SB(st, f"x1{i}", [128, D], F32) for i in range(3)]
                    build_gate_bc(g1bc, "g1bc", b, 0)

                    def pre2(j):
                        sl = j % 2
                        P.dma("sp", xs[sl][:], x[b, j * 128:(j + 1) * 128, :], f"ld_xr{sl}", (), [f"xr{sl}"])
                        for hf in range(2):
                            pb = 2 * sl + hf
                            cs = slice(hf * 512, (hf + 1) * 512)
                            mm(bk[pb][:], [(mgT[:, k, j * 128:(j + 1) * 128], wot[:, k, cs]) for k in range(KC)], ["mgT", "wot"], [BK[pb]])

                    def fin2(j):
                        sl = j % 2
                        for hf in range(2):
                            pb = 2 * sl + hf
                            cs = slice(hf * 512, (hf + 1) * 512)
                            P.op("dve", I("tensor_tensor", out=x1[j % 3][:, cs], in0=bk[pb][:], in1=g1bc[:, cs], op=ALU.mult),
                                 [BK[pb], "g1bc"], [f"x1{j % 3}"])
                        P.op("dve", I("tensor_tensor", out=x1[j % 3][:], in0=x1[j % 3][:], in1=xs[sl][:], op=ALU.add), [f"x1{j % 3}", f"xr{sl}"], [f"x1{j % 3}"])
                        P.dma("sp", out[b, j * 128:(j + 1) * 128, :], x1[j % 3][:], f"st_x1{j % 3}", [f"x1{j % 3}"], [f"out{j}"])
                        return x1[j % 3][:], [f"x1{j % 3}"]
                    blk2 = norm_to_hT(st, b, pre2, fin2, a2, 24, "b")
                    for j in range(NB + 1):
                        blk2(j)
                    P.barrier()
                st5.close()
            if stop_after <= 5:
                continue

            with contextlib.ExitStack() as st:
                wdn = SB(st, "wdn", [128, NF, D], BF16)
                g2bc = SB(st, "g2bc", [128, D], F32)
                gT = SB(st, "gT", [128, NF, 1024], BF16)
                wug = [SB(st, f"wug{i}", [128, KC, 256], BF16) for i in range(2)]
                wuv = [SB(st, f"wuv{i}", [128, KC, 256], BF16) for i in range(2)]
                ug = [SB(st, f"ug{i}", [128, 1026], F32) for i in range(2)]
                uv = [SB(st, f"uv{i}", [128, 1026], F32) for i in range(2)]
                cg = [SB(st, f"cg{i}", [128, 512], F32) for i in range(3)]
                cv = [SB(st, f"cv{i}", [128, 512], F32) for i in range(3)]
                hal = SB(st, "hal", [128, 44, 2], F32)
                xr = [SB(st, f"xq{i}", [128, D], F32) for i in range(2)]
                ot = [SB(st, f"ot{i}", [128, D], F32) for i in range(2)]
                def ld_u(u):
                    f2_, sl_ = u % (NF // 2), u % 2
                    P.dma("pool", wug[sl_][:], w_up_v[:, :, 256 * f2_:256 * f2_ + 256], f"ld_wug{sl_}", (), [f"wug{sl_}"])
                    P.dma("pool", wuv[sl_][:], w_up_v[:, :, FF + 256 * f2_:FF + 256 * f2_ + 256], f"ld_wuv{sl_}", (), [f"wuv{sl_}"])
                ld_u(0)
                for f4 in range(0, NF, 4):
                    n_ = min(4, NF - f4)
                    P.dma("pool", wdn[:, f4:f4 + n_, :], w_down_v[:, f4:f4 + n_, :], "ld_wdn", (), ["wdn"])
                build_gate_bc(g2bc, "g2bc", b, 1)
                P.op("dve", I("memset", hal[:], 0.0), (), ["hal"])
                n6 = 0
                pend6 = []
                nu6 = [0]
                for half in range(2):
                    for f2 in range(NF // 2):
                        u_ = half * (NF // 2) + f2
                        sl = u_ % 2
                        if u_ + 1 < NF:
                            ld_u(u_ + 1)
                        for ff_ in range(2):
                            f = f2 * 2 + ff_
                            fl = ff_ * 128
                            us = n6 % 2
                            n6 += 1
                            for (ubuf, ukey, hidx) in ((ug[us], f"ug{us}", f), (uv[us], f"uv{us}", 22 + f)):
                                act(ubuf[:, 0:2], hal[:, hidx, :], AF.Identity, ["hal"], [ukey])
                            for tt in range(2):
                                t = half * 2 + tt
                                ts_ = slice(t * 512, (t + 1) * 512)
                                i2 = nu6[0] % 3
                                pbg, pbv = 2 * i2, 2 * i2 + 1
                                nu6[0] += 1
                                mm(bk[pbg][:], [(wug[sl][:, k, fl:fl + 128], hT[:, k, ts_]) for k in range(KC)], [f"wug{sl}"] + hT_keys(t), [BK[pbg]])
                                mm(bk[pbv][:], [(wuv[sl][:, k, fl:fl + 128], hT[:, k, ts_]) for k in range(KC)], [f"wuv{sl}"] + hT_keys(t), [BK[pbv]])
                                for (pb, ubuf, ukey, cbuf, ckey, widx) in ((pbg, ug[us], f"ug{us}", cg[i2], f"cg{i2}", f), (pbv, uv[us], f"uv{us}", cv[i2], f"cv{i2}", 22 + f)):
                                    u0 = 2 + tt * 512
                                    act(ubuf[:, u0:u0 + 512], bk[pb][:], AF.Identity, [BK[pb]], [ukey])
                                    act(cbuf[:], bk[pb][:], AF.Identity, [BK[pb], "convw", "convb"], [ckey], bias=convb[:, widx:widx + 1], scale=convw[:, widx, 2:3])
                                    P.op("dve", I("scalar_tensor_tensor",
                                        out=cbuf[:], in0=ubuf[:, u0 - 1:u0 + 511], scalar=convw[:, widx, 1:2], in1=cbuf[:], op0=ALU.mult, op1=ALU.add),
                                        [ukey, ckey, "convw"], [ckey])
                                    P.op("dve", I("scalar_tensor_tensor",
                                        out=cbuf[:], in0=ubuf[:, u0 - 2:u0 + 510], scalar=convw[:, widx, 0:1], in1=cbuf[:], op0=ALU.mult, op1=ALU.add),
                                        [ukey, ckey, "convw"], [ckey])
                                if pend6:
                                    pend6.pop()()

                                def fin(i2=i2, f=f, tt=tt):
                                    act(cg[i2][:], cg[i2][:], AF.Silu, [f"cg{i2}"], [f"cg{i2}"])
                                    P.op("dve", I("tensor_tensor", out=gT[:, f, tt * 512:(tt + 1) * 512], in0=cg[i2][:], in1=cv[i2][:], op=ALU.mult),
                                         [f"cg{i2}", f"cv{i2}"], ["gT"])
                                pend6.append(fin)
                            for (ubuf, ukey, hidx) in ((ug[us], f"ug{us}", f), (uv[us], f"uv{us}", 22 + f)):
                                act(hal[:, hidx, :], ubuf[:, 1024:1026], AF.Identity, [ukey], ["hal"])
                    if pend6:
                        pend6.pop()()
                    for jj in range(8):
                        j = half * 8 + jj
                        sl = jj % 2
                        P.dma("sp", xr[sl][:], out[b, j * 128:(j + 1) * 128, :], f"ld_xq{sl}", [f"out{j}"], [f"xq{sl}"])
                        for hf in range(2):
                            pb = 6 + hf
                            cs = slice(hf * 512, (hf + 1) * 512)
                            mm(bk[pb][:], [(gT[:, f, jj * 128:(jj + 1) * 128], wdn[:, f, cs]) for f in range(NF)], ["gT", "wdn"], [BK[pb]])
                            P.op("dve", I("tensor_tensor", out=ot[sl][:, cs], in0=bk[pb][:], in1=g2bc[:, cs], op=ALU.mult),
                                 [BK[pb], "g2bc"], [f"ot{sl}"])
                        P.op("dve", I("tensor_tensor", out=ot[sl][:], in0=ot[sl][:], in1=xr[sl][:], op=ALU.add), [f"ot{sl}", f"xq{sl}"], [f"ot{sl}"])
                        P.dma("sp", out[b, j * 128:(j + 1) * 128, :], ot[sl][:], f"st_ot{sl}", [f"ot{sl}"], [f"out{j}"])
                P.barrier()
        P.barrier()
        P.run()
    return nc


def _prep_inputs(inp, core):
    f = lambda a: np.ascontiguousarray(np.asarray(a, dtype=np.float32))
    b0 = 2 * core
    c = np.asarray(inp["c"], dtype=np.float32)[b0:b0 + 2]

    def pk(v, n):
        return f(np.asarray(v, dtype=np.float32).reshape(n, 128).T)
    conv_w = np.asarray(inp["conv_w"], dtype=np.float32)[0]
    m = {
        "x": f(np.asarray(inp["x"])[b0:b0 + 2]),
        "cT": f(c.T.reshape(KC, 128, 2).transpose(1, 0, 2)),
        "w_ada": f(inp["w_ada"][0]),
        "b_ada": f(inp["b_ada"][0]),
        "b_adaT": pk(inp["b_ada"][0], 48),
        "n1wT": pk(inp["norm1_w"][0], KC),
        "n2wT": pk(inp["norm2_w"][0], KC),
        "mnwT": pk(inp["mlstm_norm_w"][0], KC),
        "w_in": f(inp["w_in"][0]),
        "b_i": f(np.asarray(inp["b_mlstm_i"][0]).reshape(4, 1)),
        "b_f20": f(np.concatenate([np.asarray(inp["b_mlstm_f"][0]), np.asarray(inp["b_fox_f"][0])]).reshape(20, 1)),
        "fqw": f(np.tile(np.asarray(inp["fox_q_norm_w"][0]), 2).reshape(128, 1)),
        "fkw": f(np.tile(np.asarray(inp["fox_k_norm_w"][0]), 2).reshape(128, 1)),
        "w_bm": f(inp["w_branch_mlstm"][0]),
        "w_bf": f(inp["w_branch_fox"][0]),
        "w_out": f(inp["w_out"][0]),
        "w_up": f(inp["w_up"][0]),
        "convT": f(conv_w.reshape(3, 44, 128).transpose(2, 1, 0)),
        "convbT": pk(inp["conv_b"][0], 44),
        "w_down": f(inp["w_down"][0]),
    }
    return m


def kernel(**inputs):
    nc = build_program()
    in_maps = [_prep_inputs(inputs, i) for i in range(NCORES)]
    res = run_bass_kernel_spmd(nc, in_maps, core_ids=list(range(NCORES)))
    return np.concatenate([np.asarray(r["out"]) for r in res.results], axis=0).astype(np.float32)
```

```python
import contextlib
import os
import numpy as np
import concourse.bass as bass
import concourse.mybir as mybir
from concourse.bass_utils import run_bass_kernel_spmd

F32 = mybir.dt.float32
BF16 = mybir.dt.bfloat16
AF = mybir.ActivationFunctionType
ALU = mybir.AluOpType

ENGS = ("pe", "act", "dve", "pool", "sp")
NCORES = 8
S = 2048
NB = 16
D = 1024
KC = 8
FF = 2816
NF = 22
EPS = 1e-6
O_MQ, O_MK, O_MV, O_MO, O_MI, O_MF = 0, 1024, 2048, 3072, 4096, 4100
O_FQ, O_FK, O_FV, O_FF, O_GA, O_GB = 4104, 5128, 6152, 7176, 7192, 8216


class Prog:
    def __init__(self, nc, stack):
        self.nc = nc
        self.q = {e: [] for e in ENGS}
        self.cnt = {}
        self.sem = {}
        self.stack = stack
        for e in ENGS:
            self.sem[e] = stack.enter_context(nc.semaphore("s_" + e))
            self.cnt[e] = 0
        self.waited = {e: {} for e in ENGS}
        self.last_w = {}
        self.readers = {}

    def dma_sem(self, key):
        if key not in self.sem:
            self.sem[key] = self.stack.enter_context(self.nc.semaphore("d_" + key))
            self.cnt[key] = 0
        return key

    def _deps(self, eng, reads, writes):
        deps = {}

        def add(d):
            if d is None:
                return
            k, v = d
            if deps.get(k, 0) < v:
                deps[k] = v

        for r in reads:
            add(self.last_w.get(r))
            if r.startswith("bk"):
                for k, v in self.readers.get(r, {}).items():
                    if k != eng:
                        add((k, v))
        for w in writes:
            add(self.last_w.get(w))
            for k, v in self.readers.get(w, {}).items():
                add((k, v))
        return deps

    def _emit_waits(self, eng, deps):
        for k, v in deps.items():
            if self.waited[eng].get(k, 0) >= v:
                continue
            self.waited[eng][k] = v
            sem = self.sem[k]
            self.q[eng].append(lambda e, sem=sem, v=v: e.wait_ge(sem, v))

    def _mark(self, key, val, reads, writes):
        for r in reads:
            self.readers.setdefault(r, {})[key] = val
        for w in writes:
            self.last_w[w] = (key, val)
            self.readers[w] = {}

    def op(self, eng, fns, reads=(), writes=()):
        if not isinstance(fns, (list, tuple)):
            fns = [fns]
        self._emit_waits(eng, self._deps(eng, reads, writes))
        sem = self.sem[eng]
        self.cnt[eng] += 1
        n = self.cnt[eng]
        for f in fns[:-1]:
            self.q[eng].append(f)
        last = fns[-1]
        self.q[eng].append(lambda e, last=last, sem=sem: last(e).then_inc(sem, 1))
        self._mark(eng, n, reads, writes)
        return n

    def dma(self, qeng, out, in_, semkey, reads=(), writes=()):
        self.dma_sem(semkey)
        self._emit_waits(qeng, self._deps(None, reads, writes))
        sem = self.sem[semkey]
        self.cnt[semkey] += 16
        v = self.cnt[semkey]
        self.q[qeng].append(lambda e, out=out, in_=in_, sem=sem: e.dma_start(out=out, in_=in_).then_inc(sem, 16))
        self._mark(semkey, v, reads, writes)
        return v

    def barrier(self):
        allk = {k: v for k, v in self.cnt.items() if v > 0}
        for e in ENGS:
            self._emit_waits(e, dict(allk))
        self.last_w = {}
        self.readers = {}

    def run(self):
        nc = self.nc
        with nc.Block() as block:
            @block.tensor
            def _(e):
                for f in self.q["pe"]:
                    f(e)

            @block.scalar
            def _(e):
                for f in self.q["act"]:
                    f(e)

            @block.vector
            def _(e):
                for f in self.q["dve"]:
                    f(e)

            @block.gpsimd
            def _(e):
                for f in self.q["pool"]:
                    f(e)

            @block.sync
            def _(e):
                for f in self.q["sp"]:
                    f(e)


def build_program(nseq=2, stop_after=99, dbg=None, fox_limit=99):
    nc = bass.Bass("TRN2", target_bir_lowering=False)

    def din(name, shape):
        return nc.dram_tensor(name, list(shape), F32, kind="ExternalInput").ap()

    x = din("x", [2, S, D])
    cT = din("cT", [128, KC, 2])
    w_ada = din("w_ada", [D, 6 * D])
    b_ada = din("b_ada", [6 * D])
    b_adaT = din("b_adaT", [128, 48])
    n1wT = din("n1wT", [128, KC])
    n2wT = din("n2wT", [128, KC])
    mnwT = din("mnwT", [128, KC])
    w_in = din("w_in", [D, 9240])
    b_i = din("b_i", [4, 1])
    b_f20 = din("b_f20", [20, 1])
    fqw = din("fqw", [128, 1])
    fkw = din("fkw", [128, 1])
    w_bm = din("w_bm", [D, D])
    w_bf = din("w_bf", [D, D])
    w_out = din("w_out", [D, D])
    w_up = din("w_up", [D, 2 * FF])
    convT = din("convT", [128, 44, 3])
    convbT = din("convbT", [128, 44])
    w_down = din("w_down", [FF, D])
    out = nc.dram_tensor("out", [2, S, D], F32, kind="ExternalOutput").ap()
    npscr = nc.dram_tensor("npscr", [20, 3 * S], BF16).ap()
    dbg_t = None
    if dbg is not None:
        dbg_t = nc.dram_tensor("dbg", list(dbg), F32, kind="ExternalOutput").ap()

    w_in_v = w_in.rearrange("(k p) n -> p k n", p=128)
    w_ada_v = w_ada.rearrange("(k p) n -> p k n", p=128)
    w_bm_v = w_bm.rearrange("(k p) n -> p k n", p=128)
    w_bf_v = w_bf.rearrange("(k p) n -> p k n", p=128)
    w_out_v = w_out.rearrange("(k p) n -> p k n", p=128)
    w_up_v = w_up.rearrange("(k p) n -> p k n", p=128)
    w_down_v = w_down.rearrange("(k p) n -> p k n", p=128)

    with contextlib.ExitStack() as st0:
        P = Prog(nc, st0)

        uid = [0]

        def SB(st, name, shape, dt):
            uid[0] += 1
            return st.enter_context(nc.sbuf_tensor(f"{name}_{uid[0]}", list(shape), dt))

        bk = [st0.enter_context(nc.psum_tensor(f"bk{i}", [128, 512], F32)) for i in range(8)]
        BK = [f"bk{i}" for i in range(8)]

        def bkb(i):
            return bk[i][:].bitcast(BF16)

        def I(name, *a, **kw):
            return lambda e: getattr(e, name)(*a, **kw)

        def mm(out_ap, pairs, reads, writes):
            n = len(pairs)
            fns = []
            for i, (l, r) in enumerate(pairs):
                fns.append(I("matmul", out_ap, lhsT=l, rhs=r, start=(i == 0), stop=(i == n - 1)))
            P.op("pe", fns, reads, writes)

        def act(out_ap, in_ap, func, reads, writes, bias=None, scale=None, accum_out=None):
            kw = {}
            if bias is not None:
                kw["bias"] = bias
            if scale is not None:
                kw["scale"] = scale
            if accum_out is not None:
                kw["accum_out"] = accum_out
            P.op("act", I("activation", out=out_ap, in_=in_ap, func=func, **kw), reads, writes)

        ident_bf = SB(st0, "ident_bf", [128, 128], BF16)
        ident_f = SB(st0, "ident_f", [128, 128], F32)
        maskb = SB(st0, "maskb", [128, 128], BF16)
        mask01 = SB(st0, "mask01", [128, 128], F32)
        bdones = SB(st0, "bdones", [128, 128], BF16)
        sel = SB(st0, "sel", [4, 512], F32)
        tmpc = SB(st0, "tmpc", [128, 512], F32)
        mask4 = SB(st0, "mask4", [128, 512], F32)

        P.op("pool", I("memset", tmpc[:], 1.0), (), ["tmpc"])
        P.op("pool", I("affine_select", out=ident_f[:], in_=tmpc[:, 0:128], pattern=[[-1, 128]], compare_op=ALU.is_equal,
                                               fill=0.0, base=0, channel_multiplier=1), ["tmpc"], ["ident_f"])
        P.op("pool", I("tensor_copy", out=ident_bf[:], in_=ident_f[:]), ["ident_f"], ["ident_bf"])
        P.op("pool", I("affine_select", out=mask01[:], in_=tmpc[:, 0:128], pattern=[[1, 128]], compare_op=ALU.is_ge,
                                               fill=0.0, base=0, channel_multiplier=-1), ["tmpc"], ["mask01"])
        for i4 in range(4):
            P.op("pool", I("tensor_copy", out=mask4[:, i4 * 128:(i4 + 1) * 128], in_=mask01[:]), ["mask01"], ["mask4"])
        P.op("pool", I("tensor_scalar", out=maskb[:], in0=mask01[:], scalar1=-1.0, scalar2=30000.0,
                                               op0=ALU.add, op1=ALU.mult), ["mask01"], ["maskb"])
        P.op("pool", I("memset", bdones[:], 0.0), (), ["bdones"])
        P.op("pool", I("memset", bdones[0:64, 0:64], 1.0), (), ["bdones"])
        P.op("pool", I("memset", bdones[64:128, 64:128], 1.0), (), ["bdones"])
        P.op("pool", I("affine_select", out=sel[:].rearrange("p (h j) -> p h j", j=128),
                                               in_=tmpc[0:4, :].rearrange("p (h j) -> p h j", j=128),
                                               pattern=[[-1, 4], [0, 128]], compare_op=ALU.is_equal,
                                               fill=0.0, base=0, channel_multiplier=1), ["tmpc"], ["sel"])

        cT_sb = SB(st0, "cT_sb", [128, KC, 2], F32)
        sc = SB(st0, "sc", [128, KC, 2], BF16)
        badaT = SB(st0, "badaT", [128, 48], F32)
        n1w = SB(st0, "n1w", [128, KC], F32)
        n2w = SB(st0, "n2w", [128, KC], F32)
        mnw = SB(st0, "mnw", [128, KC], F32)
        bi = SB(st0, "bi", [4, 1], F32)
        nbf20 = SB(st0, "nbf20", [20, 1], F32)
        fqw_s = SB(st0, "fqw_s", [128, 1], F32)
        fkw_s = SB(st0, "fkw_s", [128, 1], F32)
        convw = SB(st0, "convw", [128, 44, 3], F32)
        convb = SB(st0, "convb", [128, 44], F32)
        adaT = SB(st0, "adaT", [128, 48, 2], F32)
        a1 = SB(st0, "a1", [128, KC, 2], F32)
        a2 = SB(st0, "a2", [128, KC, 2], F32)
        grow = SB(st0, "grow", [2, 2048], F32)
        hT = SB(st0, "hT", [128, KC, S], BF16)

        for (t_, src, key) in [(cT_sb, cT, "cT_sb"), (badaT, b_adaT, "badaT"), (n1w, n1wT, "n1w"), (n2w, n2wT, "n2w"),
                               (mnw, mnwT, "mnw"), (bi, b_i, "bi"), (nbf20, b_f20, "nbf20"), (fqw_s, fqw, "fqw_s"),
                               (fkw_s, fkw, "fkw_s"), (convw, convT, "convw"), (convb, convbT, "convb")]:
            P.dma("sp", t_[:], src, "ld_" + key, (), [key])
        P.op("dve", I("tensor_scalar", out=nbf20[:], in0=nbf20[:], scalar1=-1.0, scalar2=None, op0=ALU.mult), ["nbf20"], ["nbf20"])
        P.op("dve", I("tensor_scalar", out=fqw_s[:], in0=fqw_s[:], scalar1=0.125, scalar2=None, op0=ALU.mult), ["fqw_s"], ["fqw_s"])

        with contextlib.ExitStack() as st:
            bgate = SB(st, "bgate", [2, 2048], F32)
            wa = [SB(st, f"wa{i}", [128, KC, 1536], BF16) for i in range(2)]
            for r in range(2):
                P.dma("sp", bgate[r:r + 1, 0:1024], b_ada[2048:3072].rearrange("(o n) -> o n", o=1), "ld_bg", (), ["bgate"])
                P.dma("sp", bgate[r:r + 1, 1024:2048], b_ada[5120:6144].rearrange("(o n) -> o n", o=1), "ld_bg", (), ["bgate"])
            act(sc[:], cT_sb[:], AF.Silu, ["cT_sb"], ["sc"])
            for pc in range(4):
                sl = pc % 2
                P.dma("pool", wa[sl][:], w_ada_v[:, :, pc * 1536:(pc + 1) * 1536], f"ld_wa{sl}", (), [f"wa{sl}"])
                for ch in range(12):
                    gch = pc * 12 + ch
                    mm(bk[0][:, gch * 2:gch * 2 + 2],
                       [(wa[sl][:, k, ch * 128:(ch + 1) * 128], sc[:, k, :]) for k in range(KC)],
                       [f"wa{sl}", "sc"], [BK[0]])
                if pc in (1, 3):
                    for hf in range(2):
                        bi_ = 1 + (pc // 2) * 2 + hf
                        mm(bk[bi_][0:2, :],
                           [(sc[:, k, :], wa[sl][:, k, 512 + hf * 512:1024 + hf * 512]) for k in range(KC)],
                           [f"wa{sl}", "sc"], [BK[bi_]])
            psv = bk[0][:, 0:96].rearrange("p (c b) -> p c b", b=2)
            for b in range(2):
                P.op("dve", I("tensor_tensor", out=adaT[:, :, b], in0=psv[:, :, b], in1=badaT[:], op=ALU.add),
                     [BK[0], "badaT"], ["adaT"])
            for b in range(2):
                P.op("dve", I("scalar_tensor_tensor", out=a1[:, :, b], in0=adaT[:, 8:16, b], scalar=1.0, in1=n1w[:],
                                                                  op0=ALU.add, op1=ALU.mult), ["adaT", "n1w"], ["a1"])
                P.op("dve", I("scalar_tensor_tensor", out=a2[:, :, b], in0=adaT[:, 32:40, b], scalar=1.0, in1=n2w[:],
                                                                  op0=ALU.add, op1=ALU.mult), ["adaT", "n2w"], ["a2"])
            for i in range(4):
                P.op("dve", I("tensor_tensor", out=grow[:, i * 512:(i + 1) * 512], in0=bk[1 + i][0:2, :],
                                                           in1=bgate[:, i * 512:(i + 1) * 512], op=ALU.add),
                     [BK[1 + i], "bgate"], ["grow"])
            P.barrier()

        def build_gate_bc(gbc, key, b, g):
            for hf in range(2):
                mm(bk[6 + hf][:], [(sel[0:2, b * 128:(b + 1) * 128], grow[0:2, g * 1024 + hf * 512:g * 1024 + (hf + 1) * 512])],
                   ["sel", "grow"], [BK[6 + hf]])
                P.op("dve", I("tensor_copy", out=gbc[:, hf * 512:(hf + 1) * 512], in_=bk[6 + hf][:]),
                     [BK[6 + hf]], [key])

        def norm_to_hT(st, b, pre_fn, fin_fn, aX, shcol0, tag):
            xn = [SB(st, f"xn{tag}{i}", [128, D], BF16) for i in range(2)]
            junk = SB(st, f"junk{tag}", [128, D], BF16)
            stt = [SB(st, f"stt{tag}{i}", [128, 4], F32) for i in range(2)]
            held = {}

            def do_block(j):
                if j == 0:
                    pre_fn(0)
                if j + 1 < NB:
                    pre_fn(j + 1)
                if j < NB:
                    xs_ap, xs_keys = fin_fn(j)
                    s2 = j % 2
                    act(junk[:], xs_ap, AF.Square, xs_keys, [f"junk{tag}", f"stt{tag}{s2}"], accum_out=stt[s2][:, 0:1])
                    act(stt[s2][:, 1:2], stt[s2][:, 0:1], AF.Ln, [f"stt{tag}{s2}"], [f"stt{tag}{s2}"], bias=EPS, scale=1.0 / D)
                    act(stt[s2][:, 2:3], stt[s2][:, 1:2], AF.Exp, [f"stt{tag}{s2}"], [f"stt{tag}{s2}"], scale=-0.5)
                    P.op("dve", I("tensor_scalar", out=xn[s2][:], in0=xs_ap, scalar1=stt[s2][:, 2:3], scalar2=None, op0=ALU.mult),
                         xs_keys + [f"stt{tag}{s2}"], [f"xn{tag}{s2}"])
                    for hf in range(2):
                        tb = 4 + 2 * hf + s2
                        tp = bkb(tb)
                        P.op("pe", [I("transpose", out=tp[:, c * 128:(c + 1) * 128], in_=xn[s2][:, (4 * hf + c) * 128:(4 * hf + c + 1) * 128],
                                      identity=ident_bf[:]) for c in range(4)], [f"xn{tag}{s2}", "ident_bf"], [BK[tb]])
                if j >= 1:
                    jb = j - 1
                    s2 = jb % 2
                    tpa, tpb = bkb(4 + s2), bkb(6 + s2)
                    for c in range(4):
                        act(hT[:, c, jb * 128:(jb + 1) * 128], tpa[:, c * 128:(c + 1) * 128], AF.Identity,
                            [BK[4 + s2]], [f"hT{jb}"], bias=adaT[:, shcol0 + c, b:b + 1], scale=aX[:, c, b:b + 1])
                    for c in range(4, 8):
                        P.op("dve", I("tensor_scalar", out=hT[:, c, jb * 128:(jb + 1) * 128], in0=tpb[:, (c - 4) * 128:(c - 3) * 128],
                                      scalar1=aX[:, c, b:b + 1], scalar2=adaT[:, shcol0 + c, b:b + 1], op0=ALU.mult, op1=ALU.add),
                             [BK[6 + s2]], [f"hT{jb}"])
            return do_block

        hT_all = [f"hT{j}" for j in range(NB)]

        def hT_keys(t):
            return [f"hT{j}" for j in range(4 * t, 4 * t + 4)]

        for b in range(nseq):
            with contextlib.ExitStack() as st:
                xs = [SB(st, f"xs{i}", [128, D], F32) for i in range(3)]

                def ldx1(j):
                    P.dma("sp", xs[j % 3][:], x[b, j * 128:(j + 1) * 128, :], f"ld_xs{j % 3}", (), [f"xs{j % 3}"])

                def fin1(j):
                    return xs[j % 3][:], [f"xs{j % 3}"]
                blk1 = norm_to_hT(st, b, ldx1, fin1, a1, 0, "a")
                for j in range(NB + 1):
                    blk1(j)
                P.barrier()
            if stop_after <= 1:
                continue

            def gates(want_m, want_f, Grow=None, A_tm=None, X_tm=None, NP_=None):
                with contextlib.ExitStack() as st:
                    wi = SB(st, "wi", [128, KC, 4], BF16)
                    wf = SB(st, "wf", [128, KC, 20], BF16)
                    ipre = SB(st, "ipre", [4, S], F32)
                    spl = SB(st, "spl", [20, S], F32)
                    negF = SB(st, "negF", [20, S], F32)
                    P.dma("pool", wi[:], w_in_v[:, :, O_MI:O_MI + 4], "ld_wi", (), ["wi"])
                    P.dma("pool", wf[:, :, 0:4], w_in_v[:, :, O_MF:O_MF + 4], "ld_wf", (), ["wf"])
                    P.dma("pool", wf[:, :, 4:20], w_in_v[:, :, O_FF:O_FF + 16], "ld_wf", (), ["wf"])
                    for t in range(4):
                        ts_ = slice(t * 512, (t + 1) * 512)
                        mm(bk[t % 2][0:4, :], [(wi[:, k, :], hT[:, k, ts_]) for k in range(KC)], ["wi"] + hT_keys(t), [BK[t % 2]])
                        mm(bk[2 + t % 2][0:20, :], [(wf[:, k, :], hT[:, k, ts_]) for k in range(KC)], ["wf"] + hT_keys(t), [BK[2 + t % 2]])
                        act(ipre[:, ts_], bk[t % 2][0:4, :], AF.Identity, [BK[t % 2], "bi"], ["ipre"], bias=bi[:, 0:1])
                        act(spl[:, ts_], bk[2 + t % 2][0:20, :], AF.Exp, [BK[2 + t % 2], "nbf20"], ["spl"], bias=nbf20[:, 0:1], scale=-1.0)
                    act(spl[:], spl[:], AF.Ln, ["spl"], ["spl"], bias=1.0)
                    P.op("dve", I("tensor_tensor_scan", out=negF[:], data0=spl[:], data1=spl[:], initial=0.0, op0=ALU.add, op1=ALU.max),
                         ["spl"], ["negF"])
                    if want_m:
                        Am = SB(st, "Am", [4, S], F32)
                        Xm = SB(st, "Xm", [4, S], F32)
                        P.op("dve", I("tensor_tensor", out=Am[:], in0=ipre[:], in1=negF[0:4, :], op=ALU.add), ["ipre", "negF"], ["Am"])
                        P.op("dve", I("memset", Grow[:, 0:1], 0.0), (), ["Grow"])
                        P.op("dve", I("tensor_tensor_scan", out=Grow[:, 1:S + 1], data0=Am[:], data1=Am[:], initial=0.0, op0=ALU.max, op1=ALU.max),
                             ["Am"], ["Grow"])
                        P.op("dve", I("tensor_tensor", out=Xm[:], in0=negF[0:4, :], in1=Grow[:, 1:S + 1], op=ALU.subtract), ["negF", "Grow"], ["Xm"])
                        P.op("pe", [I("transpose", out=bk[4][:, j * 4:(j + 1) * 4], in_=Am[0:4, j * 128:(j + 1) * 128],
                                                               identity=ident_f[0:4, 0:4]) for j in range(NB)], ["Am", "ident_f"], [BK[4]])
                        P.op("pe", [I("transpose", out=bk[5][:, j * 4:(j + 1) * 4], in_=Xm[0:4, j * 128:(j + 1) * 128],
                                                               identity=ident_f[0:4, 0:4]) for j in range(NB)], ["Xm", "ident_f"], [BK[5]])
                        P.op("dve", I("tensor_copy", out=A_tm[:], in_=bk[4][:, 0:64]), [BK[4]], ["A_tm"])
                        P.op("dve", I("tensor_copy", out=X_tm[:], in_=bk[5][:, 0:64]), [BK[5]], ["X_tm"])
                    if want_f:
                        r1 = SB(st, "r1", [20, S], F32)
                        NP_ = SB(st, "NP", [20, 3, S], BF16)
                        P.op("dve", I("tensor_copy", out=NP_[:, 0, :], in_=negF[:]), ["negF"], ["NP"])
                        P.op("dve", I("tensor_tensor", out=r1[:], in0=negF[:], in1=NP_[:, 0, :], op=ALU.subtract), ["negF", "NP"], ["r1"])
                        P.op("dve", I("tensor_copy", out=NP_[:, 1, :], in_=r1[:]), ["r1"], ["NP"])
                        P.op("dve", I("tensor_tensor", out=r1[:], in0=r1[:], in1=NP_[:, 1, :], op=ALU.subtract), ["r1", "NP"], ["r1"])
                        P.op("dve", I("tensor_copy", out=NP_[:, 2, :], in_=r1[:]), ["r1"], ["NP"])
                        P.dma("sp", npscr.rearrange("p (r t) -> p r t", r=3), NP_[:], "st_np", ["NP"], ["npscr"])
                    P.barrier()

            with contextlib.ExitStack() as st2:
                hmT = SB(st2, "hmT", [128, KC, S], BF16)
                st2b = contextlib.ExitStack()
                Grow = SB(st2b, "Grow", [4, S + 1], F32)
                A_tm = SB(st2b, "A_tm", [128, 64], F32)
                X_tm = SB(st2b, "X_tm", [128, 64], F32)
                gates(True, True, Grow=Grow, A_tm=A_tm, X_tm=X_tm)

                with contextlib.ExitStack() as st:
                    qT = SB(st, "qT", [128, 2, S], BF16)
                    kT = SB(st, "kT", [128, 2, S], BF16)
                    ktm = SB(st, "ktm", [128, NB, 256], BF16)
                    vA = SB(st, "vA", [128, NB, 258], BF16)
                    vwa = SB(st, "vwa", [128, NB, 258], BF16)
                    STb = SB(st, "STb", [128, S], BF16)
                    oS = SB(st, "oS", [128, NB, 256], BF16)
                    Cf = SB(st, "Cf", [128, 2, 257], F32)
                    Cb = [SB(st, f"Cb{i}", [128, 2, 258], BF16) for i in range(2)]
                    sta = SB(st, "sta", [128, 12, 16], F32)
                    hmtm = [SB(st, f"hmtm{i}", [128, 256], BF16) for i in range(4)]
                    junk4 = SB(st, "junk4", [128, 256], BF16)
                    wq_ = [SB(st, f"wq{i}", [128, KC, 256], BF16) for i in range(2)]
                    wk_ = [SB(st, f"wk{i}", [128, KC, 256], BF16) for i in range(2)]
                    wv_ = [SB(st, f"wv{i}", [128, KC, 256], BF16) for i in range(2)]
                    Gbc = SB(st, "Gbc", [128, S + 1], F32)
                    DTa = SB(st, "DTa", [128, S], F32)
                    Eba = SB(st, "Eba", [128, S], F32)
                    tmpS = [SB(st, f"tmpS{i}", [128, 512], F32) for i in range(2)]

                    def ld_m(h):
                        i = h % 2
                        P.dma("pool", wq_[i][:], w_in_v[:, :, O_MQ + 256 * h:O_MQ + 256 * h + 256], f"ld_wq{i}", (), [f"wq{i}"])
                        P.dma("pool", wk_[i][:], w_in_v[:, :, O_MK + 256 * h:O_MK + 256 * h + 256], f"ld_wk{i}", (), [f"wk{i}"])
                        P.dma("pool", wv_[i][:], w_in_v[:, :, O_MV + 256 * h:O_MV + 256 * h + 256], f"ld_wv{i}", (), [f"wv{i}"])
                    ld_m(0)
                    P.op("dve", I("memset", vA[:, :, 256:257], 1.0), (), [f"vA{j}" for j in range(NB)])
                    WALL, DCY, EXA, SS, ND, MDEN, RR, VAR, FAC, TMP, DEN, FACP = range(12)

                    def post_block(h, j):
                        s2 = j % 4
                        js = slice(j * 128, (j + 1) * 128)
                        tb = (2, 3, 0, 1)[s2]
                        tp = bkb(tb)
                        act(hmtm[s2][:], oS[:, j, :], AF.Identity, [f"oS{j}", "staFP"], [f"hmtm{s2}"], scale=sta[:, FACP, j:j + 1])
                        P.op("pe", [I("transpose", out=tp[:, c * 128:(c + 1) * 128], in_=hmtm[s2][:, c * 128:(c + 1) * 128],
                                      identity=ident_bf[:]) for c in range(2)], [f"hmtm{s2}", "ident_bf"], [BK[tb]])
                        P.op("dve", I("tensor_copy", out=hmT[:, 2 * h:2 * h + 2, js], in_=tp[:, 0:256].rearrange("p (c t) -> p c t", t=128)),
                             [BK[tb]], ["hmT"])

                    pending_post = []
                    for h in range(4):
                        wq, wk, wv = wq_[h % 2], wk_[h % 2], wv_[h % 2]
                        wqk, wkk, wvk = f"wq{h % 2}", f"wk{h % 2}", f"wv{h % 2}"
                        if h + 1 < 4:
                            ld_m(h + 1)
                        for t5 in range(5):
                            c0, c1 = t5 * 512, min(S + 1, (t5 + 1) * 512)
                            gb_ = 6 + t5 % 2
                            mm(bk[gb_][:, 0:c1 - c0], [(sel[0:4, h * 128:(h + 1) * 128], Grow[0:4, c0:c1])], ["sel", "Grow"], [BK[gb_]])
                            P.op("dve", I("tensor_copy", out=Gbc[:, c0:c1], in_=bk[gb_][:, 0:c1 - c0]), [BK[gb_]], ["Gbc"])
                        g0v = Gbc[:, 0:S].rearrange("p (j t) -> p j t", t=128)[:, :, 0]
                        gEv = Gbc[:, 1:S + 1].rearrange("p (j t) -> p j t", t=128)[:, :, 127]
                        Av = A_tm[:].rearrange("p (j h) -> p j h", h=4)[:, :, h]
                        Xv = X_tm[:].rearrange("p (j h) -> p j h", h=4)[:, :, h]
                        P.op("dve", I("tensor_tensor", out=sta[:, TMP, :], in0=Av, in1=gEv, op=ALU.subtract), ["A_tm", "Gbc"], ["staT"])
                        act(sta[:, WALL, :], sta[:, TMP, :], AF.Exp, ["staT"], ["staW"])
                        P.op("dve", I("tensor_tensor", out=sta[:, ND, :], in0=g0v, in1=gEv, op=ALU.subtract), ["Gbc"], ["staN"])
                        act(sta[:, DCY, :], sta[:, ND, :], AF.Exp, ["staN"], ["staD"])
                        act(sta[:, EXA, :], Xv, AF.Exp, ["X_tm"], ["staE"])
                        exq = []
                        for j in range(NB):
                            js = slice(j * 128, (j + 1) * 128)
                            jg = slice(1 + j * 128, 1 + (j + 1) * 128)
                            exq.append((DTa[:, js], Gbc[:, jg], ["Gbc", "A_tm"], [f"DTa{j // 4}"], A_tm[:, j * 4 + h:j * 4 + h + 1]))
                            exq.append((Eba[:, js], Gbc[:, jg], ["Gbc"], ["Eba"], Gbc[:, j * 128:j * 128 + 1]))
                        pj = 0
                        for (wt, wkey, dst, dkey, scl) in ((wq, wqk, qT, "qT", 1.0 / 16), (wk, wkk, kT, "kT", 1.0)):
                            for c in range(2):
                                for t in range(4):
                                    ts_ = slice(t * 512, (t + 1) * 512)
                                    pb = 4 + pj % 2
                                    mm(bk[pb][:], [(wt[:, k, c * 128:(c + 1) * 128], hT[:, k, ts_]) for k in range(KC)], [wkey] + hT_keys(t), [BK[pb]])
                                    act(dst[:, c, ts_], bk[pb][:], AF.Identity, [BK[pb]], [dkey], scale=scl)
                                    for _ in range(2):
                                        o_, i_, r_, w_, b_ = exq.pop(0)
                                        act(o_, i_, AF.Exp, r_, w_, bias=b_, scale=-1.0)
                                    if pending_post:
                                        post_block(*pending_post.pop(0))
                                    pj += 1
                        for g4 in range(4):
                            gs = slice(g4 * 512, (g4 + 1) * 512)
                            P.op("pool", I("tensor_tensor", out=DTa[:, gs], in0=DTa[:, gs], in1=mask4[:], op=ALU.mult),
                                 [f"DTa{g4}", "mask4"], [f"DTm{g4}", f"DTa{g4}"])
                        for j2 in range(NB // 2):
                            pb = 4 + pj % 2
                            pj += 1
                            for jj in range(2):
                                j = j2 * 2 + jj
                                mm(bk[pb][:, jj * 256:(jj + 1) * 256], [(hT[:, k, j * 128:(j + 1) * 128], wv[:, k, :]) for k in range(KC)],
                                   [wvk, f"hT{j}"], [BK[pb]])
                            act(vA[:, j2 * 2:j2 * 2 + 2, 0:256], bk[pb][:].rearrange("p (j c) -> p j c", c=256), AF.Identity, [BK[pb]],
                                [f"vA{j2 * 2}", f"vA{j2 * 2 + 1}"])
                            for jj in range(2):
                                j = j2 * 2 + jj
                                P.op("dve", I("tensor_scalar", out=vwa[:, j, 0:257], in0=vA[:, j, 0:257], scalar1=sta[:, WALL, j:j + 1], scalar2=None, op0=ALU.mult),
                                     [f"vA{j}", "staW"], [f"vwa{j}"])
                        for g4 in range(4):
                            tb = 6 + g4 % 2
                            tp = bkb(tb)
                            fns = []
                            for jj in range(4):
                                j = g4 * 4 + jj
                                for c in range(2):
                                    fns.append(I("transpose", out=tp[:, jj * 256 + c * 128:jj * 256 + (c + 1) * 128],
                                                 in_=kT[:, c, j * 128:(j + 1) * 128], identity=ident_bf[:]))
                            P.op("pe", fns, ["kT", "ident_bf"], [BK[tb]])
                            P.op("dve", I("tensor_copy", out=ktm[:, g4 * 4:g4 * 4 + 4, :], in_=tp.rearrange("p (j c) -> p j c", c=256)),
                                 [BK[tb]], ["ktm"])
                        while pending_post:
                            post_block(*pending_post.pop(0))
                        for g4 in range(4):
                            gs = slice(g4 * 512, (g4 + 1) * 512)
                            fns = []
                            for jj in range(4):
                                j = g4 * 4 + jj
                                js = slice(j * 128, (j + 1) * 128)
                                for c in range(2):
                                    fns.append(I("matmul", bk[g4][:, jj * 128:(jj + 1) * 128], lhsT=kT[:, c, js], rhs=qT[:, c, js],
                                                 start=(c == 0), stop=(c == 1)))
                            P.op("pe", fns, ["kT", "qT"], [BK[g4]])
                            P.op("dve", I("tensor_tensor", out=STb[:, gs], in0=bk[g4][:], in1=DTa[:, gs], op=ALU.mult),
                                 [BK[g4], f"DTm{g4}"], [f"STb{g4}"])
                        for c in range(2):
                            P.op("dve", I("tensor_tensor", out=qT[:, c, :], in0=qT[:, c, :], in1=Eba[:], op=ALU.mult), ["qT", "Eba"], ["qT"])
                        P.op("dve", I("memset", Cf[:], 0.0), (), ["Cf0", "Cf1"])
                        P.op("dve", I("memset", Cb[0][:], 0.0), (), ["Cb0a", "Cb0b"])

                        def upd(j):
                            for c in range(2):
                                ub = 4 + (j % 2) * 2 + c
                                mm(bk[ub][:, 0:257], [(ktm[:, j, c * 128:(c + 1) * 128], vwa[:, j, 0:257])], ["ktm", f"vwa{j}"], [BK[ub]])
                        upd(0)
                        for j in range(NB):
                            s2 = j % 2
                            js = slice(j * 128, (j + 1) * 128)
                            if j + 1 < NB:
                                upd(j + 1)
                            ob = s2
                            mm(bk[ob][:, 0:257], [(STb[:, js], vA[:, j, 0:257])] + [(qT[:, c, js], Cb[s2][:, c, 0:257]) for c in range(2)],
                               [f"STb{j // 4}", f"vA{j}", "qT", f"Cb{s2}a", f"Cb{s2}b"], [BK[ob]])
                            for c in range(2):
                                ub = 4 + s2 * 2 + c
                                P.op("dve", I("scalar_tensor_tensor", out=Cf[:, c, :], in0=Cf[:, c, :], scalar=sta[:, DCY, j:j + 1], in1=bk[ub][:, 0:257],
                                              op0=ALU.mult, op1=ALU.add), [f"Cf{c}", "staD", BK[ub]], [f"Cf{c}"])
                            if j + 1 < NB:
                                act(Cb[1 - s2][:, 0, 0:257], Cf[:, 0, :], AF.Identity, ["Cf0"], [f"Cb{1 - s2}a"])
                                P.op("dve", I("tensor_copy", out=Cb[1 - s2][:, 1, 0:257], in_=Cf[:, 1, :]), ["Cf1"], [f"Cb{1 - s2}b"])
                            act(junk4[:], bk[ob][:, 0:256], AF.Square, [BK[ob]], ["junk4", "staS"], accum_out=sta[:, SS, j:j + 1])
                            act(oS[:, j, :], bk[ob][:, 0:256], AF.Identity, [BK[ob]], [f"oS{j}"])
                            act(sta[:, DEN, j:j + 1], bk[ob][:, 256:257], AF.Identity, [BK[ob]], ["staDen"])
                        denv = sta[:, DEN, :]
                        P.op("dve", I("tensor_scalar", out=sta[:, ND, :], in0=denv, scalar1=-1.0, scalar2=None, op0=ALU.mult), ["staDen"], ["staN"])
                        P.op("dve", I("tensor_tensor", out=sta[:, MDEN, :], in0=sta[:, ND, :], in1=sta[:, EXA, :], op=ALU.max), ["staN", "staE"], ["staM"])
                        P.op("dve", I("tensor_tensor", out=sta[:, MDEN, :], in0=sta[:, MDEN, :], in1=denv, op=ALU.max), ["staM", "staDen"], ["staM"])
                        P.op("dve", I("reciprocal", out=sta[:, RR, :], in_=sta[:, MDEN, :]), ["staM"], ["staR"])
                        P.op("dve", I("tensor_tensor", out=sta[:, VAR, :], in0=sta[:, SS, :], in1=sta[:, RR, :], op=ALU.mult), ["staS", "staR"], ["staV"])
                        P.op("dve", I("tensor_tensor", out=sta[:, VAR, :], in0=sta[:, VAR, :], in1=sta[:, RR, :], op=ALU.mult), ["staV", "staR"], ["staV"])
                        act(sta[:, VAR, :], sta[:, VAR, :], AF.Ln, ["staV"], ["staV"], bias=EPS, scale=1.0 / 256)
                        act(sta[:, VAR, :], sta[:, VAR, :], AF.Exp, ["staV"], ["staV"], scale=-0.5)
                        P.op("dve", I("tensor_tensor", out=sta[:, FACP, :], in0=sta[:, VAR, :], in1=sta[:, RR, :], op=ALU.mult), ["staV", "staR"], ["staFP"])
                        for j in range(NB):
                            post_block(h, j)
                    while pending_post:
                        post_block(*pending_post.pop(0))
                    P.barrier()
                st2b.close()

                hfT = SB(st2, "hfT", [128, KC, S], BF16)
                st3b = contextlib.ExitStack()

                with contextlib.ExitStack() as st:
                    qaug = [[SB(st, f"qaug{s}{i}", [128, S], BF16) for i in range(2)] for s in range(2)]
                    kaug = [[SB(st, f"kaug{s}{i}", [128, S], BF16) for i in range(2)] for s in range(2)]
                    vaug = [SB(st, f"vaug{s}", [128, NB, 192], BF16) for s in range(2)]
                    wq2_ = [SB(st, f"wq2{i}", [128, KC, 128], BF16) for i in range(2)]
                    wk2_ = [SB(st, f"wk2{i}", [128, KC, 128], BF16) for i in range(2)]
                    wv2_ = [SB(st, f"wv2{i}", [128, KC, 128], BF16) for i in range(2)]

                    def ld_f(p):
                        i = p % 2
                        P.dma("pool", wq2_[i][:], w_in_v[:, :, O_FQ + 128 * p:O_FQ + 128 * p + 128], f"ld_wq2{i}", (), [f"wq2{i}"])
                        P.dma("pool", wk2_[i][:], w_in_v[:, :, O_FK + 128 * p:O_FK + 128 * p + 128], f"ld_wk2{i}", (), [f"wk2{i}"])
                        P.dma("pool", wv2_[i][:], w_in_v[:, :, O_FV + 128 * p:O_FV + 128 * p + 128], f"ld_wv2{i}", (), [f"wv2{i}"])
                    ld_f(0)
                    sq = [SB(st, f"sq{i}", [128, 512], BF16) for i in range(2)]
                    rs = [SB(st, f"rs{i}", [128, 512], F32) for i in range(2)]
                    pT = [SB(st, f"pT{i}", [128, 512], BF16) for i in range(4)]
                    rden = [SB(st, f"rden{i}", [128, 512], F32) for i in range(2)]
                    for s_ in range(2):
                        for i in range(2):
                            P.op("dve", I("memset", qaug[s_][i][64:128, :], 0.0), (), [f"qaug{s_}{i}"])
                            P.op("dve", I("memset", kaug[s_][i][64:128, :], 0.0), (), [f"kaug{s_}{i}"])
                            P.op("dve", I("memset", qaug[s_][i][64:70, :], 1.0), (), [f"qaug{s_}{i}"])
                            P.op("dve", I("memset", kaug[s_][i][64:70, :], -1.0), (), [f"kaug{s_}{i}"])
                        P.op("dve", I("memset", vaug[s_][:, :, 64:128], 1.0), (), [f"vaug{s_}"])
                    cnt3 = {"sq": 0, "pj": 0}

                    def fox_proj(p):
                        for _ in fox_proj_gen(p):
                            pass

                    def fox_proj_gen(p):
                        s_ = p % 2
                        wq2, wk2, wv2 = wq2_[p % 2], wk2_[p % 2], wv2_[p % 2]
                        if p + 1 < 8:
                            ld_f(p + 1)
                        sched = {}

                        def at(tick, fn):
                            sched.setdefault(tick, []).append(fn)
                        u = 0
                        for (wt, wkey, aug, akey, wcol) in ((wq2, f"wq2{p % 2}", qaug, "qaug", fqw_s), (wk2, f"wk2{p % 2}", kaug, "kaug", fkw_s)):
                            for t in range(4):
                                ts_ = slice(t * 512, (t + 1) * 512)
                                pb = 5 + u % 2
                                i2 = u % 2
                                t0 = 8 * u

                                def s0c(ch, wt=wt, wkey=wkey, ts_=ts_, pb=pb, t=t):
                                    P.op("pe", [I("matmul", bk[pb][:], lhsT=wt[:, k, 0:128], rhs=hT[:, k, ts_], start=(k == 0), stop=(k == KC - 1))
                                                for k in (2 * ch, 2 * ch + 1)], [wkey] + hT_keys(t), [BK[pb]])

                                def s1(pb=pb, i2=i2):
                                    act(sq[i2][:], bk[pb][:], AF.Square, [BK[pb]], [f"sq{i2}"])

                                def s2(i2=i2):
                                    mm(bk[7][:], [(bdones[:], sq[i2][:])], ["bdones", f"sq{i2}"], [BK[7]])

                                def s3(i2=i2):
                                    act(rs[i2][:], bk[7][:], AF.Ln, [BK[7]], [f"rs{i2}"], bias=EPS, scale=1.0 / 64)
                                    act(rs[i2][:], rs[i2][:], AF.Exp, [f"rs{i2}"], [f"rs{i2}"], scale=-0.5)

                                def s4(aug=aug, akey=akey, wcol=wcol, ts_=ts_, pb=pb, i2=i2):
                                    for hh in range(2):
                                        pr = slice(hh * 64, hh * 64 + 64)
                                        P.op("dve", I("scalar_tensor_tensor", out=aug[s_][hh][0:64, ts_], in0=bk[pb][pr, :], scalar=wcol[pr, 0:1],
                                                      in1=rs[i2][pr, :], op0=ALU.mult, op1=ALU.mult),
                                             [BK[pb], f"rs{i2}", "fqw_s", "fkw_s"], [f"{akey}{s_}{hh}"])
                                for ch in range(4):
                                    at(t0 + ch, (lambda ch=ch, f=s0c: f(ch)))
                                for si, fn in enumerate((s1, s2, s3, s4)):
                                    at(t0 + 6 + 3 * si, fn)
                                u += 1
                        for g4, t0 in enumerate((64, 72, 76, 80)):
                            pb = 5 + g4 % 2

                            def v0(jj, g4=g4, pb=pb):
                                j = g4 * 4 + jj
                                mm(bk[pb][:, jj * 128:(jj + 1) * 128], [(hT[:, k, j * 128:(j + 1) * 128], wv2[:, k, 0:128]) for k in range(KC)],
                                   [f"wv2{p % 2}", f"hT{j}"], [BK[pb]])

                            def v1(g4=g4, pb=pb):
                                pv = bk[pb][:].rearrange("p (j c) -> p j c", c=128)
                                act(vaug[s_][:, g4 * 4:g4 * 4 + 4, 0:64], pv[:, :, 0:64], AF.Identity, [BK[pb]], [f"vaug{s_}"])
                                act(vaug[s_][:, g4 * 4:g4 * 4 + 4, 128:192], pv[:, :, 64:128], AF.Identity, [BK[pb]], [f"vaug{s_}"])
                            for jj in range(4):
                                at(t0 + jj, (lambda jj=jj, f=v0: f(jj)))
                            at(t0 + 6, v1)

                        def dmas():
                            for hh in range(2):
                                h = 2 * p + hh
                                for r3 in range(3):
                                    P.dma("sp", kaug[s_][hh][64 + r3:65 + r3, :], npscr[4 + h:5 + h, r3 * S:(r3 + 1) * S], f"ld_ka{s_}{hh}", ["npscr"], [f"kaug{s_}{hh}"])
                                    P.dma("sp", qaug[s_][hh][67 + r3:68 + r3, :], npscr[4 + h:5 + h, r3 * S:(r3 + 1) * S], f"ld_qa{s_}{hh}", ["npscr"], [f"qaug{s_}{hh}"])
                        at(1, dmas)
                        for tick in range(max(sched) + 1):
                            for fn in sched.get(tick, ()):
                                fn()
                            yield

                    cnt_a = {"s": 0, "p": 0, "o": 0}

                    def fox_att(p, hh, gen=None):
                        s_ = p % 2
                        qa, ka, va = qaug[s_][hh], kaug[s_][hh], vaug[s_]
                        qk, kk, vk = f"qaug{s_}{hh}", f"kaug{s_}{hh}", f"vaug{s_}"
                        vlo = 0 if hh == 0 else 64
                        ol = 0 if hh == 0 else 64
                        dl = 64 if hh == 0 else 0
                        for g in range(4):
                            ob = 3 + cnt_a["o"] % 2
                            cnt_a["o"] += 1
                            nkb = 4 * g + 4
                            pend = []

                            def score(i):
                                r = i - 4 * g
                                qo = 128 * r if r >= 0 else 0
                                sb_ = cnt_a["s"] % 3
                                cnt_a["s"] += 1
                                pairs = [(ka[:, i * 128:(i + 1) * 128], qa[:, 512 * g + qo:512 * g + 512])]
                                fns = [I("matmul", bk[sb_][:, qo:512], lhsT=pairs[0][0], rhs=pairs[0][1], start=True, stop=(r < 0))]
                                if r >= 0:
                                    fns.append(I("matmul", bk[sb_][:, qo:qo + 128], lhsT=ident_bf[:], rhs=maskb[:], start=False, stop=True))
                                P.op("pe", fns, [kk, qk, "ident_bf", "maskb"], [BK[sb_]])
                                return (i, qo, sb_)

                            def expo(i, qo, sb_):
                                pi = cnt_a["p"] % 4
                                cnt_a["p"] += 1
                                act(pT[pi][:, qo:512], bk[sb_][:, qo:512], AF.Exp, [BK[sb_]], [f"pT{pi}"])
                                return pi

                            def pv(i, qo, pi):
                                P.op("pe", I("matmul", bk[ob][:, qo:512], lhsT=va[:, i, vlo:vlo + 128], rhs=pT[pi][:, qo:512],
                                                                             start=(i == 0), stop=(i == nkb - 1)), [vk, f"pT{pi}"], [BK[ob]])
                            scq = [score(i) for i in range(min(2, nkb))]
                            for i in range(nkb):
                                if i + 2 < nkb:
                                    scq.append(score(i + 2))
                                sc0 = scq.pop(0)
                                pi = expo(*sc0)
                                pv(sc0[0], sc0[1], pi)
                                if gen is not None:
                                    next(gen, None)
                            rd = rden[g % 2]
                            P.op("dve", I("reciprocal", out=rd[dl:dl + 64, :], in_=bk[ob][dl:dl + 64, :]), [BK[ob]], [f"rden{g % 2}"])
                            P.op("dve", I("tensor_tensor", out=hfT[ol:ol + 64, p, g * 512:(g + 1) * 512], in0=bk[ob][ol:ol + 64, :],
                                                                         in1=rd[dl:dl + 64, :], op=ALU.mult), [BK[ob], f"rden{g % 2}"], ["hfT"])

                    fox_proj(0)
                    for p in range(8):
                        if fox_limit <= 2 * p:
                            break
                        gen = fox_proj_gen(p + 1) if p + 1 < 8 else None
                        fox_att(p, 0, gen)
                        if fox_limit <= 2 * p + 1:
                            break
                        fox_att(p, 1, gen)
                        if gen is not None:
                            for _ in gen:
                                pass
                    P.barrier()
                st3b.close()

                with contextlib.ExitStack() as st:
                    wo = [SB(st, f"wo{i}", [128, KC, 512], BF16) for i in range(2)]
                    sg = [SB(st, f"sg{i}", [128, 512], F32) for i in range(2)]
                    n4 = 0
                    for half in range(2):
                        P.dma("pool", wo[half][:], w_in_v[:, :, O_MO + 512 * half:O_MO + 512 * half + 512], f"ld_wo{half}", (), [f"wo{half}"])
                    for c in range(KC):
                        half, cl = c // 4, (c % 4) * 128
                        for t in range(4):
                            ts_ = slice(t * 512, (t + 1) * 512)
                            pb = n4 % 4
                            i2 = n4 % 2
                            n4 += 1
                            mm(bk[pb][:], [(wo[half][:, k, cl:cl + 128], hT[:, k, ts_]) for k in range(KC)], [f"wo{half}"] + hT_keys(t), [BK[pb]])
                            act(sg[i2][:], bk[pb][:], AF.Sigmoid, [BK[pb]], [f"sg{i2}"])
                            P.op("dve", I("scalar_tensor_tensor", out=hmT[:, c, ts_], in0=hmT[:, c, ts_], scalar=mnw[:, c:c + 1],
                                                                                             in1=sg[i2][:], op0=ALU.mult, op1=ALU.mult),
                                 [f"hmTc{c}", "mnw", f"sg{i2}"], [f"hmTc{c}"])
                    P.barrier()

                st5 = contextlib.ExitStack()
                mgT = SB(st5, "mgT", [128, KC, S], BF16)
                wot = SB(st5, "wot", [128, KC, D], BF16)
                with contextlib.ExitStack() as st:
                    wsl = {nm: [SB(st, f"w5{nm}{i}", [128, KC, 256], BF16) for i in range(2)] for nm in ("bm", "bf", "ga", "gb")}
                    sga = [SB(st, "sga0", [128, 512], F32)] * 2
                    sgb = [SB(st, "sgb0", [128, 512], F32)] * 2
                    t1 = [SB(st, f"t1{i}", [128, 512], F32) for i in range(2)]
                    t2 = [SB(st, f"t2{i}", [128, 512], F32) for i in range(2)]
                    srcs = {"bm": (w_bm_v, 0), "bf": (w_bf_v, 0), "ga": (w_in_v, O_GA), "gb": (w_in_v, O_GB)}
                    n5 = 0
                    def ld_5(c2):
                        sl = c2 % 2
                        for nm in ("bm", "bf", "ga", "gb"):
                            src, off = srcs[nm]
                            P.dma("pool", wsl[nm][sl][:], src[:, :, off + 256 * c2:off + 256 * c2 + 256], f"ld_w5{nm}{sl}", (), [f"w5{nm}{sl}"])
                    ld_5(0)
                    for c2 in range(4):
                        sl = c2 % 2
                        if c2 + 1 < 4:
                            ld_5(c2 + 1)
                        else:
                            for half in range(2):
                                P.dma("pool", wot[:, :, half * 512:(half + 1) * 512], w_out_v[:, :, half * 512:(half + 1) * 512], "ld_wot", (), ["wot"])
                        for cc in range(2):
                            c = c2 * 2 + cc
                            cl = cc * 128
                            for t in range(4):
                                ts_ = slice(t * 512, (t + 1) * 512)
                                i2 = n5 % 2
                                pb0 = (n5 % 2) * 4
                                n5 += 1
                                mm(bk[pb0][:], [(wsl["bm"][sl][:, k, cl:cl + 128], hmT[:, k, ts_]) for k in range(KC)], [f"w5bm{sl}", "hmT"], [BK[pb0]])
                                mm(bk[pb0 + 1][:], [(wsl["bf"][sl][:, k, cl:cl + 128], hfT[:, k, ts_]) for k in range(KC)], [f"w5bf{sl}", "hfT"], [BK[pb0 + 1]])
                                mm(bk[pb0 + 2][:], [(wsl["ga"][sl][:, k, cl:cl + 128], hT[:, k, ts_]) for k in range(KC)], [f"w5ga{sl}"] + hT_keys(t), [BK[pb0 + 2]])
                                mm(bk[pb0 + 3][:], [(wsl["gb"][sl][:, k, cl:cl + 128], hT[:, k, ts_]) for k in range(KC)], [f"w5gb{sl}"] + hT_keys(t), [BK[pb0 + 3]])
                                act(sga[i2][:], bk[pb0 + 2][:], AF.Sigmoid, [BK[pb0 + 2]], ["sga"])
                                act(sgb[i2][:], bk[pb0 + 3][:], AF.Sigmoid, [BK[pb0 + 3]], ["sgb"])
                                P.op("dve", I("tensor_tensor", out=t1[i2][:], in0=bk[pb0][:], in1=sga[i2][:], op=ALU.mult),
                                     [BK[pb0], "sga"], [f"t1{i2}"])
                                P.op("dve", I("tensor_tensor", out=t2[i2][:], in0=bk[pb0 + 1][:], in1=sgb[i2][:], op=ALU.mult),
                                     [BK[pb0 + 1], "sgb"], [f"t2{i2}"])
                                P.op("dve", I("tensor_tensor", out=mgT[:, c, ts_], in0=t1[i2][:], in1=t2[i2][:], op=ALU.add),
                                     [f"t1{i2}", f"t2{i2}"], ["mgT"])
                    P.barrier()

                with contextlib.ExitStack() as st:
                    g1bc = SB(st, "g1bc", [128, D], F32)
                    xs = [SB(st, f"xr{i}", [128, D], F32) for i in range(2)]
                    x1 = [SB(st, f"x1{i}", [128, D], F32) for i in range(3)]
                    build_gate_bc(g1bc, "g1bc", b, 0)

                    def pre2(j):
                        sl = j % 2
                        P.dma("sp", xs[sl][:], x[b, j * 128:(j + 1) * 128, :], f"ld_xr{sl}", (), [f"xr{sl}"])
                        for hf in range(2):
                            pb = 2 * sl + hf
                            cs = slice(hf * 512, (hf + 1) * 512)
                            mm(bk[pb][:], [(mgT[:, k, j * 128:(j + 1) * 128], wot[:, k, cs]) for k in range(KC)], ["mgT", "wot"], [BK[pb]])

                    def fin2(j):
                        sl = j % 2
                        for hf in range(2):
                            pb = 2 * sl + hf
                            cs = slice(hf * 512, (hf + 1) * 512)
                            P.op("dve", I("tensor_tensor", out=x1[j % 3][:, cs], in0=bk[pb][:], in1=g1bc[:, cs], op=ALU.mult),
                                 [BK[pb], "g1bc"], [f"x1{j % 3}"])
                        P.op("dve", I("tensor_tensor", out=x1[j % 3][:], in0=x1[j % 3][:], in1=xs[sl][:], op=ALU.add), [f"x1{j % 3}", f"xr{sl}"], [f"x1{j % 3}"])
                        P.dma("sp", out[b, j * 128:(j + 1) * 128, :], x1[j % 3][:], f"st_x1{j % 3}", [f"x1{j % 3}"], [f"out{j}"])
                        return x1[j % 3][:], [f"x1{j % 3}"]
                    blk2 = norm_to_hT(st, b, pre2, fin2, a2, 24, "b")
                    for j in range(NB + 1):
                        blk2(j)
                    P.barrier()
                st5.close()
            if stop_after <= 5:
                continue

            with contextlib.ExitStack() as st:
                wdn = SB(st, "wdn", [128, NF, D], BF16)
                g2bc = SB(st, "g2bc", [128, D], F32)
                gT = SB(st, "gT", [128, NF, 1024], BF16)
                wug = [SB(st, f"wug{i}", [128, KC, 256], BF16) for i in range(2)]
                wuv = [SB(st, f"wuv{i}", [128, KC, 256], BF16) for i in range(2)]
                ug = [SB(st, f"ug{i}", [128, 1026], F32) for i in range(2)]
                uv = [SB(st, f"uv{i}", [128, 1026], F32) for i in range(2)]
                cg = [SB(st, f"cg{i}", [128, 512], F32) for i in range(3)]
                cv = [SB(st, f"cv{i}", [128, 512], F32) for i in range(3)]
                hal = SB(st, "hal", [128, 44, 2], F32)
                xr = [SB(st, f"xq{i}", [128, D], F32) for i in range(2)]
                ot = [SB(st, f"ot{i}", [128, D], F32) for i in range(2)]
                def ld_u(u):
                    f2_, sl_ = u % (NF // 2), u % 2
                    P.dma("pool", wug[sl_][:], w_up_v[:, :, 256 * f2_:256 * f2_ + 256], f"ld_wug{sl_}", (), [f"wug{sl_}"])
                    P.dma("pool", wuv[sl_][:], w_up_v[:, :, FF + 256 * f2_:FF + 256 * f2_ + 256], f"ld_wuv{sl_}", (), [f"wuv{sl_}"])
                ld_u(0)
                for f4 in range(0, NF, 4):
                    n_ = min(4, NF - f4)
                    P.dma("pool", wdn[:, f4:f4 + n_, :], w_down_v[:, f4:f4 + n_, :], "ld_wdn", (), ["wdn"])
                build_gate_bc(g2bc, "g2bc", b, 1)
                P.op("dve", I("memset", hal[:], 0.0), (), ["hal"])
                n6 = 0
                pend6 = []
                nu6 = [0]
                for half in range(2):
                    for f2 in range(NF // 2):
                        u_ = half * (NF // 2) + f2
                        sl = u_ % 2
                        if u_ + 1 < NF:
                            ld_u(u_ + 1)
                        for ff_ in range(2):
                            f = f2 * 2 + ff_
                            fl = ff_ * 128
                            us = n6 % 2
                            n6 += 1
                            for (ubuf, ukey, hidx) in ((ug[us], f"ug{us}", f), (uv[us], f"uv{us}", 22 + f)):
                                act(ubuf[:, 0:2], hal[:, hidx, :], AF.Identity, ["hal"], [ukey])
                            for tt in range(2):
                                t = half * 2 + tt
                                ts_ = slice(t * 512, (t + 1) * 512)
                                i2 = nu6[0] % 3
                                pbg, pbv = 2 * i2, 2 * i2 + 1
                                nu6[0] += 1
                                mm(bk[pbg][:], [(wug[sl][:, k, fl:fl + 128], hT[:, k, ts_]) for k in range(KC)], [f"wug{sl}"] + hT_keys(t), [BK[pbg]])
                                mm(bk[pbv][:], [(wuv[sl][:, k, fl:fl + 128], hT[:, k, ts_]) for k in range(KC)], [f"wuv{sl}"] + hT_keys(t), [BK[pbv]])
                                for (pb, ubuf, ukey, cbuf, ckey, widx) in ((pbg, ug[us], f"ug{us}", cg[i2], f"cg{i2}", f), (pbv, uv[us], f"uv{us}", cv[i2], f"cv{i2}", 22 + f)):
                                    u0 = 2 + tt * 512
                                    act(ubuf[:, u0:u0 + 512], bk[pb][:], AF.Identity, [BK[pb]], [ukey])
                                    act(cbuf[:], bk[pb][:], AF.Identity, [BK[pb], "convw", "convb"], [ckey], bias=convb[:, widx:widx + 1], scale=convw[:, widx, 2:3])
                                    P.op("dve", I("scalar_tensor_tensor",
                                        out=cbuf[:], in0=ubuf[:, u0 - 1:u0 + 511], scalar=convw[:, widx, 1:2], in1=cbuf[:], op0=ALU.mult, op1=ALU.add),
                                        [ukey, ckey, "convw"], [ckey])
                                    P.op("dve", I("scalar_tensor_tensor",
                                        out=cbuf[:], in0=ubuf[:, u0 - 2:u0 + 510], scalar=convw[:, widx, 0:1], in1=cbuf[:], op0=ALU.mult, op1=ALU.add),
                                        [ukey, ckey, "convw"], [ckey])
                                if pend6:
                                    pend6.pop()()

                                def fin(i2=i2, f=f, tt=tt):
                                    act(cg[i2][:], cg[i2][:], AF.Silu, [f"cg{i2}"], [f"cg{i2}"])
                                    P.op("dve", I("tensor_tensor", out=gT[:, f, tt * 512:(tt + 1) * 512], in0=cg[i2][:], in1=cv[i2][:], op=ALU.mult),
                                         [f"cg{i2}", f"cv{i2}"], ["gT"])
                                pend6.append(fin)
                            for (ubuf, ukey, hidx) in ((ug[us], f"ug{us}", f), (uv[us], f"uv{us}", 22 + f)):
                                act(hal[:, hidx, :], ubuf[:, 1024:1026], AF.Identity, [ukey], ["hal"])
                    if pend6:
                        pend6.pop()()
                    for jj in range(8):
                        j = half * 8 + jj
                        sl = jj % 2
                        P.dma("sp", xr[sl][:], out[b, j * 128:(j + 1) * 128, :], f"ld_xq{sl}", [f"out{j}"], [f"xq{sl}"])
                        for hf in range(2):
                            pb = 6 + hf
                            cs = slice(hf * 512, (hf + 1) * 512)
                            mm(bk[pb][:], [(gT[:, f, jj * 128:(jj + 1) * 128], wdn[:, f, cs]) for f in range(NF)], ["gT", "wdn"], [BK[pb]])
                            P.op("dve", I("tensor_tensor", out=ot[sl][:, cs], in0=bk[pb][:], in1=g2bc[:, cs], op=ALU.mult),
                                 [BK[pb], "g2bc"], [f"ot{sl}"])
                        P.op("dve", I("tensor_tensor", out=ot[sl][:], in0=ot[sl][:], in1=xr[sl][:], op=ALU.add), [f"ot{sl}", f"xq{sl}"], [f"ot{sl}"])
                        P.dma("sp", out[b, j * 128:(j + 1) * 128, :], ot[sl][:], f"st_ot{sl}", [f"ot{sl}"], [f"out{j}"])
                P.barrier()
        P.barrier()
        P.run()
    return nc


def _prep_inputs(inp, core):
    f = lambda a: np.ascontiguousarray(np.asarray(a, dtype=np.float32))
    b0 = 2 * core
    c = np.asarray(inp["c"], dtype=np.float32)[b0:b0 + 2]

    def pk(v, n):
        return f(np.asarray(v, dtype=np.float32).reshape(n, 128).T)
    conv_w = np.asarray(inp["conv_w"], dtype=np.float32)[0]
    m = {
        "x": f(np.asarray(inp["x"])[b0:b0 + 2]),
        "cT": f(c.T.reshape(KC, 128, 2).transpose(1, 0, 2)),
        "w_ada": f(inp["w_ada"][0]),
        "b_ada": f(inp["b_ada"][0]),
        "b_adaT": pk(inp["b_ada"][0], 48),
        "n1wT": pk(inp["norm1_w"][0], KC),
        "n2wT": pk(inp["norm2_w"][0], KC),
        "mnwT": pk(inp["mlstm_norm_w"][0], KC),
        "w_in": f(inp["w_in"][0]),
        "b_i": f(np.asarray(inp["b_mlstm_i"][0]).reshape(4, 1)),
        "b_f20": f(np.concatenate([np.asarray(inp["b_mlstm_f"][0]), np.asarray(inp["b_fox_f"][0])]).reshape(20, 1)),
        "fqw": f(np.tile(np.asarray(inp["fox_q_norm_w"][0]), 2).reshape(128, 1)),
        "fkw": f(np.tile(np.asarray(inp["fox_k_norm_w"][0]), 2).reshape(128, 1)),
        "w_bm": f(inp["w_branch_mlstm"][0]),
        "w_bf": f(inp["w_branch_fox"][0]),
        "w_out": f(inp["w_out"][0]),
        "w_up": f(inp["w_up"][0]),
        "convT": f(conv_w.reshape(3, 44, 128).transpose(2, 1, 0)),
        "convbT": pk(inp["conv_b"][0], 44),
        "w_down": f(inp["w_down"][0]),
    }
    return m


def kernel(**inputs):
    nc = build_program()
    in_maps = [_prep_inputs(inputs, i) for i in range(NCORES)]
    res = run_bass_kernel_spmd(nc, in_maps, core_ids=list(range(NCORES)))
    return np.concatenate([np.asarray(r["out"]) for r in res.results], axis=0).astype(np.float32)
```
